# Optimizing a Trainium2 kernel written in Bass

```python
import jax, jax.numpy as jnp
from jax import lax
import numpy as np

D_MODEL = 4096
BATCH = 2
SEQ = 4096
DEPTH = 1
DEC_BATCH = 4
DEC_SEQ = 2048
PAST_LEN = 128

N_META = 16
MIX_WIDTH = D_MODEL
GLA_WIDTH = MIX_WIDTH // 2
CONV_WIDTH = MIX_WIDTH - GLA_WIDTH
GLA_HEADS = 8
GLA_DV = GLA_WIDTH // GLA_HEADS
GLA_DK = GLA_DV // 2
GLA_KEY_WIDTH = GLA_HEADS * GLA_DK
GATE_RANK = 16
GATE_TAU = 16.0
CHUNK = 64
META_PAD = CHUNK - N_META
CONV_GROUPS = 16
D_FF = 11008
EPS = 1e-6
IN_SIZES = (GLA_KEY_WIDTH, GLA_KEY_WIDTH, GLA_WIDTH, GLA_WIDTH, GATE_RANK, GATE_RANK,
            CONV_WIDTH, CONV_WIDTH, CONV_WIDTH)
IN_COLS = 2 * GLA_KEY_WIDTH + 2 * GLA_WIDTH + 2 * GATE_RANK + 3 * CONV_WIDTH

kernel_name = "hymba_gla_shortconv_encoder"


def _rmsnorm(x, g):
    xf = x.astype(jnp.float32)
    y = xf * lax.rsqrt(jnp.mean(xf * xf, axis=-1, keepdims=True) + EPS)
    return (y * g.astype(jnp.float32)).astype(x.dtype)


def _dwconv3(u, w):
    up = jnp.pad(u, ((0, 0), (1, 1), (0, 0)))
    w = w.astype(u.dtype)
    return up[:, :-2] * w[0] + up[:, 1:-1] * w[1] + up[:, 2:] * w[2]


def _split(p, sizes):
    outs, off = [], 0
    for s in sizes:
        outs.append(p[..., off:off + s])
        off += s
    return outs


def _gla_scan(q, k, v, lg):
    z, b, lp, h, dk = q.shape
    dv = v.shape[-1]
    n = lp // CHUNK

    def chunks(t):
        return t.astype(jnp.float32).reshape(z, b, n, CHUNK, h, t.shape[-1]).transpose(2, 0, 1, 4, 3, 5)

    tri = jnp.tril(jnp.ones((CHUNK, CHUNK), dtype=bool))
    mask = jnp.stack([tri, jnp.tril(tri, -1)])[:, None, None, :, :, None]

    def step(state, inp):
        qc, kc, vc, gc = inp
        bcum = jnp.cumsum(gc, axis=-2)
        diff = bcum[..., :, None, :] - bcum[..., None, :, :]
        decay = jnp.exp(jnp.where(mask, diff, -jnp.inf))
        scores = jnp.einsum('zbhtd,zbhsd,zbhtsd->zbhts', qc, kc, decay)
        out = (jnp.einsum('zbhts,zbhsv->zbhtv', scores, vc)
               + jnp.einsum('zbhtd,zbhdv->zbhtv', qc * jnp.exp(bcum), state))
        blast = bcum[..., -1:, :]
        state = (jnp.exp(blast[..., 0, :])[..., None] * state
                 + jnp.einsum('zbhsd,zbhsv->zbhdv', kc * jnp.exp(blast - bcum), vc))
        return state, out

    s0 = jnp.zeros((z, b, h, dk, dv), jnp.float32)
    _, o = lax.scan(step, s0, (chunks(q), chunks(k), chunks(v), chunks(lg)))
    return o.transpose(1, 2, 0, 4, 3, 5).reshape(z, b, lp, h, dv)


def _layer(x, mix_norm_g, w_in, w_gate2, b_gate2, head_norm_g, conv_mix_w, w_out,
           ffn_norm_g, w_up, ffn_conv_w, ffn_conv_b, w_down):
    bsz, length, _ = x.shape
    h = _rmsnorm(x, mix_norm_g)
    q, k, v, r, a_f, a_b, cb, cc, ch = _split(h @ w_in, IN_SIZES)

    q = q.reshape(bsz, length, GLA_HEADS, GLA_DK) * (GLA_DK ** -0.5)
    k = k.reshape(bsz, length, GLA_HEADS, GLA_DK)
    v = v.reshape(bsz, length, GLA_HEADS, GLA_DV)
    lg_f = (jax.nn.log_sigmoid((a_f @ w_gate2[0] + b_gate2[0]).astype(jnp.float32)) / GATE_TAU
            ).reshape(bsz, length, GLA_HEADS, GLA_DK)
    lg_b = (jax.nn.log_sigmoid((a_b @ w_gate2[1] + b_gate2[1]).astype(jnp.float32)) / GATE_TAU
            ).reshape(bsz, length, GLA_HEADS, GLA_DK)
    pad = ((0, 0), (META_PAD, 0), (0, 0), (0, 0))

    def both(tf, tb):
        return jnp.stack([jnp.pad(tf, pad), jnp.pad(tb, pad)[:, ::-1]])

    o2 = _gla_scan(both(q, q), both(k, k), both(v, v), both(lg_f, lg_b))
    o = (o2[0] + o2[1][:, ::-1])[:, META_PAD:]
    o = _rmsnorm(o, head_norm_g).reshape(bsz, length, GLA_WIDTH).astype(x.dtype) * jax.nn.silu(r)

    conv_out = cb * _dwconv3(cc * ch, conv_mix_w)

    x = x + jnp.concatenate([o, conv_out], axis=-1) @ w_out

    h2 = _rmsnorm(x, ffn_norm_g)
    u, g = jnp.split(h2 @ w_up, 2, axis=-1)
    g = _dwconv3(g, ffn_conv_w) + ffn_conv_b
    return x + (jax.nn.silu(g) * u) @ w_down


def _trunk(x, meta_tokens, mix_norm_g, w_in, w_gate2, b_gate2, head_norm_g, conv_mix_w, w_out,
           ffn_norm_g, w_up, ffn_conv_w, ffn_conv_b, w_down, final_norm_g):
    bsz = x.shape[0]
    meta = jnp.broadcast_to(meta_tokens.astype(x.dtype)[None], (bsz, N_META, x.shape[-1]))
    hs = jnp.concatenate([meta, x], axis=1)
    for l in range(DEPTH):
        hs = _layer(hs, mix_norm_g[l], w_in[l], w_gate2[l], b_gate2[l], head_norm_g[l],
                    conv_mix_w[l], w_out[l], ffn_norm_g[l], w_up[l], ffn_conv_w[l],
                    ffn_conv_b[l], w_down[l])
    hs = _rmsnorm(hs, final_norm_g)
    return hs[:, N_META:]


def setup_inputs(seed: int = 0) -> dict:
    key = jax.random.key(seed)
    ks = jax.random.split(key, 16)
    f32 = jnp.float32
    nrm = lambda k, s: jax.random.normal(k, s, f32)
    return {
        "x_prompt": nrm(ks[0], (BATCH, SEQ, D_MODEL)),
        "x_sample": nrm(ks[1], (DEC_BATCH, DEC_SEQ, D_MODEL)),
        "meta_tokens": nrm(ks[2], (N_META, D_MODEL)),
        "mix_norm_g": 1.0 + 0.01 * nrm(ks[3], (DEPTH, D_MODEL)),
        "w_in": nrm(ks[4], (DEPTH, D_MODEL, IN_COLS)) * D_MODEL ** -0.5,
        "w_gate2": nrm(ks[5], (DEPTH, 2, GATE_RANK, GLA_KEY_WIDTH)) * GATE_RANK ** -0.5,
        "b_gate2": 0.1 * nrm(ks[6], (DEPTH, 2, GLA_KEY_WIDTH)),
        "head_norm_g": 1.0 + 0.01 * nrm(ks[7], (DEPTH, GLA_DV)),
        "conv_mix_w": nrm(ks[8], (DEPTH, 3, CONV_WIDTH)) * 3.0 ** -0.5,
        "w_out": nrm(ks[9], (DEPTH, MIX_WIDTH, D_MODEL)) * MIX_WIDTH ** -0.5,
        "ffn_norm_g": 1.0 + 0.01 * nrm(ks[10], (DEPTH, D_MODEL)),
        "w_up": nrm(ks[11], (DEPTH, D_MODEL, 2 * D_FF)) * D_MODEL ** -0.5,
        "ffn_conv_w": nrm(ks[12], (DEPTH, 3, D_FF)) * 3.0 ** -0.5,
        "ffn_conv_b": 0.01 * nrm(ks[13], (DEPTH, D_FF)),
        "w_down": nrm(ks[14], (DEPTH, D_FF, D_MODEL)) * D_FF ** -0.5,
        "final_norm_g": 1.0 + 0.01 * nrm(ks[15], (D_MODEL,)),
    }


def reference(x_prompt, x_sample, meta_tokens, mix_norm_g, w_in, w_gate2, b_gate2, head_norm_g,
              conv_mix_w, w_out, ffn_norm_g, w_up, ffn_conv_w, ffn_conv_b, w_down, final_norm_g):
    y_prompt = _trunk(x_prompt, meta_tokens, mix_norm_g, w_in, w_gate2, b_gate2, head_norm_g,
                      conv_mix_w, w_out, ffn_norm_g, w_up, ffn_conv_w, ffn_conv_b, w_down,
                      final_norm_g)
    y_sample = _trunk(x_sample, meta_tokens, mix_norm_g, w_in, w_gate2, b_gate2, head_norm_g,
                      conv_mix_w, w_out, ffn_norm_g, w_up, ffn_conv_w, ffn_conv_b, w_down,
                      final_norm_g)
    return (y_prompt, y_sample)
```

```python
import os
import numpy as np
from contextlib import ExitStack
import concourse.bass as bass
import concourse.mybir as mybir
from concourse.bass_utils import run_bass_kernel_spmd

F32 = mybir.dt.float32
BF16 = mybir.dt.bfloat16
AF = mybir.ActivationFunctionType
ALU = mybir.AluOpType

D = 4096
KC = 32
R = 2064
E = 2048
DFF = 11008
NF = 86
EPS = 1e-6
TB = [0, 413, 826, 1239, 1652, 2064]
ET = 256
CHUNKS = []
for _i in range(5):
    _s, _e = TB[_i], TB[_i + 1]
    _w = _e - _s
    _c0 = _w - 309
    CHUNKS.append((_s, _c0))
    for _k in range(3):
        CHUNKS.append((_s + _c0 + 103 * _k, 103))
FGROUPS = []
_f = 0
for _n in [11, 11, 11, 11, 11, 11, 10, 10]:
    FGROUPS.append((_f, _n))
    _f += _n
OFF_Q, OFF_K, OFF_V, OFF_R, OFF_CB, OFF_CC, OFF_CH = 0, 1024, 2048, 4096, 6176, 8224, 10272


class _Op:
    __slots__ = ("eng", "fn", "deps", "is_dma", "sem", "val", "signal")

    def __init__(self, eng, fn, is_dma=False):
        self.eng = eng
        self.fn = fn
        self.deps = []
        self.is_dma = is_dma
        self.sem = None
        self.val = 0
        self.signal = False


class Prog:
    ENGS = ("pe", "act", "dve", "pool", "sp")

    def __init__(self, nc, es):
        self.nc = nc
        self.es = es
        self.ops = {e: [] for e in self.ENGS}
        self.last_w = {}
        self.readers = {}
        self.eng_sem = {e: es.enter_context(nc.semaphore("prog_" + e)) for e in self.ENGS}
        self.dma_sem_count = {}
        self.pending_dmas = []

    def new_dma_sem(self, name):
        s = self.es.enter_context(self.nc.semaphore(name))
        self.dma_sem_count[id(s)] = 0
        return s

    def _collect(self, op, reads, writes):
        deps = []
        for k in reads:
            w = self.last_w.get(k)
            if w is not None:
                deps.append(w)
        for k in writes:
            w = self.last_w.get(k)
            if w is not None:
                deps.append(w)
            deps.extend(self.readers.get(k, ()))
        seen = set()
        for d in deps:
            if d is op or id(d) in seen:
                continue
            seen.add(id(d))
            if (not d.is_dma) and (not op.is_dma) and d.eng == "pe" and op.eng == "pe":
                continue
            op.deps.append(d)
            if not d.is_dma:
                d.signal = True
        for k in writes:
            self.last_w[k] = op
            self.readers[k] = []
        for k in reads:
            self.readers.setdefault(k, []).append(op)

    def op(self, eng, fn, reads=(), writes=()):
        pr = [k for k in reads if k.startswith("ps")]
        if pr:
            reads = [k for k in reads if not k.startswith("ps")]
            writes = list(writes) + [k for k in pr if k not in writes]
        o = _Op(eng, fn)
        self._collect(o, reads, writes)
        self.ops[eng].append(o)
        return o

    def dma(self, eng, fn, sem, reads=(), writes=()):
        o = _Op(eng, fn, is_dma=True)
        o.sem = sem
        self.dma_sem_count[id(sem)] += 16
        o.val = self.dma_sem_count[id(sem)]
        self._collect(o, reads, writes)
        self.ops[eng].append(o)
        self.pending_dmas.append(o)
        return o

    def barrier_wait(self, eng, deps):
        o = _Op(eng, None)
        for d in deps:
            o.deps.append(d)
            if not d.is_dma:
                d.signal = True
        self.ops[eng].append(o)
        return o

    def fence(self):
        lasts = []
        for e in self.ENGS:
            for o in reversed(self.ops[e]):
                if (not o.is_dma) and o.fn is not None:
                    lasts.append(o)
                    break
        deps = lasts + self.pending_dmas
        for e in self.ENGS:
            self.barrier_wait(e, deps)
        self.pending_dmas = []
        self.last_w = {}
        self.readers = {}

    def emit(self, block):
        for e in self.ENGS:
            c = 0
            for o in self.ops[e]:
                if o.is_dma:
                    continue
                if o.signal:
                    c += 1
                    o.sem = self.eng_sem[e]
                    o.val = c
        stats = {}

        def run(e, engine):
            known = {}
            nw = 0
            for o in self.ops[e]:
                need = {}
                for d in o.deps:
                    key = id(d.sem)
                    if need.get(key, (None, 0))[1] < d.val:
                        need[key] = (d.sem, d.val)
                for key, (sem, val) in need.items():
                    if known.get(key, 0) < val:
                        engine.wait_ge(sem, val)
                        known[key] = val
                        nw += 1
                if o.fn is None:
                    continue
                ins = o.fn(engine)
                if o.is_dma:
                    ins.then_inc(o.sem, 16)
                elif o.signal:
                    ins.then_inc(o.sem, 1)
            stats[e] = (len(self.ops[e]), nw)

        @block.tensor
        def _(t):
            run("pe", t)

        @block.scalar
        def _(s):
            run("act", s)

        @block.vector
        def _(v):
            run("dve", v)

        @block.gpsimd
        def _(g):
            run("pool", g)

        @block.sync
        def _(s):
            run("sp", s)
        return stats


def build_program(debug=False, stages=3, cut=99):
    nc = bass.Bass("TRN2", target_bir_lowering=False)
    din = lambda name, shape: nc.dram_tensor(name, shape, F32, kind="ExternalInput").ap()
    xown = din("xown", [R, D])
    xext = din("xext", [E, D])
    w_in = din("w_in", [D, 12320])
    w_out = din("w_out", [D, D])
    w_up = din("w_up", [D, 2 * DFF])
    w_down = din("w_down", [DFF, D])
    wgin = din("wgin", [D, 80])
    wgb = din("wgb", [96, 2048])
    gvec_d = din("gvec", [128, 96])
    ghead_d = din("ghead", [128, 2])
    cmw_d = din("cmw", [128, 48])
    fcw_d = din("fcw", [128, NF * 4])
    consts_d = din("consts", [128, 7 * 128])
    flags_d = din("flags", [128, 2])
    y = nc.dram_tensor("y", [R, D], F32, kind="ExternalOutput").ap()
    skind = "ExternalOutput" if debug else "Internal"
    hT_scr = nc.dram_tensor("hT_scr", [128, KC, R + E], BF16, kind=skind).ap()
    xT_scr = nc.dram_tensor("xT_scr", [128, KC, R], F32, kind=skind).ap()
    mT_scr = nc.dram_tensor("mT_scr", [128, KC, R], BF16, kind=skind).ap()

    w_inv = w_in.rearrange("(kc p) c -> p kc c", p=128)
    w_outv = w_out.rearrange("(kc p) c -> p kc c", p=128)
    w_upv = w_up.rearrange("(kc p) c -> p kc c", p=128)
    w_downv = w_down.rearrange("(f p) c -> p f c", p=128)
    wginv = wgin.rearrange("(kc p) c -> p kc c", p=128)

    with ExitStack() as es:
        P = Prog(nc, es)
        sb = lambda ctx, name, shape, dt: ctx.enter_context(nc.sbuf_tensor(name, shape, dt))
        psA = [es.enter_context(nc.psum_tensor("psA%d" % i, [128, 512], F32)) for i in range(2)]
        psG = [es.enter_context(nc.psum_tensor("psG%d" % i, [128, 512], F32)) for i in range(2)]
        psS = es.enter_context(nc.psum_tensor("psS", [128, 512], F32))
        psO = es.enter_context(nc.psum_tensor("psO", [128, 512], F32))
        psT = es.enter_context(nc.psum_tensor("psT", [128, 1024], BF16))
        psX = es.enter_context(nc.psum_tensor("psX", [128, 512], F32))

        def MM(out, lhsT, rhs, start=True, stop=True, r=(), w=()):
            return P.op("pe", lambda e: e.matmul(out, lhsT=lhsT, rhs=rhs, start=start, stop=stop), r, w)

        def TR(out, in_, ident, r=(), w=()):
            return P.op("pe", lambda e: e.transpose(out=out, in_=in_, identity=ident), r, w)

        def ACT(out, in_, func, r=(), w=(), **kw):
            return P.op("act", lambda e: e.activation(out=out, in_=in_, func=func, **kw), r, w)

        def TT(eng, out, in0, in1, op, r=(), w=()):
            return P.op(eng, lambda e: e.tensor_tensor(out=out, in0=in0, in1=in1, op=op), r, w)

        def TS(eng, out, in0, s1, s2, op0, op1=None, r=(), w=()):
            if op1 is None:
                return P.op(eng, lambda e: e.tensor_scalar(out=out, in0=in0, scalar1=s1, scalar2=None, op0=op0), r, w)
            return P.op(eng, lambda e: e.tensor_scalar(out=out, in0=in0, scalar1=s1, scalar2=s2, op0=op0, op1=op1), r, w)

        def STT(eng, out, in0, scalar, in1, op0, op1, r=(), w=()):
            return P.op(eng, lambda e: e.scalar_tensor_tensor(out=out, in0=in0, scalar=scalar, in1=in1, op0=op0, op1=op1), r, w)

        def CP(eng, out, in_, r=(), w=()):
            return P.op(eng, lambda e: e.tensor_copy(out=out, in_=in_), r, w)

        def MSET(eng, ap, val, r=(), w=()):
            return P.op(eng, lambda e: e.memset(ap, val), r, w)

        def DMA(eng, out, in_, sem, r=(), w=()):
            return P.dma(eng, lambda e: e.dma_start(out=out, in_=in_), sem, r, w)

        consts = sb(es, "consts_sb", [128, 7, 128], F32)
        identb = sb(es, "identb", [128, 128], BF16)
        onesb = sb(es, "onesb", [128, 128], BF16)
        gvec = sb(es, "gvecs", [128, 96], F32)
        ghead = sb(es, "gheads", [128, 2], F32)
        cmw = sb(es, "cmws", [128, 16, 3], F32)
        fcw = sb(es, "fcws", [128, NF, 4], F32)
        flags = sb(es, "flagss", [128, 2], F32)
        sc = [P.new_dma_sem("sc%d" % i) for i in range(6)]
        DMA("sp", consts[:].rearrange("p a b -> p (a b)"), consts_d[:, :], sc[0], w=["consts"])
        DMA("sp", gvec[:], gvec_d[:, :], sc[1], w=["gvec"])
        DMA("sp", ghead[:], ghead_d[:, :], sc[2], w=["ghead"])
        DMA("sp", cmw[:].rearrange("p a b -> p (a b)"), cmw_d[:, :], sc[3], w=["cmw"])
        DMA("sp", fcw[:].rearrange("p a b -> p (a b)"), fcw_d[:, :], sc[4], w=["fcw"])
        DMA("sp", flags[:], flags_d[:, :], sc[5], w=["flags"])
        CP("dve", identb[:], consts[:, 0, :], r=["consts"], w=["identb"])
        MSET("dve", onesb[:], 1.0 / 4096.0, w=["onesb"])
        ident = consts[:, 0, :]
        Mle, Mge, Mgt, Mlt = consts[:, 1, :], consts[:, 2, :], consts[:, 3, :], consts[:, 4, :]
        masks = consts[:, 5:7, :]

        with ExitStack() as sa:
            xin = [sb(sa, "xin%d" % i, [128, D], F32) for i in range(2)]
            xTs = [sb(sa, "xTs%d" % i, [128, KC, 128], F32) for i in range(2)]
            sq = sb(sa, "sqA", [128, KC, 128], BF16)
            hTs = [sb(sa, "hTs%d" % i, [128, KC, 128], BF16) for i in range(2)]
            rstdA = sb(sa, "rstdA", [128, 128], F32)
            tmpA = sb(sa, "tmpA", [128, KC, 128], F32)
            s_xin = [P.new_dma_sem("s_xin%d" % i) for i in range(2)]
            s_xst = [P.new_dma_sem("s_xst%d" % i) for i in range(2)]
            s_hst = [P.new_dma_sem("s_hst%d" % i) for i in range(2)]
            blocks = [(xown, i * 128, 128, True, i * 128) for i in range(16)] + [(xown, 2048, 16, True, 2048)]
            blocks += [(xext, i * 128, 128, False, R + i * 128) for i in range(16)]
            for bi, (src, r0, nr, own, c0) in enumerate(blocks):
                sl = bi % 2
                DMA("sp", xin[sl][:nr, :], src[r0:r0 + nr, :], s_xin[sl], w=["xin%d" % sl])
                for g in range(8):
                    bank = psA[g % 2]
                    for j in range(4):
                        kc = g * 4 + j
                        TR(bank[:, j * 128:j * 128 + nr], xin[sl][:nr, kc * 128:(kc + 1) * 128], ident[:nr, :nr],
                           r=["xin%d" % sl, "consts"], w=["psA%d" % (g % 2)])
                    ACT(xTs[sl][:, g * 4:(g + 1) * 4, :nr], bank[:].rearrange("p (a b) -> p a b", a=4)[:, :, :nr], AF.Copy,
                        r=["psA%d" % (g % 2)], w=["xTs%d_%d" % (sl, g)])
                xk = ["xTs%d_%d" % (sl, g) for g in range(8)]
                TT("pool", sq[:, :, :nr], xTs[sl][:, :, :nr], xTs[sl][:, :, :nr], ALU.mult, r=xk, w=["sqA"])
                for kc in range(KC):
                    MM(psX[:, :nr], onesb[:, :], sq[:, kc, :nr], start=(kc == 0), stop=(kc == KC - 1), r=["sqA", "onesb"], w=["psX"])
                ACT(rstdA[:, :nr], psX[:, :nr], AF.Ln, r=["psX"], w=["rstdA"], bias=EPS)
                ACT(rstdA[:, :nr], rstdA[:, :nr], AF.Exp, r=["rstdA"], w=["rstdA"], scale=-0.5)
                TT("dve", tmpA[:, :, :nr], xTs[sl][:, :, :nr], rstdA[:, :nr].unsqueeze(1).to_broadcast([128, KC, nr]), ALU.mult,
                   r=xk + ["rstdA"], w=["tmpA"])
                TT("dve", hTs[sl][:, :, :nr], tmpA[:, :, :nr], gvec[:, 0:32].unsqueeze(2).to_broadcast([128, KC, nr]), ALU.mult,
                   r=["tmpA", "gvec"], w=["hTs%d" % sl])
                DMA("sp", hT_scr[:, :, c0:c0 + nr], hTs[sl][:, :, :nr], s_hst[sl], r=["hTs%d" % sl], w=["hT_scr"])
                if own:
                    DMA("sp", xT_scr[:, :, r0:r0 + nr], xTs[sl][:, :, :nr], s_xst[sl], r=xk, w=["xT_scr"])
        P.fence()

        if stages >= 2:
          with ExitStack() as sB:
            wq = sb(sB, "wq", [128, KC, 128], BF16)
            wk = sb(sB, "wk", [128, KC, 128], BF16)
            wv = sb(sB, "wv", [128, KC, 256], BF16)
            wr = sb(sB, "wr", [128, KC, 256], BF16)
            hts = [sb(sB, "hts%d" % i, [128, KC, 413], BF16) for i in range(2)]
            s_w = {k: P.new_dma_sem("s_w" + k) for k in ("q", "k", "v", "r")}
            s_ht = [P.new_dma_sem("s_ht%d" % i) for i in range(2)]
            s_mst = P.new_dma_sem("s_mst")
            ht_ctr = [0]

            def load_ht(c0, n):
                sl = ht_ctr[0] % 2
                ht_ctr[0] += 1
                DMA("sp", hts[sl][:, :, :n], hT_scr[:, :, c0:c0 + n], s_ht[sl], r=["hT_scr"], w=["hts%d" % sl])
                return hts[sl], "hts%d" % sl

            acc_ctr = [0]

            def proj(wt, wkey, wcols, ht, htkey, n, evac, M=128):
                i = acc_ctr[0] % 2
                acc_ctr[0] += 1
                acc = psA[i]
                for kc in range(KC):
                    MM(acc[:M, :n], wt[:, kc, wcols], ht[:, kc, :n], start=(kc == 0), stop=(kc == KC - 1),
                       r=[wkey, htkey], w=["psA%d" % i])
                evac(acc, "psA%d" % i)

            with ExitStack() as sg:
                wg_sb = sb(sg, "wg_sb", [128, KC, 80], BF16)
                qT = sb(sg, "qT", [128, R], BF16)
                kT = sb(sg, "kT", [128, R], BF16)
                rs = sb(sg, "rs", [128, 2, R], BF16)
                vt = sb(sg, "vt", [128, 2, 413], BF16)
                kte = sb(sg, "kte", [128, ET], BF16)
                kv = sb(sg, "kv", [128, 20, 384], BF16)
                kve = sb(sg, "kve", [128, 2, 384], BF16)
                Sbb = sb(sg, "Sbb", [128, 20, 256], BF16)
                aT = sb(sg, "aT", [96, R], F32)
                wgb_sb = sb(sg, "wgb_sb", [96, 2048], F32)
                mst = sb(sg, "mst", [128, 2, R], BF16)
                ex = sb(sg, "ex", [128, 256], F32)
                lsb = sb(sg, "lsb", [128, 256], F32)
                E1 = sb(sg, "E1", [128, 256], F32)
                E2 = sb(sg, "E2", [128, 256], F32)
                E3 = sb(sg, "E3", [128, 256], F32)
                qq = sb(sg, "qq", [128, 2, 128], BF16)
                kk = sb(sg, "kk", [128, 2, 128], BF16)
                khf = sb(sg, "khf", [128, 128], BF16)
                khb = sb(sg, "khb", [128, 128], BF16)
                PT = sb(sg, "PT", [128, 2, 128], BF16)
                on = sb(sg, "on", [128, 256], BF16)
                junk = sb(sg, "junk", [128, 256], BF16)
                Sf = sb(sg, "Sf", [128, 256], F32)
                Sb_ = sb(sg, "Sb_", [128, 256], F32)
                Se = sb(sg, "Se", [128, 256], F32)
                Sfb = sb(sg, "Sfb", [128, 256], BF16)
                ss1 = sb(sg, "ss1", [128, 1], F32)
                rstd1 = sb(sg, "rstd1", [128, 1], F32)
                s_wg = P.new_dma_sem("s_wg")
                s_wgb = P.new_dma_sem("s_wgb")

                DMA("pool", wg_sb[:], wginv[:, :, :], s_wg, w=["wg_sb"])
                DMA("sp", wgb_sb[:], wgb[:, :], s_wgb, w=["wgb_sb"])
                MSET("dve", aT[:, :], 1.0, w=["aT"])

                for ti in range(5):
                    s0, e0 = TB[ti], TB[ti + 1]
                    n = e0 - s0
                    ht, hk = load_ht(s0, n)

                    def ev(acc, ak, s0=s0, n=n):
                        ACT(aT[0:16, s0:s0 + n], acc[0:16, :n], AF.Copy, r=[ak], w=["aT"])
                        ACT(aT[32:48, s0:s0 + n], acc[32:48, :n], AF.Copy, r=[ak], w=["aT"])
                    proj(wg_sb, "wg_sb", slice(0, 80), ht, hk, n, ev, M=80)
                for et in range(E // ET):
                    ht, hk = load_ht(R + et * ET, ET)

                    def ev(acc, ak, et=et):
                        ACT(aT[64:80, et * ET:(et + 1) * ET], acc[64:80, :ET], AF.Copy, r=[ak], w=["aT"])
                    proj(wg_sb, "wg_sb", slice(0, 80), ht, hk, ET, ev, M=80)

                def gates_decays(cols0, C, j, frow, do_b):
                    if do_b:
                        MM(psG[0][:C, 0:256], aT[0:64, cols0:cols0 + C], wgb_sb[0:64, j * 256:(j + 1) * 256],
                           r=["aT", "wgb_sb"], w=["psG0"])
                        W = 256
                    else:
                        MM(psG[0][:C, 0:128], aT[frow:frow + 32, cols0:cols0 + C], wgb_sb[frow:frow + 32, j * 256:j * 256 + 128],
                           r=["aT", "wgb_sb"], w=["psG0"])
                        W = 128
                    ACT(ex[:C, :W], psG[0][:C, :W], AF.Exp, r=["psG0"], w=["ex"], scale=-1.0)
                    ACT(lsb[:C, :W], ex[:C, :W], AF.Ln, r=["ex"], w=["lsb"], bias=1.0)
                    MM(psG[1][:, 0:C], lsb[:C, 0:128], Mle[:C, :C], r=["lsb", "consts"], w=["psG1"])
                    MM(psG[1][:C, 256:384], Mgt[:C, :C], lsb[:C, 0:128], r=["lsb", "consts"], w=["psG1"])
                    if do_b:
                        MM(psG[1][:, 128:128 + C], lsb[:C, 128:256], Mge[:C, :C], r=["lsb", "consts"], w=["psG1"])
                        MM(psG[1][:C, 384:512], Mlt[:C, :C], lsb[:C, 128:256], r=["lsb", "consts"], w=["psG1"])

                for j in range(8 if cut >= 6 else (1 if cut >= 2 else 0)):
                    DMA("pool", wq[:], w_inv[:, :, OFF_Q + j * 128:OFF_Q + (j + 1) * 128], s_w["q"], w=["wq"])
                    DMA("pool", wk[:], w_inv[:, :, OFF_K + j * 128:OFF_K + (j + 1) * 128], s_w["k"], w=["wk"])
                    DMA("pool", wv[:], w_inv[:, :, OFF_V + j * 256:OFF_V + (j + 1) * 256], s_w["v"], w=["wv"])
                    DMA("pool", wr[:], w_inv[:, :, OFF_R + j * 256:OFF_R + (j + 1) * 256], s_w["r"], w=["wr"])
                    MSET("dve", Se[:], 0.0, w=["Se"])
                    for et in range(E // ET):
                        ht, hk = load_ht(R + et * ET, ET)
                        proj(wk, "wk", slice(0, 128), ht, hk, ET,
                             lambda acc, ak: ACT(kte[:, :ET], acc[:, :ET], AF.Copy, r=[ak], w=["kte"]))
                        for h in range(2):
                            proj(wv, "wv", slice(h * 128, (h + 1) * 128), ht, hk, ET,
                                 lambda acc, ak, h=h: ACT(vt[:, h, :ET], acc[:, :ET], AF.Copy, r=[ak], w=["vt%d" % h]))
                        for ci in range(ET // 128):
                            lc = ci * 128
                            TR(psT[:, 0:128], kte[:, lc:lc + 128], identb[:, :], r=["kte", "identb"], w=["psT"])
                            for h in range(2):
                                TR(psT[:, 128 + h * 128:256 + h * 128], vt[:, h, lc:lc + 128], identb[:, :],
                                   r=["vt%d" % h, "identb"], w=["psT"])
                            CP("dve", kve[:, ci, :], psT[:, 0:384], r=["psT"], w=["kve%d" % ci])
                            ec = et * ET + lc
                            gates_decays(ec, 128, j, 64, False)
                            ACT(E1[:, 0:128], psG[1][:, 0:128], AF.Exp, r=["psG1"], w=["E1"])
                            ACT(E3[:, 0:128], psG[1][:, 256:384], AF.Exp, r=["psG1"], w=["E3"])
                            TT("dve", khf[:, :], kve[:, ci, 0:128], E3[:, 0:128], ALU.mult, r=["kve%d" % ci, "E3"], w=["khf"])
                            MM(psX[:, 0:256], khf[:, :], kve[:, ci, 128:384], r=["khf", "kve%d" % ci], w=["psX"])
                            STT("dve", Se[:], Se[:], E1[:, 127:128], psX[:, 0:256], ALU.mult, ALU.add,
                                r=["Se", "E1", "psX"], w=["Se"])
                    TS("dve", Sf[:], Se[:], flags[:, 0:1], None, ALU.mult, r=["Se", "flags"], w=["Sf"])
                    TS("dve", Sb_[:], Se[:], flags[:, 1:2], None, ALU.mult, r=["Se", "flags"], w=["Sb"])
                    for ti in range(5 if cut >= 3 else 0):
                        s0, e0 = TB[ti], TB[ti + 1]
                        n = e0 - s0
                        ht, hk = load_ht(s0, n)
                        proj(wq, "wq", slice(0, 128), ht, hk, n,
                             lambda acc, ak, s0=s0, n=n: ACT(qT[:, s0:s0 + n], acc[:, :n], AF.Copy, r=[ak], w=["qT"], scale=128.0 ** -0.5))
                        proj(wk, "wk", slice(0, 128), ht, hk, n,
                             lambda acc, ak, s0=s0, n=n: ACT(kT[:, s0:s0 + n], acc[:, :n], AF.Copy, r=[ak], w=["kT"]))
                        for h in range(2):
                            proj(wv, "wv", slice(h * 128, (h + 1) * 128), ht, hk, n,
                                 lambda acc, ak, h=h, n=n: ACT(vt[:, h, :n], acc[:, :n], AF.Copy, r=[ak], w=["vt%d" % h]))
                        for h in range(2):
                            proj(wr, "wr", slice(h * 128, (h + 1) * 128), ht, hk, n,
                                 lambda acc, ak, h=h, s0=s0, n=n: ACT(rs[:, h, s0:s0 + n], acc[:, :n], AF.Silu, r=[ak], w=["rs"]))
                        for c in range(ti * 4, ti * 4 + 4):
                            cs, C = CHUNKS[c]
                            lc = cs - s0
                            TR(psT[:C, 0:128], kT[:, cs:cs + C], identb[:, :], r=["kT", "identb"], w=["psT"])
                            for h in range(2):
                                TR(psT[:C, 128 + h * 128:256 + h * 128], vt[:, h, lc:lc + C], identb[:, :],
                                   r=["vt%d" % h, "identb"], w=["psT"])
                            CP("dve", kv[:C, c, :], psT[:C, 0:384], r=["psT"], w=["kv"])
                    for c in reversed(range(20 if cut >= 4 else 0)):
                        cs, C = CHUNKS[c]
                        gates_decays(cs, C, j, 0, True)
                        ACT(E1[:, 0:256], psG[1][:, 0:256], AF.Exp, r=["psG1"], w=["E1"])
                        ACT(E3[:C, 0:256], psG[1][:C, 256:512], AF.Exp, r=["psG1"], w=["E3"])
                        ACT(Sbb[:, c, :], Sb_[:], AF.Copy, r=["Sb"], w=["Sbb"])
                        TT("dve", khb[:C, :], kv[:C, c, 0:128], E3[:C, 128:256], ALU.mult, r=["kv", "E3"], w=["khb"])
                        MM(psX[:, 0:256], khb[:C, :], kv[:C, c, 128:384], r=["khb", "kv"], w=["psX"])
                        STT("dve", Sb_[:], Sb_[:], E1[:, 128:129], psX[:, 0:256], ALU.mult, ALU.add,
                            r=["Sb", "E1", "psX"], w=["Sb"])
                    for c in range(20 if cut >= 5 else 0):
                        cs, C = CHUNKS[c]
                        gates_decays(cs, C, j, 0, True)
                        ACT(E1[:, 0:256], psG[1][:, 0:256], AF.Exp, r=["psG1"], w=["E1"])
                        ACT(E2[:, 0:256], psG[1][:, 0:256], AF.Exp, r=["psG1"], w=["E2"], scale=-1.0)
                        ACT(E3[:C, 0:256], psG[1][:C, 256:512], AF.Exp, r=["psG1"], w=["E3"])
                        TT("dve", qq[:, :, :C], qT[:, cs:cs + C].unsqueeze(1).to_broadcast([128, 2, C]),
                           E1[:].rearrange("p (a b) -> p a b", a=2)[:, :, :C], ALU.mult, r=["qT", "E1"], w=["qq"])
                        TT("dve", kk[:, :, :C], kT[:, cs:cs + C].unsqueeze(1).to_broadcast([128, 2, C]),
                           E2[:].rearrange("p (a b) -> p a b", a=2)[:, :, :C], ALU.mult, r=["kT", "E2"], w=["kk"])
                        TT("dve", khf[:C, :], kv[:C, c, 0:128], E3[:C, 0:128], ALU.mult, r=["kv", "E3"], w=["khf"])
                        MM(psS[:C, 0:C], kk[:, 0, :C], qq[:, 0, :C], r=["kk", "qq"], w=["psS"])
                        MM(psS[:C, 128:128 + C], kk[:, 1, :C], qq[:, 1, :C], r=["kk", "qq"], w=["psS"])
                        TT("dve", PT[:C, :, :C], psS[:C, 0:256].rearrange("p (a b) -> p a b", a=2)[:, :, :C], masks[:C, :, :C],
                           ALU.mult, r=["psS", "consts"], w=["PT"])
                        ACT(Sfb[:], Sf[:], AF.Copy, r=["Sf"], w=["Sfb"])
                        MM(psO[:C, 0:256], PT[:C, 0, :C], kv[:C, c, 128:384], start=True, stop=False, r=["PT", "kv"], w=["psOo"])
                        MM(psO[:C, 0:256], PT[:C, 1, :C], kv[:C, c, 128:384], start=False, stop=False, r=["PT", "kv"], w=["psOo"])
                        MM(psO[:C, 0:256], qq[:, 0, :C], Sfb[:, :], start=False, stop=False, r=["qq", "Sfb"], w=["psOo"])
                        MM(psO[:C, 0:256], qq[:, 1, :C], Sbb[:, c, :], start=False, stop=True, r=["qq", "Sbb"], w=["psOo"])
                        MM(psX[:, 0:256], khf[:C, :], kv[:C, c, 128:384], r=["khf", "kv"], w=["psX"])
                        STT("dve", Sf[:], Sf[:], E1[:, C - 1:C], psX[:, 0:256], ALU.mult, ALU.add,
                            r=["Sf", "E1", "psX"], w=["Sf"])
                        MSET("dve", ss1[:], 0.0, w=["ss1"])
                        ACT(junk[:C, :], psO[:C, 0:256], AF.Square, r=["psOo", "ss1"], w=["junk", "ss1"], accum_out=ss1[:C, 0:1])
                        ACT(rstd1[:C, :], ss1[:C, :], AF.Ln, r=["ss1"], w=["rstd1"], scale=1.0 / 256, bias=EPS)
                        ACT(rstd1[:C, :], rstd1[:C, :], AF.Exp, r=["rstd1"], w=["rstd1"], scale=-0.5)
                        TS("dve", on[:C, :], psO[:C, 0:256], rstd1[:C, 0:1], None, ALU.mult, r=["psOo", "rstd1"], w=["on"])
                        for h in range(2):
                            TR(psT[:, 512 + h * 128:512 + h * 128 + C], on[:C, h * 128:(h + 1) * 128], identb[:C, :C],
                               r=["on", "identb"], w=["psT"])
                        for h in range(2):
                            STT("dve", mst[:, h, cs:cs + C], psT[:, 512 + h * 128:512 + h * 128 + C], ghead[:, h:h + 1],
                                rs[:, h, cs:cs + C], ALU.mult, ALU.mult, r=["psT", "ghead", "rs"], w=["mst"])
                    DMA("sp", mT_scr[:, 2 * j:2 * j + 2, :], mst[:, :, :], s_mst, r=["mst"], w=["mT_scr"])
            P.fence()
            with ExitStack() as scv:
                ccs = sb(scv, "ccs", [128, 413], F32)
                prod = sb(scv, "prod", [128, R + 2], F32)
                cbs = sb(scv, "cbs", [128, R], F32)
                t1 = sb(scv, "t1", [128, R], F32)
                mcv = sb(scv, "mcv", [128, R], BF16)
                s_mcv = P.new_dma_sem("s_mcv")
                MSET("dve", prod[:, 0:1], 0.0, w=["prod"])
                MSET("dve", prod[:, R + 1:R + 2], 0.0, w=["prod"])
                for cg in range(16 if cut >= 7 else 0):
                    DMA("pool", wq[:], w_inv[:, :, OFF_CB + cg * 128:OFF_CB + (cg + 1) * 128], s_w["q"], w=["wq"])
                    DMA("pool", wk[:], w_inv[:, :, OFF_CC + cg * 128:OFF_CC + (cg + 1) * 128], s_w["k"], w=["wk"])
                    DMA("pool", wv[:, :, 0:128], w_inv[:, :, OFF_CH + cg * 128:OFF_CH + (cg + 1) * 128], s_w["v"], w=["wv"])
                    for ti in range(5):
                        s0, e0 = TB[ti], TB[ti + 1]
                        n = e0 - s0
                        ht, hk = load_ht(s0, n)
                        proj(wq, "wq", slice(0, 128), ht, hk, n,
                             lambda acc, ak, s0=s0, n=n: ACT(cbs[:, s0:s0 + n], acc[:, :n], AF.Copy, r=[ak], w=["cbs"]))
                        proj(wk, "wk", slice(0, 128), ht, hk, n,
                             lambda acc, ak, n=n: ACT(ccs[:, :n], acc[:, :n], AF.Copy, r=[ak], w=["ccs"]))
                        proj(wv, "wv", slice(0, 128), ht, hk, n,
                             lambda acc, ak, s0=s0, n=n: TT("dve", prod[:, 1 + s0:1 + s0 + n], acc[:, :n], ccs[:, :n], ALU.mult,
                                                            r=[ak, "ccs"], w=["prod"]))
                    TS("dve", t1[:, :], prod[:, 0:R], cmw[:, cg, 0:1], None, ALU.mult, r=["prod", "cmw"], w=["t1"])
                    STT("dve", t1[:, :], prod[:, 1:R + 1], cmw[:, cg, 1:2], t1[:, :], ALU.mult, ALU.add, r=["prod", "cmw", "t1"], w=["t1"])
                    STT("dve", t1[:, :], prod[:, 2:R + 2], cmw[:, cg, 2:3], t1[:, :], ALU.mult, ALU.add, r=["prod", "cmw", "t1"], w=["t1"])
                    TT("dve", mcv[:, :], t1[:, :], cbs[:, :], ALU.mult, r=["t1", "cbs"], w=["mcv"])
                    DMA("sp", mT_scr[:, 16 + cg, :], mcv[:, :], s_mcv, r=["mcv"], w=["mT_scr"])
          P.fence()

        if stages >= 3:
          with ExitStack() as sC:
            X1 = sb(sC, "X1", [128, KC, 415], F32)
            MH = sb(sC, "MH", [128, KC, 415], BF16)
            aTt = sb(sC, "aTt", [128, 11, 413], BF16)
            wring = [sb(sC, "wring%d" % i, [128, KC, 256], BF16) for i in range(3)]
            wdring = [sb(sC, "wdring%d" % i, [128, 11, 512], BF16) for i in range(2)]
            sqt = [sb(sC, "sqt%d" % i, [128, 415], BF16) for i in range(2)]
            rstdt = sb(sC, "rstdt", [128, 415], F32)
            c1 = [sb(sC, "c1_%d" % i, [128, 413], F32) for i in range(2)]
            sg_ = [sb(sC, "sg%d" % i, [128, 413], F32) for i in range(2)]
            ost = [sb(sC, "ost%d" % i, [128, 2048], F32) for i in range(2)]
            s_wr = [[P.new_dma_sem("s_wr%d_%d" % (i, h)) for h in range(2)] for i in range(3)]
            s_wd = [P.new_dma_sem("s_wd%d" % i) for i in range(2)]
            s_x1 = P.new_dma_sem("s_x1")
            s_mh = P.new_dma_sem("s_mh")
            s_ost = [P.new_dma_sem("s_ost%d" % i) for i in range(2)]
            wr_ctr = [0]
            wd_ctr = [0]
            ost_ctr = [0]
            out_dmas = []
            for ti in range(5):
                s0, e0 = TB[ti], TB[ti + 1]
                lo, hi = max(s0 - 1, 0), min(e0 + 1, R)
                N = e0 - s0 + 2
                NV = N - 2
                off = lo - (s0 - 1)
                if ti == 0:
                    MSET("dve", X1[:, :, 0:1], 0.0, w=["X1"])
                    MSET("dve", MH[:, :, 0:1], 0.0, w=["MH"])
                if ti == 4:
                    MSET("dve", X1[:, :, N - 1:N], 0.0, w=["X1"])
                    MSET("dve", MH[:, :, N - 1:N], 0.0, w=["MH"])
                DMA("sp", MH[:, :, off:off + hi - lo], mT_scr[:, :, lo:hi], s_mh, r=["mT_scr"], w=["MH"])
                DMA("sp", X1[:, :, off:off + hi - lo], xT_scr[:, :, lo:hi], s_x1, r=["xT_scr"], w=["X1"])
                for cb2 in range(16):
                    sl = wr_ctr[0] % 3
                    wr_ctr[0] += 1
                    DMA("pool", wring[sl][:, :, :], w_outv[:, :, cb2 * 256:(cb2 + 1) * 256], s_wr[sl][0], w=["wring%d" % sl])
                    for h in range(2):
                        cb = cb2 * 2 + h
                        acc = psA[cb % 2]
                        for kc in range(KC):
                            MM(acc[:, :N], wring[sl][:, kc, h * 128:(h + 1) * 128], MH[:, kc, :N], start=(kc == 0), stop=(kc == KC - 1),
                               r=["wring%d" % sl, "MH"], w=["psA%d" % (cb % 2)])
                        TT("dve", X1[:, cb, :N], acc[:, :N], X1[:, cb, :N], ALU.add, r=["psA%d" % (cb % 2), "X1"], w=["X1c%d" % cb])
                        ACT(sqt[cb % 2][:, :N], X1[:, cb, :N], AF.Square, r=["X1c%d" % cb], w=["sqt%d" % (cb % 2)])
                        MM(psX[:, :N], onesb[:, :], sqt[cb % 2][:, :N], start=(cb == 0), stop=(cb == 31), r=["sqt%d" % (cb % 2), "onesb"], w=["psX"])
                xck = ["X1c%d" % cb for cb in range(32)]
                ACT(rstdt[:, :N], psX[:, :N], AF.Ln, r=["psX"], w=["rstdt"], bias=EPS)
                ACT(rstdt[:, :N], rstdt[:, :N], AF.Exp, r=["rstdt"], w=["rstdt"], scale=-0.5)
                for kc in range(KC):
                    STT("dve", MH[:, kc, :N], X1[:, kc, :N], gvec[:, 32 + kc:33 + kc], rstdt[:, :N], ALU.mult, ALU.mult,
                        r=["X1c%d" % kc, "gvec", "rstdt"], w=["MH"])
                for (f0, nf) in FGROUPS:
                    for fl in range(nf):
                        f = f0 + fl
                        sl = wr_ctr[0] % 3
                        wr_ctr[0] += 1
                        DMA("pool", wring[sl][:, :, 0:128], w_upv[:, :, f * 128:(f + 1) * 128], s_wr[sl][0], w=["wring%d" % sl])
                        DMA("pool", wring[sl][:, :, 128:256], w_upv[:, :, DFF + f * 128:DFF + (f + 1) * 128], s_wr[sl][1], w=["wring%db" % sl])
                        pu, pg = psA[f % 2], psG[f % 2]
                        for kc in range(KC):
                            MM(pu[:, :N], wring[sl][:, kc, 0:128], MH[:, kc, :N], start=(kc == 0), stop=(kc == KC - 1),
                               r=["wring%d" % sl, "MH"], w=["psA%d" % (f % 2)])
                        for kc in range(KC):
                            MM(pg[:, :N], wring[sl][:, kc, 128:256], MH[:, kc, :N], start=(kc == 0), stop=(kc == KC - 1),
                               r=["wring%db" % sl, "MH"], w=["psG%d" % (f % 2)])
                        cc1 = c1[f % 2]
                        TS("dve", cc1[:, :NV], pg[:, 0:NV], fcw[:, f, 0:1], fcw[:, f, 3:4], ALU.mult, ALU.add,
                           r=["psG%d" % (f % 2), "fcw"], w=["c1_%d" % (f % 2)])
                        STT("dve", cc1[:, :NV], pg[:, 1:NV + 1], fcw[:, f, 1:2], cc1[:, :NV], ALU.mult, ALU.add,
                            r=["psG%d" % (f % 2), "fcw", "c1_%d" % (f % 2)], w=["c1_%d" % (f % 2)])
                        STT("dve", cc1[:, :NV], pg[:, 2:NV + 2], fcw[:, f, 2:3], cc1[:, :NV], ALU.mult, ALU.add,
                            r=["psG%d" % (f % 2), "fcw", "c1_%d" % (f % 2)], w=["c1_%d" % (f % 2)])
                        ACT(sg_[f % 2][:, :NV], cc1[:, :NV], AF.Silu, r=["c1_%d" % (f % 2)], w=["sg%d" % (f % 2)])
                        TT("dve", aTt[:, fl, :NV], sg_[f % 2][:, :NV], pu[:, 1:NV + 1], ALU.mult,
                           r=["sg%d" % (f % 2), "psA%d" % (f % 2)], w=["aTt"])
                    for cb4 in range(8):
                        dl = wd_ctr[0] % 2
                        wd_ctr[0] += 1
                        DMA("pool", wdring[dl][:, :nf, :], w_downv[:, f0:f0 + nf, cb4 * 512:(cb4 + 1) * 512], s_wd[dl], w=["wdring%d" % dl])
                        for h in range(4):
                            cb = cb4 * 4 + h
                            yps = psS if cb % 2 == 0 else psO
                            yk = "psS" if cb % 2 == 0 else "psO"
                            for fl in range(nf):
                                MM(yps[:, :NV], wdring[dl][:, fl, h * 128:(h + 1) * 128], aTt[:, fl, :NV], start=(fl == 0), stop=(fl == nf - 1),
                                   r=["wdring%d" % dl, "aTt"], w=[yk])
                            TT("dve", X1[:, cb, 1:NV + 1], yps[:, :NV], X1[:, cb, 1:NV + 1], ALU.add, r=[yk, "X1c%d" % cb], w=["X1c%d" % cb])
                for cb in range(32):
                    ACT(sqt[cb % 2][:, :NV], X1[:, cb, 1:NV + 1], AF.Square, r=["X1c%d" % cb], w=["sqt%d" % (cb % 2)])
                    MM(psX[:, :NV], onesb[:, :], sqt[cb % 2][:, :NV], start=(cb == 0), stop=(cb == 31), r=["sqt%d" % (cb % 2), "onesb"], w=["psX"])
                ACT(rstdt[:, :NV], psX[:, :NV], AF.Ln, r=["psX"], w=["rstdt"], bias=EPS)
                ACT(rstdt[:, :NV], rstdt[:, :NV], AF.Exp, r=["rstdt"], w=["rstdt"], scale=-0.5)
                for kc in range(KC):
                    STT("dve", X1[:, kc, 1:NV + 1], X1[:, kc, 1:NV + 1], gvec[:, 64 + kc:65 + kc], rstdt[:, :NV], ALU.mult, ALU.mult,
                        r=["X1c%d" % kc, "gvec", "rstdt"], w=["X1c%d" % kc])
                for c in range(ti * 4, ti * 4 + 4):
                    cs, C = CHUNKS[c]
                    lc = cs - s0 + 1
                    for half in range(2):
                        ol = ost_ctr[0] % 2
                        ost_ctr[0] += 1
                        for g4 in range(4):
                            bank = psG[g4 % 2]
                            for jj in range(4):
                                kc = half * 16 + g4 * 4 + jj
                                TR(bank[:C, jj * 128:(jj + 1) * 128], X1[:, kc, lc:lc + C], ident[:, :],
                                   r=["X1c%d" % kc, "consts"], w=["psG%d" % (g4 % 2)])
                            ACT(ost[ol][:C, g4 * 512:(g4 + 1) * 512], bank[:C, :], AF.Copy, r=["psG%d" % (g4 % 2)], w=["ost%d" % ol])
                        out_dmas.append(DMA("sp", y[cs:cs + C, half * 2048:(half + 1) * 2048], ost[ol][:C, :], s_ost[ol], r=["ost%d" % ol]))
                P.op("dve", lambda e: e.memset(rstdt[:, 0:1], 0.0), reads=xck + ["MH"], writes=["X1", "MH", "rstdt"] + xck)
          P.fence()
        else:
            P.fence()
        P.barrier_wait("sp", list(P.ops["sp"][-1].deps))
        block = es.enter_context(nc.Block())
        stats = P.emit(block)
    return nc, stats


def _consts():
    a = np.arange(128)[:, None]
    b = np.arange(128)[None, :]
    c = np.zeros((128, 7, 128), np.float32)
    c[:, 0] = (a == b)
    c[:, 1] = (a <= b) * (-1.0 / 16.0)
    c[:, 2] = (a >= b) * (-1.0 / 16.0)
    c[:, 3] = (a > b) * (-1.0 / 16.0)
    c[:, 4] = (a < b) * (-1.0 / 16.0)
    c[:, 5] = (a <= b)
    c[:, 6] = (a > b)
    return c.reshape(128, 7 * 128)


def _prepare_inputs(x_prompt, x_sample, meta_tokens, mix_norm_g, w_in, w_gate2, b_gate2, head_norm_g,
                    conv_mix_w, w_out, ffn_norm_g, w_up, ffn_conv_w, ffn_conv_b, w_down, final_norm_g):
    f = lambda a: np.ascontiguousarray(np.asarray(a, dtype=np.float32))
    x_prompt, x_sample, meta = f(x_prompt), f(x_sample), f(meta_tokens)
    w_in0, w_out0, w_up0, w_down0 = f(w_in[0]), f(w_out[0]), f(w_up[0]), f(w_down[0])
    wg2, bg2 = f(w_gate2[0]), f(b_gate2[0])
    gvec = np.concatenate([f(mix_norm_g[0]).reshape(32, 128).T, f(ffn_norm_g[0]).reshape(32, 128).T,
                           f(final_norm_g).reshape(32, 128).T], axis=1)
    ghead = f(head_norm_g[0]).reshape(2, 128).T
    cmw = f(conv_mix_w[0]).reshape(3, 16, 128).transpose(2, 1, 0).reshape(128, 48)
    fcw = np.concatenate([f(ffn_conv_w[0]), f(ffn_conv_b[0])[None]], axis=0).reshape(4, NF, 128).transpose(2, 1, 0).reshape(128, NF * 4)
    consts = _consts()
    shared = dict(w_in=w_in0, w_out=w_out0, w_up=w_up0, w_down=w_down0,
                  gvec=f(gvec), ghead=f(ghead), cmw=f(cmw), fcw=f(fcw), consts=consts)

    def core(xown, xext, ext_dir, flag_f, flag_b):
        wgin = np.zeros((D, 80), np.float32)
        wgin[:, 0:16] = w_in0[:, 6144:6160]
        wgin[:, 32:48] = w_in0[:, 6160:6176]
        wgin[:, 64:80] = w_in0[:, 6144 + 16 * ext_dir:6160 + 16 * ext_dir]
        wgb = np.zeros((96, 8, 256), np.float32)
        wgb[0:16, :, 0:128] = wg2[0].reshape(16, 8, 128)
        wgb[16, :, 0:128] = bg2[0].reshape(8, 128)
        wgb[32:48, :, 128:256] = wg2[1].reshape(16, 8, 128)
        wgb[48, :, 128:256] = bg2[1].reshape(8, 128)
        wgb[64:80, :, 0:128] = wg2[ext_dir].reshape(16, 8, 128)
        wgb[80, :, 0:128] = bg2[ext_dir].reshape(8, 128)
        wgb = wgb.reshape(96, 2048)
        flags = np.zeros((128, 2), np.float32)
        flags[:, 0] = flag_f
        flags[:, 1] = flag_b
        m = dict(shared)
        m.update(xown=f(xown), xext=f(xext), wgin=wgin, wgb=wgb, flags=flags)
        return m

    maps = []
    for b in range(2):
        xa = np.concatenate([meta, x_prompt[b, 0:2048]], axis=0)
        ea = x_prompt[b, 2048:4096][::-1]
        maps.append(core(xa, ea, 1, 0.0, 1.0))
        xb = x_prompt[b, 2032:4096]
        eb = np.concatenate([meta, x_prompt[b, 0:2032]], axis=0)
        maps.append(core(xb, eb, 0, 1.0, 0.0))
    for s in range(4):
        xs = np.concatenate([meta, x_sample[s]], axis=0)
        maps.append(core(xs, np.zeros((E, D), np.float32), 0, 0.0, 0.0))
    return maps


_NC_CACHE = {}


def kernel(x_prompt, x_sample, meta_tokens, mix_norm_g, w_in, w_gate2, b_gate2, head_norm_g,
           conv_mix_w, w_out, ffn_norm_g, w_up, ffn_conv_w, ffn_conv_b, w_down, final_norm_g):
    maps = _prepare_inputs(x_prompt, x_sample, meta_tokens, mix_norm_g, w_in, w_gate2, b_gate2, head_norm_g,
                           conv_mix_w, w_out, ffn_norm_g, w_up, ffn_conv_w, ffn_conv_b, w_down, final_norm_g)
    if "nc" not in _NC_CACHE:
        _NC_CACHE["nc"] = build_program()[0]
    nc = _NC_CACHE["nc"]
    res = run_bass_kernel_spmd(nc, maps, core_ids=list(range(8)))
    ys = [np.asarray(r["y"], dtype=np.float32) for r in res.results]
    y_prompt = np.zeros((2, 4096, D), np.float32)
    y_sample = np.zeros((4, 2048, D), np.float32)
    for b in range(2):
        y_prompt[b, 0:2040] = ys[2 * b][16:16 + 2040]
        y_prompt[b, 2040:4096] = ys[2 * b + 1][8:2064]
    for s in range(4):
        y_sample[s] = ys[4 + s][16:2064]
    return (y_prompt, y_sample)
```

```python
import os
import numpy as np
from contextlib import ExitStack
import concourse.bass as bass
import concourse.mybir as mybir
from concourse.bass_utils import run_bass_kernel_spmd

F32 = mybir.dt.float32
BF16 = mybir.dt.bfloat16
AF = mybir.ActivationFunctionType
ALU = mybir.AluOpType

D = 4096
KC = 32
R = 2064
E = 2048
DFF = 11008
NF = 86
EPS = 1e-6
TB = [0, 413, 826, 1239, 1652, 2064]
ET = 256
CHUNKS = []
for _i in range(5):
    _s, _e = TB[_i], TB[_i + 1]
    _w = _e - _s
    _c0 = _w - 309
    CHUNKS.append((_s, _c0))
    for _k in range(3):
        CHUNKS.append((_s + _c0 + 103 * _k, 103))
FGROUPS = []
_f = 0
for _n in [11, 11, 11, 11, 11, 11, 10, 10]:
    FGROUPS.append((_f, _n))
    _f += _n
OFF_Q, OFF_K, OFF_V, OFF_R, OFF_CB, OFF_CC, OFF_CH = 0, 1024, 2048, 4096, 6176, 8224, 10272


class _Op:
    __slots__ = ("eng", "fn", "deps", "is_dma", "sem", "val", "signal")

    def __init__(self, eng, fn, is_dma=False):
        self.eng = eng
        self.fn = fn
        self.deps = []
        self.is_dma = is_dma
        self.sem = None
        self.val = 0
        self.signal = False


class Prog:
    ENGS = ("pe", "act", "dve", "pool", "sp")

    def __init__(self, nc, es):
        self.nc = nc
        self.es = es
        self.ops = {e: [] for e in self.ENGS}
        self.last_w = {}
        self.readers = {}
        self.eng_sem = {e: es.enter_context(nc.semaphore("prog_" + e)) for e in self.ENGS}
        self.dma_sem_count = {}
        self.pending_dmas = []

    def new_dma_sem(self, name):
        s = self.es.enter_context(self.nc.semaphore(name))
        self.dma_sem_count[id(s)] = 0
        return s

    def _collect(self, op, reads, writes):
        deps = []
        for k in reads:
            w = self.last_w.get(k)
            if w is not None:
                deps.append(w)
        for k in writes:
            w = self.last_w.get(k)
            if w is not None:
                deps.append(w)
            deps.extend(self.readers.get(k, ()))
        seen = set()
        for d in deps:
            if d is op or id(d) in seen:
                continue
            seen.add(id(d))
            if (not d.is_dma) and (not op.is_dma) and d.eng == "pe" and op.eng == "pe":
                continue
            op.deps.append(d)
            if not d.is_dma:
                d.signal = True
        for k in writes:
            self.last_w[k] = op
            self.readers[k] = []
        for k in reads:
            self.readers.setdefault(k, []).append(op)

    def op(self, eng, fn, reads=(), writes=()):
        pr = [k for k in reads if k.startswith("ps")]
        if pr:
            reads = [k for k in reads if not k.startswith("ps")]
            writes = list(writes) + [k for k in pr if k not in writes]
        o = _Op(eng, fn)
        self._collect(o, reads, writes)
        self.ops[eng].append(o)
        return o

    def dma(self, eng, fn, sem, reads=(), writes=()):
        o = _Op(eng, fn, is_dma=True)
        o.sem = sem
        self.dma_sem_count[id(sem)] += 16
        o.val = self.dma_sem_count[id(sem)]
        self._collect(o, reads, writes)
        self.ops[eng].append(o)
        self.pending_dmas.append(o)
        return o

    def barrier_wait(self, eng, deps):
        o = _Op(eng, None)
        for d in deps:
            o.deps.append(d)
            if not d.is_dma:
                d.signal = True
        self.ops[eng].append(o)
        return o

    def fence(self):
        lasts = []
        for e in self.ENGS:
            for o in reversed(self.ops[e]):
                if (not o.is_dma) and o.fn is not None:
                    lasts.append(o)
                    break
        deps = lasts + self.pending_dmas
        for e in self.ENGS:
            self.barrier_wait(e, deps)
        self.pending_dmas = []
        self.last_w = {}
        self.readers = {}

    def emit(self, block):
        for e in self.ENGS:
            c = 0
            for o in self.ops[e]:
                if o.is_dma:
                    continue
                if o.signal:
                    c += 1
                    o.sem = self.eng_sem[e]
                    o.val = c
        stats = {}

        def run(e, engine):
            known = {}
            nw = 0
            for o in self.ops[e]:
                need = {}
                for d in o.deps:
                    key = id(d.sem)
                    if need.get(key, (None, 0))[1] < d.val:
                        need[key] = (d.sem, d.val)
                for key, (sem, val) in need.items():
                    if known.get(key, 0) < val:
                        engine.wait_ge(sem, val)
                        known[key] = val
                        nw += 1
                if o.fn is None:
                    continue
                ins = o.fn(engine)
                if o.is_dma:
                    ins.then_inc(o.sem, 16)
                elif o.signal:
                    ins.then_inc(o.sem, 1)
            stats[e] = (len(self.ops[e]), nw)

        @block.tensor
        def _(t):
            run("pe", t)

        @block.scalar
        def _(s):
            run("act", s)

        @block.vector
        def _(v):
            run("dve", v)

        @block.gpsimd
        def _(g):
            run("pool", g)

        @block.sync
        def _(s):
            run("sp", s)
        return stats


def build_program(debug=False, stages=3, cut=99):
    nc = bass.Bass("TRN2", target_bir_lowering=False)
    din = lambda name, shape: nc.dram_tensor(name, shape, F32, kind="ExternalInput").ap()
    xown = din("xown", [R, D])
    xext = din("xext", [E, D])
    w_in = din("w_in", [D, 12320])
    w_out = din("w_out", [D, D])
    w_up = din("w_up", [D, 2 * DFF])
    w_down = din("w_down", [DFF, D])
    wgin = din("wgin", [D, 80])
    wgb = din("wgb", [96, 2048])
    gvec_d = din("gvec", [128, 96])
    ghead_d = din("ghead", [128, 2])
    cmw_d = din("cmw", [128, 48])
    fcw_d = din("fcw", [128, NF * 4])
    consts_d = din("consts", [128, 7 * 128])
    flags_d = din("flags", [128, 2])
    y = nc.dram_tensor("y", [R, D], F32, kind="ExternalOutput").ap()
    skind = "ExternalOutput" if debug else "Internal"
    hT_scr = nc.dram_tensor("hT_scr", [128, KC, R + E], BF16, kind=skind).ap()
    xT_scr = nc.dram_tensor("xT_scr", [128, KC, R], F32, kind=skind).ap()
    mT_scr = nc.dram_tensor("mT_scr", [128, KC, R], BF16, kind=skind).ap()

    wo_scr = nc.dram_tensor("wo_scr", [16, 128, KC, 256], BF16).ap()
    wu_scr = nc.dram_tensor("wu_scr", [NF, 128, KC, 256], BF16).ap()
    wd_scr = nc.dram_tensor("wd_scr", [8, 8, 128, 11, 512], BF16).ap()
    w_inv = w_in.rearrange("(kc p) c -> p kc c", p=128)
    w_outv = w_out.rearrange("(kc p) c -> p kc c", p=128)
    w_upv = w_up.rearrange("(kc p) c -> p kc c", p=128)
    w_downv = w_down.rearrange("(f p) c -> p f c", p=128)
    wginv = wgin.rearrange("(kc p) c -> p kc c", p=128)

    with ExitStack() as es:
        P = Prog(nc, es)
        sb = lambda ctx, name, shape, dt: ctx.enter_context(nc.sbuf_tensor(name, shape, dt))
        psA = [es.enter_context(nc.psum_tensor("psA%d" % i, [128, 512], F32)) for i in range(2)]
        psG = [es.enter_context(nc.psum_tensor("psG%d" % i, [128, 512], F32)) for i in range(2)]
        psS = es.enter_context(nc.psum_tensor("psS", [128, 512], F32))
        psO = es.enter_context(nc.psum_tensor("psO", [128, 512], F32))
        psT = es.enter_context(nc.psum_tensor("psT", [128, 1024], BF16))
        psX = es.enter_context(nc.psum_tensor("psX", [128, 512], F32))

        def MM(out, lhsT, rhs, start=True, stop=True, r=(), w=()):
            return P.op("pe", lambda e: e.matmul(out, lhsT=lhsT, rhs=rhs, start=start, stop=stop), r, w)

        def TR(out, in_, ident, r=(), w=()):
            return P.op("pe", lambda e: e.transpose(out=out, in_=in_, identity=ident), r, w)

        def ACT(out, in_, func, r=(), w=(), **kw):
            return P.op("act", lambda e: e.activation(out=out, in_=in_, func=func, **kw), r, w)

        def TT(eng, out, in0, in1, op, r=(), w=()):
            return P.op(eng, lambda e: e.tensor_tensor(out=out, in0=in0, in1=in1, op=op), r, w)

        def TS(eng, out, in0, s1, s2, op0, op1=None, r=(), w=()):
            if op1 is None:
                return P.op(eng, lambda e: e.tensor_scalar(out=out, in0=in0, scalar1=s1, scalar2=None, op0=op0), r, w)
            return P.op(eng, lambda e: e.tensor_scalar(out=out, in0=in0, scalar1=s1, scalar2=s2, op0=op0, op1=op1), r, w)

        def STT(eng, out, in0, scalar, in1, op0, op1, r=(), w=()):
            return P.op(eng, lambda e: e.scalar_tensor_tensor(out=out, in0=in0, scalar=scalar, in1=in1, op0=op0, op1=op1), r, w)

        def CP(eng, out, in_, r=(), w=()):
            return P.op(eng, lambda e: e.tensor_copy(out=out, in_=in_), r, w)

        def MSET(eng, ap, val, r=(), w=()):
            return P.op(eng, lambda e: e.memset(ap, val), r, w)

        def DMA(eng, out, in_, sem, r=(), w=()):
            return P.dma(eng, lambda e: e.dma_start(out=out, in_=in_), sem, r, w)

        conv_jobs = []
        for cb2 in range(16):
            conv_jobs.append((wo_scr[cb2], w_outv[:, :, cb2 * 256:(cb2 + 1) * 256]))
        for f in range(NF):
            conv_jobs.append((wu_scr[f, :, :, 0:128], w_upv[:, :, f * 128:(f + 1) * 128]))
            conv_jobs.append((wu_scr[f, :, :, 128:256], w_upv[:, :, DFF + f * 128:DFF + (f + 1) * 128]))
        for gi, (f0, nf) in enumerate(FGROUPS):
            for cb4 in range(8):
                conv_jobs.append((wd_scr[gi, cb4, :, 0:nf, :], w_downv[:, f0:f0 + nf, cb4 * 512:(cb4 + 1) * 512]))
        conv_sems = [P.new_dma_sem("s_cv%d" % i) for i in range(4)]
        conv_state = [0, 0.0]
        n_hooks = 33 + 8 * 13 + 16 * 5
        per_hook = len(conv_jobs) / float(n_hooks) + 0.01

        def conv_hook(flush=False):
            conv_state[1] += per_hook
            while conv_state[0] < len(conv_jobs) and (flush or conv_state[0] < conv_state[1]):
                o_ap, i_ap = conv_jobs[conv_state[0]]
                DMA("pool", o_ap, i_ap, conv_sems[conv_state[0] % 4], w=["cv%d" % (conv_state[0] % 4)])
                conv_state[0] += 1

        consts = sb(es, "consts_sb", [128, 7, 128], F32)
        identb = sb(es, "identb", [128, 128], BF16)
        onesb = sb(es, "onesb", [128, 128], BF16)
        gvec = sb(es, "gvecs", [128, 96], F32)
        ghead = sb(es, "gheads", [128, 2], F32)
        cmw = sb(es, "cmws", [128, 16, 3], F32)
        fcw = sb(es, "fcws", [128, NF, 4], F32)
        flags = sb(es, "flagss", [128, 2], F32)
        sc = [P.new_dma_sem("sc%d" % i) for i in range(6)]
        DMA("sp", consts[:].rearrange("p a b -> p (a b)"), consts_d[:, :], sc[0], w=["consts"])
        DMA("sp", gvec[:], gvec_d[:, :], sc[1], w=["gvec"])
        DMA("sp", ghead[:], ghead_d[:, :], sc[2], w=["ghead"])
        DMA("sp", cmw[:].rearrange("p a b -> p (a b)"), cmw_d[:, :], sc[3], w=["cmw"])
        DMA("sp", fcw[:].rearrange("p a b -> p (a b)"), fcw_d[:, :], sc[4], w=["fcw"])
        DMA("sp", flags[:], flags_d[:, :], sc[5], w=["flags"])
        CP("dve", identb[:], consts[:, 0, :], r=["consts"], w=["identb"])
        MSET("dve", onesb[:], 1.0 / 4096.0, w=["onesb"])
        ident = consts[:, 0, :]
        Mle, Mge, Mgt, Mlt = consts[:, 1, :], consts[:, 2, :], consts[:, 3, :], consts[:, 4, :]
        masks = consts[:, 5:7, :]

        with ExitStack() as sa:
            xin = [sb(sa, "xin%d" % i, [128, D], F32) for i in range(2)]
            xs = [sb(sa, "xs%d" % i, [128, D], F32) for i in range(2)]
            xTs = [sb(sa, "xTs%d" % i, [128, KC, 128], F32) for i in range(2)]
            hTs = [sb(sa, "hTs%d" % i, [128, KC, 128], BF16) for i in range(2)]
            junkA = sb(sa, "junkA", [128, D], BF16)
            ssA = [sb(sa, "ssA%d" % i, [128, 1], F32) for i in range(2)]
            rsA = [sb(sa, "rsA%d" % i, [128, 1], F32) for i in range(2)]
            s_xin = [P.new_dma_sem("s_xin%d" % i) for i in range(2)]
            s_xst = [P.new_dma_sem("s_xst%d" % i) for i in range(2)]
            s_hst = [P.new_dma_sem("s_hst%d" % i) for i in range(2)]
            blocks = [(xown, i * 128, 128, True, i * 128) for i in range(16)] + [(xown, 2048, 16, True, 2048)]
            blocks += [(xext, i * 128, 128, False, R + i * 128) for i in range(16)]
            for bi, (src, r0, nr, own, c0) in enumerate(blocks):
                sl = bi % 2
                conv_hook()
                DMA("sp", xin[sl][:nr, :], src[r0:r0 + nr, :], s_xin[sl], w=["xin%d" % sl])
                MSET("dve", ssA[sl][:], 0.0, w=["ssA%d" % sl])
                ACT(junkA[:nr, :], xin[sl][:nr, :], AF.Square, r=["xin%d" % sl, "ssA%d" % sl], w=["junkA", "ssA%d" % sl],
                    accum_out=ssA[sl][:nr, 0:1])
                ACT(rsA[sl][:nr, :], ssA[sl][:nr, :], AF.Ln, r=["ssA%d" % sl], w=["rsA%d" % sl], scale=1.0 / D, bias=EPS)
                ACT(rsA[sl][:nr, :], rsA[sl][:nr, :], AF.Exp, r=["rsA%d" % sl], w=["rsA%d" % sl], scale=-0.5)
                TS("dve", xs[sl][:nr, :], xin[sl][:nr, :], rsA[sl][:nr, 0:1], None, ALU.mult, r=["xin%d" % sl, "rsA%d" % sl], w=["xs%d" % sl])
                for g in range(8):
                    if own:
                        bank = psA[g % 2]
                        for j in range(4):
                            kc = g * 4 + j
                            TR(bank[:, j * 128:j * 128 + nr], xin[sl][:nr, kc * 128:(kc + 1) * 128], ident[:nr, :nr],
                               r=["xin%d" % sl, "consts"], w=["psA%d" % (g % 2)])
                        ACT(xTs[sl][:, g * 4:(g + 1) * 4, :nr], bank[:].rearrange("p (a b) -> p a b", a=4)[:, :, :nr], AF.Copy,
                            r=["psA%d" % (g % 2)], w=["xTs%d_%d" % (sl, g)])
                    bank2 = psG[g % 2]
                    for j in range(4):
                        kc = g * 4 + j
                        TR(bank2[:, j * 128:j * 128 + nr], xs[sl][:nr, kc * 128:(kc + 1) * 128], ident[:nr, :nr],
                           r=["xs%d" % sl, "consts"], w=["psG%d" % (g % 2)])
                    TT("dve", hTs[sl][:, g * 4:(g + 1) * 4, :nr], bank2[:].rearrange("p (a b) -> p a b", a=4)[:, :, :nr],
                       gvec[:, g * 4:(g + 1) * 4].unsqueeze(2).to_broadcast([128, 4, nr]), ALU.mult,
                       r=["psG%d" % (g % 2), "gvec"], w=["hTs%d_%d" % (sl, g)])
                hk = ["hTs%d_%d" % (sl, g) for g in range(8)]
                DMA("sp", hT_scr[:, :, c0:c0 + nr], hTs[sl][:, :, :nr], s_hst[sl], r=hk, w=["hT_scr"])
                if own:
                    xk = ["xTs%d_%d" % (sl, g) for g in range(8)]
                    DMA("sp", xT_scr[:, :, r0:r0 + nr], xTs[sl][:, :, :nr], s_xst[sl], r=xk, w=["xT_scr"])
        P.fence()

        if stages >= 2:
          with ExitStack() as sB:
            wq = sb(sB, "wq", [128, KC, 128], BF16)
            wk = sb(sB, "wk", [128, KC, 128], BF16)
            wv = sb(sB, "wv", [128, KC, 256], BF16)
            wr = sb(sB, "wr", [128, KC, 256], BF16)
            hts = [sb(sB, "hts%d" % i, [128, KC, 413], BF16) for i in range(2)]
            s_w = {k: P.new_dma_sem("s_w" + k) for k in ("q", "k", "v", "r")}
            s_ht = [P.new_dma_sem("s_ht%d" % i) for i in range(2)]
            s_mst = P.new_dma_sem("s_mst")
            ht_ctr = [0]

            def load_ht(c0, n):
                sl = ht_ctr[0] % 2
                ht_ctr[0] += 1
                conv_hook()
                DMA("sp", hts[sl][:, :, :n], hT_scr[:, :, c0:c0 + n], s_ht[sl], r=["hT_scr"], w=["hts%d" % sl])
                return hts[sl], "hts%d" % sl

            acc_ctr = [0]

            def proj(wt, wkey, wcols, ht, htkey, n, evac, M=128):
                i = acc_ctr[0] % 2
                acc_ctr[0] += 1
                acc = psA[i]
                for kc in range(KC):
                    MM(acc[:M, :n], wt[:, kc, wcols], ht[:, kc, :n], start=(kc == 0), stop=(kc == KC - 1),
                       r=[wkey, htkey], w=["psA%d" % i])
                evac(acc, "psA%d" % i)

            with ExitStack() as sg:
                wg_sb = sb(sg, "wg_sb", [128, KC, 80], BF16)
                qT = sb(sg, "qT", [128, R], BF16)
                kT = sb(sg, "kT", [128, R], BF16)
                rs = sb(sg, "rs", [128, 2, R], BF16)
                vt = sb(sg, "vt", [128, 2, 413], BF16)
                kte = sb(sg, "kte", [128, ET], BF16)
                kv = sb(sg, "kv", [128, 20, 384], BF16)
                kve = sb(sg, "kve", [128, 2, 384], BF16)
                Sbb = sb(sg, "Sbb", [128, 20, 256], BF16)
                aT = sb(sg, "aT", [96, R], F32)
                wgb_sb = sb(sg, "wgb_sb", [96, 2048], F32)
                mst = sb(sg, "mst", [128, 2, R], BF16)
                ex = sb(sg, "ex", [128, 256], F32)
                lsb = sb(sg, "lsb", [128, 256], F32)
                E1 = sb(sg, "E1", [128, 256], F32)
                E2 = sb(sg, "E2", [128, 256], F32)
                E3 = sb(sg, "E3", [128, 256], F32)
                qq = sb(sg, "qq", [128, 2, 128], BF16)
                kk = sb(sg, "kk", [128, 2, 128], BF16)
                khf = sb(sg, "khf", [128, 128], BF16)
                khb = sb(sg, "khb", [128, 128], BF16)
                PT = sb(sg, "PT", [128, 2, 128], BF16)
                on = sb(sg, "on", [128, 256], BF16)
                junk = sb(sg, "junk", [128, 256], BF16)
                Sf = sb(sg, "Sf", [128, 256], F32)
                Sb_ = sb(sg, "Sb_", [128, 256], F32)
                Se = sb(sg, "Se", [128, 256], F32)
                Sfb = sb(sg, "Sfb", [128, 256], BF16)
                ss1 = sb(sg, "ss1", [128, 1], F32)
                rstd1 = sb(sg, "rstd1", [128, 1], F32)
                s_wg = P.new_dma_sem("s_wg")
                s_wgb = P.new_dma_sem("s_wgb")

                DMA("pool", wg_sb[:], wginv[:, :, :], s_wg, w=["wg_sb"])
                DMA("sp", wgb_sb[:], wgb[:, :], s_wgb, w=["wgb_sb"])
                MSET("dve", aT[:, :], 1.0, w=["aT"])

                for ti in range(5):
                    s0, e0 = TB[ti], TB[ti + 1]
                    n = e0 - s0
                    ht, hk = load_ht(s0, n)

                    def ev(acc, ak, s0=s0, n=n):
                        ACT(aT[0:16, s0:s0 + n], acc[0:16, :n], AF.Copy, r=[ak], w=["aT"])
                        ACT(aT[32:48, s0:s0 + n], acc[32:48, :n], AF.Copy, r=[ak], w=["aT"])
                    proj(wg_sb, "wg_sb", slice(0, 80), ht, hk, n, ev, M=80)
                for et in range(E // ET):
                    ht, hk = load_ht(R + et * ET, ET)

                    def ev(acc, ak, et=et):
                        ACT(aT[64:80, et * ET:(et + 1) * ET], acc[64:80, :ET], AF.Copy, r=[ak], w=["aT"])
                    proj(wg_sb, "wg_sb", slice(0, 80), ht, hk, ET, ev, M=80)

                def gates_decays(cols0, C, j, frow, do_b):
                    if do_b:
                        MM(psG[0][:C, 0:256], aT[0:64, cols0:cols0 + C], wgb_sb[0:64, j * 256:(j + 1) * 256],
                           r=["aT", "wgb_sb"], w=["psG0"])
                        W = 256
                    else:
                        MM(psG[0][:C, 0:128], aT[frow:frow + 32, cols0:cols0 + C], wgb_sb[frow:frow + 32, j * 256:j * 256 + 128],
                           r=["aT", "wgb_sb"], w=["psG0"])
                        W = 128
                    ACT(ex[:C, :W], psG[0][:C, :W], AF.Exp, r=["psG0"], w=["ex"], scale=-1.0)
                    ACT(lsb[:C, :W], ex[:C, :W], AF.Ln, r=["ex"], w=["lsb"], bias=1.0)
                    MM(psG[1][:, 0:C], lsb[:C, 0:128], Mle[:C, :C], r=["lsb", "consts"], w=["psG1"])
                    MM(psG[1][:C, 256:384], Mgt[:C, :C], lsb[:C, 0:128], r=["lsb", "consts"], w=["psG1"])
                    if do_b:
                        MM(psG[1][:, 128:128 + C], lsb[:C, 128:256], Mge[:C, :C], r=["lsb", "consts"], w=["psG1"])
                        MM(psG[1][:C, 384:512], Mlt[:C, :C], lsb[:C, 128:256], r=["lsb", "consts"], w=["psG1"])

                for j in range(8 if cut >= 6 else (1 if cut >= 2 else 0)):
                    DMA("pool", wq[:], w_inv[:, :, OFF_Q + j * 128:OFF_Q + (j + 1) * 128], s_w["q"], w=["wq"])
                    DMA("pool", wk[:], w_inv[:, :, OFF_K + j * 128:OFF_K + (j + 1) * 128], s_w["k"], w=["wk"])
                    DMA("pool", wv[:], w_inv[:, :, OFF_V + j * 256:OFF_V + (j + 1) * 256], s_w["v"], w=["wv"])
                    DMA("pool", wr[:], w_inv[:, :, OFF_R + j * 256:OFF_R + (j + 1) * 256], s_w["r"], w=["wr"])
                    MSET("dve", Se[:], 0.0, w=["Se"])
                    for et in range(E // ET):
                        ht, hk = load_ht(R + et * ET, ET)
                        proj(wk, "wk", slice(0, 128), ht, hk, ET,
                             lambda acc, ak: ACT(kte[:, :ET], acc[:, :ET], AF.Copy, r=[ak], w=["kte"]))
                        for h in range(2):
                            proj(wv, "wv", slice(h * 128, (h + 1) * 128), ht, hk, ET,
                                 lambda acc, ak, h=h: ACT(vt[:, h, :ET], acc[:, :ET], AF.Copy, r=[ak], w=["vt%d" % h]))
                        for ci in range(ET // 128):
                            lc = ci * 128
                            TR(psT[:, 0:128], kte[:, lc:lc + 128], identb[:, :], r=["kte", "identb"], w=["psT"])
                            for h in range(2):
                                TR(psT[:, 128 + h * 128:256 + h * 128], vt[:, h, lc:lc + 128], identb[:, :],
                                   r=["vt%d" % h, "identb"], w=["psT"])
                            CP("dve", kve[:, ci, :], psT[:, 0:384], r=["psT"], w=["kve%d" % ci])
                            ec = et * ET + lc
                            gates_decays(ec, 128, j, 64, False)
                            ACT(E1[:, 0:128], psG[1][:, 0:128], AF.Exp, r=["psG1"], w=["E1"])
                            ACT(E3[:, 0:128], psG[1][:, 256:384], AF.Exp, r=["psG1"], w=["E3"])
                            TT("dve", khf[:, :], kve[:, ci, 0:128], E3[:, 0:128], ALU.mult, r=["kve%d" % ci, "E3"], w=["khf"])
                            MM(psX[:, 0:256], khf[:, :], kve[:, ci, 128:384], r=["khf", "kve%d" % ci], w=["psX"])
                            STT("dve", Se[:], Se[:], E1[:, 127:128], psX[:, 0:256], ALU.mult, ALU.add,
                                r=["Se", "E1", "psX"], w=["Se"])
                    TS("dve", Sf[:], Se[:], flags[:, 0:1], None, ALU.mult, r=["Se", "flags"], w=["Sf"])
                    TS("dve", Sb_[:], Se[:], flags[:, 1:2], None, ALU.mult, r=["Se", "flags"], w=["Sb"])
                    for ti in range(5 if cut >= 3 else 0):
                        s0, e0 = TB[ti], TB[ti + 1]
                        n = e0 - s0
                        ht, hk = load_ht(s0, n)
                        proj(wq, "wq", slice(0, 128), ht, hk, n,
                             lambda acc, ak, s0=s0, n=n: ACT(qT[:, s0:s0 + n], acc[:, :n], AF.Copy, r=[ak], w=["qT"], scale=128.0 ** -0.5))
                        proj(wk, "wk", slice(0, 128), ht, hk, n,
                             lambda acc, ak, s0=s0, n=n: ACT(kT[:, s0:s0 + n], acc[:, :n], AF.Copy, r=[ak], w=["kT"]))
                        for h in range(2):
                            proj(wv, "wv", slice(h * 128, (h + 1) * 128), ht, hk, n,
                                 lambda acc, ak, h=h, n=n: ACT(vt[:, h, :n], acc[:, :n], AF.Copy, r=[ak], w=["vt%d" % h]))
                        for h in range(2):
                            proj(wr, "wr", slice(h * 128, (h + 1) * 128), ht, hk, n,
                                 lambda acc, ak, h=h, s0=s0, n=n: ACT(rs[:, h, s0:s0 + n], acc[:, :n], AF.Silu, r=[ak], w=["rs"]))
                        for c in range(ti * 4, ti * 4 + 4):
                            cs, C = CHUNKS[c]
                            lc = cs - s0
                            TR(psT[:C, 0:128], kT[:, cs:cs + C], identb[:, :], r=["kT", "identb"], w=["psT"])
                            for h in range(2):
                                TR(psT[:C, 128 + h * 128:256 + h * 128], vt[:, h, lc:lc + C], identb[:, :],
                                   r=["vt%d" % h, "identb"], w=["psT"])
                            CP("dve", kv[:C, c, :], psT[:C, 0:384], r=["psT"], w=["kv"])
                    for c in reversed(range(20 if cut >= 4 else 0)):
                        cs, C = CHUNKS[c]
                        gates_decays(cs, C, j, 0, True)
                        ACT(E1[:, 0:256], psG[1][:, 0:256], AF.Exp, r=["psG1"], w=["E1"])
                        ACT(E3[:C, 0:256], psG[1][:C, 256:512], AF.Exp, r=["psG1"], w=["E3"])
                        ACT(Sbb[:, c, :], Sb_[:], AF.Copy, r=["Sb"], w=["Sbb"])
                        TT("dve", khb[:C, :], kv[:C, c, 0:128], E3[:C, 128:256], ALU.mult, r=["kv", "E3"], w=["khb"])
                        MM(psX[:, 0:256], khb[:C, :], kv[:C, c, 128:384], r=["khb", "kv"], w=["psX"])
                        STT("dve", Sb_[:], Sb_[:], E1[:, 128:129], psX[:, 0:256], ALU.mult, ALU.add,
                            r=["Sb", "E1", "psX"], w=["Sb"])
                    for c in range(20 if cut >= 5 else 0):
                        cs, C = CHUNKS[c]
                        gates_decays(cs, C, j, 0, True)
                        ACT(E1[:, 0:256], psG[1][:, 0:256], AF.Exp, r=["psG1"], w=["E1"])
                        ACT(E2[:, 0:256], psG[1][:, 0:256], AF.Exp, r=["psG1"], w=["E2"], scale=-1.0)
                        ACT(E3[:C, 0:256], psG[1][:C, 256:512], AF.Exp, r=["psG1"], w=["E3"])
                        TT("dve", qq[:, :, :C], qT[:, cs:cs + C].unsqueeze(1).to_broadcast([128, 2, C]),
                           E1[:].rearrange("p (a b) -> p a b", a=2)[:, :, :C], ALU.mult, r=["qT", "E1"], w=["qq"])
                        TT("dve", kk[:, :, :C], kT[:, cs:cs + C].unsqueeze(1).to_broadcast([128, 2, C]),
                           E2[:].rearrange("p (a b) -> p a b", a=2)[:, :, :C], ALU.mult, r=["kT", "E2"], w=["kk"])
                        TT("dve", khf[:C, :], kv[:C, c, 0:128], E3[:C, 0:128], ALU.mult, r=["kv", "E3"], w=["khf"])
                        MM(psS[:C, 0:C], kk[:, 0, :C], qq[:, 0, :C], r=["kk", "qq"], w=["psS"])
                        MM(psS[:C, 128:128 + C], kk[:, 1, :C], qq[:, 1, :C], r=["kk", "qq"], w=["psS"])
                        TT("dve", PT[:C, :, :C], psS[:C, 0:256].rearrange("p (a b) -> p a b", a=2)[:, :, :C], masks[:C, :, :C],
                           ALU.mult, r=["psS", "consts"], w=["PT"])
                        ACT(Sfb[:], Sf[:], AF.Copy, r=["Sf"], w=["Sfb"])
                        MM(psO[:C, 0:256], PT[:C, 0, :C], kv[:C, c, 128:384], start=True, stop=False, r=["PT", "kv"], w=["psOo"])
                        MM(psO[:C, 0:256], PT[:C, 1, :C], kv[:C, c, 128:384], start=False, stop=False, r=["PT", "kv"], w=["psOo"])
                        MM(psO[:C, 0:256], qq[:, 0, :C], Sfb[:, :], start=False, stop=False, r=["qq", "Sfb"], w=["psOo"])
                        MM(psO[:C, 0:256], qq[:, 1, :C], Sbb[:, c, :], start=False, stop=True, r=["qq", "Sbb"], w=["psOo"])
                        MM(psX[:, 0:256], khf[:C, :], kv[:C, c, 128:384], r=["khf", "kv"], w=["psX"])
                        STT("dve", Sf[:], Sf[:], E1[:, C - 1:C], psX[:, 0:256], ALU.mult, ALU.add,
                            r=["Sf", "E1", "psX"], w=["Sf"])
                        MSET("dve", ss1[:], 0.0, w=["ss1"])
                        ACT(junk[:C, :], psO[:C, 0:256], AF.Square, r=["psOo", "ss1"], w=["junk", "ss1"], accum_out=ss1[:C, 0:1])
                        ACT(rstd1[:C, :], ss1[:C, :], AF.Ln, r=["ss1"], w=["rstd1"], scale=1.0 / 256, bias=EPS)
                        ACT(rstd1[:C, :], rstd1[:C, :], AF.Exp, r=["rstd1"], w=["rstd1"], scale=-0.5)
                        TS("dve", on[:C, :], psO[:C, 0:256], rstd1[:C, 0:1], None, ALU.mult, r=["psOo", "rstd1"], w=["on"])
                        for h in range(2):
                            TR(psT[:, 512 + h * 128:512 + h * 128 + C], on[:C, h * 128:(h + 1) * 128], identb[:C, :C],
                               r=["on", "identb"], w=["psT"])
                        for h in range(2):
                            STT("dve", mst[:, h, cs:cs + C], psT[:, 512 + h * 128:512 + h * 128 + C], ghead[:, h:h + 1],
                                rs[:, h, cs:cs + C], ALU.mult, ALU.mult, r=["psT", "ghead", "rs"], w=["mst"])
                    DMA("sp", mT_scr[:, 2 * j:2 * j + 2, :], mst[:, :, :], s_mst, r=["mst"], w=["mT_scr"])
            P.fence()
            with ExitStack() as scv:
                ccs = sb(scv, "ccs", [128, 413], F32)
                prod = sb(scv, "prod", [128, R + 2], F32)
                cbs = sb(scv, "cbs", [128, R], F32)
                t1 = sb(scv, "t1", [128, R], F32)
                mcv = sb(scv, "mcv", [128, R], BF16)
                s_mcv = P.new_dma_sem("s_mcv")
                MSET("dve", prod[:, 0:1], 0.0, w=["prod"])
                MSET("dve", prod[:, R + 1:R + 2], 0.0, w=["prod"])
                for cg in range(16 if cut >= 7 else 0):
                    DMA("pool", wq[:], w_inv[:, :, OFF_CB + cg * 128:OFF_CB + (cg + 1) * 128], s_w["q"], w=["wq"])
                    DMA("pool", wk[:], w_inv[:, :, OFF_CC + cg * 128:OFF_CC + (cg + 1) * 128], s_w["k"], w=["wk"])
                    DMA("pool", wv[:, :, 0:128], w_inv[:, :, OFF_CH + cg * 128:OFF_CH + (cg + 1) * 128], s_w["v"], w=["wv"])
                    for ti in range(5):
                        s0, e0 = TB[ti], TB[ti + 1]
                        n = e0 - s0
                        ht, hk = load_ht(s0, n)
                        proj(wq, "wq", slice(0, 128), ht, hk, n,
                             lambda acc, ak, s0=s0, n=n: ACT(cbs[:, s0:s0 + n], acc[:, :n], AF.Copy, r=[ak], w=["cbs"]))
                        proj(wk, "wk", slice(0, 128), ht, hk, n,
                             lambda acc, ak, n=n: ACT(ccs[:, :n], acc[:, :n], AF.Copy, r=[ak], w=["ccs"]))
                        proj(wv, "wv", slice(0, 128), ht, hk, n,
                             lambda acc, ak, s0=s0, n=n: TT("dve", prod[:, 1 + s0:1 + s0 + n], acc[:, :n], ccs[:, :n], ALU.mult,
                                                            r=[ak, "ccs"], w=["prod"]))
                    TS("dve", t1[:, :], prod[:, 0:R], cmw[:, cg, 0:1], None, ALU.mult, r=["prod", "cmw"], w=["t1"])
                    STT("dve", t1[:, :], prod[:, 1:R + 1], cmw[:, cg, 1:2], t1[:, :], ALU.mult, ALU.add, r=["prod", "cmw", "t1"], w=["t1"])
                    STT("dve", t1[:, :], prod[:, 2:R + 2], cmw[:, cg, 2:3], t1[:, :], ALU.mult, ALU.add, r=["prod", "cmw", "t1"], w=["t1"])
                    TT("dve", mcv[:, :], t1[:, :], cbs[:, :], ALU.mult, r=["t1", "cbs"], w=["mcv"])
                    DMA("sp", mT_scr[:, 16 + cg, :], mcv[:, :], s_mcv, r=["mcv"], w=["mT_scr"])
          conv_hook(flush=True)
          P.fence()

        if stages >= 3:
          with ExitStack() as sC:
            X1 = sb(sC, "X1", [128, KC, 415], F32)
            MH = sb(sC, "MH", [128, KC, 415], BF16)
            aTt = sb(sC, "aTt", [128, 11, 413], BF16)
            wring = [sb(sC, "wring%d" % i, [128, KC, 256], BF16) for i in range(3)]
            wdring = [sb(sC, "wdring%d" % i, [128, 11, 512], BF16) for i in range(2)]
            sqt = [sb(sC, "sqt%d" % i, [128, 415], BF16) for i in range(2)]
            rstdt = sb(sC, "rstdt", [128, 415], F32)
            c1 = [sb(sC, "c1_%d" % i, [128, 413], F32) for i in range(2)]
            sg_ = [sb(sC, "sg%d" % i, [128, 413], F32) for i in range(2)]
            ost = [sb(sC, "ost%d" % i, [128, 2048], F32) for i in range(2)]
            s_wr = [[P.new_dma_sem("s_wr%d_%d" % (i, h)) for h in range(2)] for i in range(3)]
            s_wd = [P.new_dma_sem("s_wd%d" % i) for i in range(2)]
            s_x1 = P.new_dma_sem("s_x1")
            s_mh = P.new_dma_sem("s_mh")
            s_ost = [P.new_dma_sem("s_ost%d" % i) for i in range(2)]
            wr_ctr = [0]
            wd_ctr = [0]
            ost_ctr = [0]
            out_dmas = []
            for ti in range(5):
                s0, e0 = TB[ti], TB[ti + 1]
                lo, hi = max(s0 - 1, 0), min(e0 + 1, R)
                N = e0 - s0 + 2
                NV = N - 2
                off = lo - (s0 - 1)
                if ti == 0:
                    MSET("dve", X1[:, :, 0:1], 0.0, w=["X1"])
                    MSET("dve", MH[:, :, 0:1], 0.0, w=["MH"])
                if ti == 4:
                    MSET("dve", X1[:, :, N - 1:N], 0.0, w=["X1"])
                    MSET("dve", MH[:, :, N - 1:N], 0.0, w=["MH"])
                DMA("sp", MH[:, :, off:off + hi - lo], mT_scr[:, :, lo:hi], s_mh, r=["mT_scr"], w=["MH"])
                DMA("sp", X1[:, :, off:off + hi - lo], xT_scr[:, :, lo:hi], s_x1, r=["xT_scr"], w=["X1"])
                for cb2 in range(16):
                    sl = wr_ctr[0] % 3
                    wr_ctr[0] += 1
                    DMA("pool", wring[sl][:, :, :], wo_scr[cb2], s_wr[sl][0], w=["wring%d" % sl])
                    for h in range(2):
                        cb = cb2 * 2 + h
                        acc = psA[cb % 2]
                        for kc in range(KC):
                            MM(acc[:, :N], wring[sl][:, kc, h * 128:(h + 1) * 128], MH[:, kc, :N], start=(kc == 0), stop=(kc == KC - 1),
                               r=["wring%d" % sl, "MH"], w=["psA%d" % (cb % 2)])
                        TT("dve", X1[:, cb, :N], acc[:, :N], X1[:, cb, :N], ALU.add, r=["psA%d" % (cb % 2), "X1"], w=["X1c%d" % cb])
                        ACT(sqt[cb % 2][:, :N], X1[:, cb, :N], AF.Square, r=["X1c%d" % cb], w=["sqt%d" % (cb % 2)])
                        MM(psX[:, :N], onesb[:, :], sqt[cb % 2][:, :N], start=(cb == 0), stop=(cb == 31), r=["sqt%d" % (cb % 2), "onesb"], w=["psX"])
                xck = ["X1c%d" % cb for cb in range(32)]
                ACT(rstdt[:, :N], psX[:, :N], AF.Ln, r=["psX"], w=["rstdt"], bias=EPS)
                ACT(rstdt[:, :N], rstdt[:, :N], AF.Exp, r=["rstdt"], w=["rstdt"], scale=-0.5)
                for kc in range(KC):
                    STT("dve", MH[:, kc, :N], X1[:, kc, :N], gvec[:, 32 + kc:33 + kc], rstdt[:, :N], ALU.mult, ALU.mult,
                        r=["X1c%d" % kc, "gvec", "rstdt"], w=["MH"])
                for gi, (f0, nf) in enumerate(FGROUPS):
                    for fl in range(nf):
                        f = f0 + fl
                        sl = wr_ctr[0] % 3
                        wr_ctr[0] += 1
                        DMA("pool", wring[sl][:, :, :], wu_scr[f], s_wr[sl][0], w=["wring%d" % sl])
                        pu, pg = psA[f % 2], psG[f % 2]
                        for kc in range(KC):
                            MM(pu[:, :N], wring[sl][:, kc, 0:128], MH[:, kc, :N], start=(kc == 0), stop=(kc == KC - 1),
                               r=["wring%d" % sl, "MH"], w=["psA%d" % (f % 2)])
                        for kc in range(KC):
                            MM(pg[:, :N], wring[sl][:, kc, 128:256], MH[:, kc, :N], start=(kc == 0), stop=(kc == KC - 1),
                               r=["wring%d" % sl, "MH"], w=["psG%d" % (f % 2)])
                        cc1 = c1[f % 2]
                        TS("dve", cc1[:, :NV], pg[:, 0:NV], fcw[:, f, 0:1], fcw[:, f, 3:4], ALU.mult, ALU.add,
                           r=["psG%d" % (f % 2), "fcw"], w=["c1_%d" % (f % 2)])
                        STT("dve", cc1[:, :NV], pg[:, 1:NV + 1], fcw[:, f, 1:2], cc1[:, :NV], ALU.mult, ALU.add,
                            r=["psG%d" % (f % 2), "fcw", "c1_%d" % (f % 2)], w=["c1_%d" % (f % 2)])
                        STT("dve", cc1[:, :NV], pg[:, 2:NV + 2], fcw[:, f, 2:3], cc1[:, :NV], ALU.mult, ALU.add,
                            r=["psG%d" % (f % 2), "fcw", "c1_%d" % (f % 2)], w=["c1_%d" % (f % 2)])
                        ACT(sg_[f % 2][:, :NV], cc1[:, :NV], AF.Silu, r=["c1_%d" % (f % 2)], w=["sg%d" % (f % 2)])
                        TT("dve", aTt[:, fl, :NV], sg_[f % 2][:, :NV], pu[:, 1:NV + 1], ALU.mult,
                           r=["sg%d" % (f % 2), "psA%d" % (f % 2)], w=["aTt"])
                    for cb4 in range(8):
                        dl = wd_ctr[0] % 2
                        wd_ctr[0] += 1
                        DMA("pool", wdring[dl][:, :nf, :], wd_scr[gi, cb4, :, 0:nf, :], s_wd[dl], w=["wdring%d" % dl])
                        for h in range(4):
                            cb = cb4 * 4 + h
                            yps = psS if cb % 2 == 0 else psO
                            yk = "psS" if cb % 2 == 0 else "psO"
                            for fl in range(nf):
                                MM(yps[:, :NV], wdring[dl][:, fl, h * 128:(h + 1) * 128], aTt[:, fl, :NV], start=(fl == 0), stop=(fl == nf - 1),
                                   r=["wdring%d" % dl, "aTt"], w=[yk])
                            TT("dve", X1[:, cb, 1:NV + 1], yps[:, :NV], X1[:, cb, 1:NV + 1], ALU.add, r=[yk, "X1c%d" % cb], w=["X1c%d" % cb])
                for cb in range(32):
                    ACT(sqt[cb % 2][:, :NV], X1[:, cb, 1:NV + 1], AF.Square, r=["X1c%d" % cb], w=["sqt%d" % (cb % 2)])
                    MM(psX[:, :NV], onesb[:, :], sqt[cb % 2][:, :NV], start=(cb == 0), stop=(cb == 31), r=["sqt%d" % (cb % 2), "onesb"], w=["psX"])
                ACT(rstdt[:, :NV], psX[:, :NV], AF.Ln, r=["psX"], w=["rstdt"], bias=EPS)
                ACT(rstdt[:, :NV], rstdt[:, :NV], AF.Exp, r=["rstdt"], w=["rstdt"], scale=-0.5)
                for kc in range(KC):
                    STT("dve", X1[:, kc, 1:NV + 1], X1[:, kc, 1:NV + 1], gvec[:, 64 + kc:65 + kc], rstdt[:, :NV], ALU.mult, ALU.mult,
                        r=["X1c%d" % kc, "gvec", "rstdt"], w=["X1c%d" % kc])
                for c in range(ti * 4, ti * 4 + 4):
                    cs, C = CHUNKS[c]
                    lc = cs - s0 + 1
                    for half in range(2):
                        ol = ost_ctr[0] % 2
                        ost_ctr[0] += 1
                        for g4 in range(4):
                            bank = psG[g4 % 2]
                            for jj in range(4):
                                kc = half * 16 + g4 * 4 + jj
                                TR(bank[:C, jj * 128:(jj + 1) * 128], X1[:, kc, lc:lc + C], ident[:, :],
                                   r=["X1c%d" % kc, "consts"], w=["psG%d" % (g4 % 2)])
                            ACT(ost[ol][:C, g4 * 512:(g4 + 1) * 512], bank[:C, :], AF.Copy, r=["psG%d" % (g4 % 2)], w=["ost%d" % ol])
                        out_dmas.append(DMA("sp", y[cs:cs + C, half * 2048:(half + 1) * 2048], ost[ol][:C, :], s_ost[ol], r=["ost%d" % ol]))
                P.op("dve", lambda e: e.memset(rstdt[:, 0:1], 0.0), reads=xck + ["MH"], writes=["X1", "MH", "rstdt"] + xck)
          P.fence()
        else:
            P.fence()
        P.barrier_wait("sp", list(P.ops["sp"][-1].deps))
        block = es.enter_context(nc.Block())
        stats = P.emit(block)
    return nc, stats


def _consts():
    a = np.arange(128)[:, None]
    b = np.arange(128)[None, :]
    c = np.zeros((128, 7, 128), np.float32)
    c[:, 0] = (a == b)
    c[:, 1] = (a <= b) * (-1.0 / 16.0)
    c[:, 2] = (a >= b) * (-1.0 / 16.0)
    c[:, 3] = (a > b) * (-1.0 / 16.0)
    c[:, 4] = (a < b) * (-1.0 / 16.0)
    c[:, 5] = (a <= b)
    c[:, 6] = (a > b)
    return c.reshape(128, 7 * 128)


def _prepare_inputs(x_prompt, x_sample, meta_tokens, mix_norm_g, w_in, w_gate2, b_gate2, head_norm_g,
                    conv_mix_w, w_out, ffn_norm_g, w_up, ffn_conv_w, ffn_conv_b, w_down, final_norm_g):
    f = lambda a: np.ascontiguousarray(np.asarray(a, dtype=np.float32))
    x_prompt, x_sample, meta = f(x_prompt), f(x_sample), f(meta_tokens)
    w_in0, w_out0, w_up0, w_down0 = f(w_in[0]), f(w_out[0]), f(w_up[0]), f(w_down[0])
    wg2, bg2 = f(w_gate2[0]), f(b_gate2[0])
    gvec = np.concatenate([f(mix_norm_g[0]).reshape(32, 128).T, f(ffn_norm_g[0]).reshape(32, 128).T,
                           f(final_norm_g).reshape(32, 128).T], axis=1)
    ghead = f(head_norm_g[0]).reshape(2, 128).T
    cmw = f(conv_mix_w[0]).reshape(3, 16, 128).transpose(2, 1, 0).reshape(128, 48)
    fcw = np.concatenate([f(ffn_conv_w[0]), f(ffn_conv_b[0])[None]], axis=0).reshape(4, NF, 128).transpose(2, 1, 0).reshape(128, NF * 4)
    consts = _consts()
    shared = dict(w_in=w_in0, w_out=w_out0, w_up=w_up0, w_down=w_down0,
                  gvec=f(gvec), ghead=f(ghead), cmw=f(cmw), fcw=f(fcw), consts=consts)

    def core(xown, xext, ext_dir, flag_f, flag_b):
        wgin = np.zeros((D, 80), np.float32)
        wgin[:, 0:16] = w_in0[:, 6144:6160]
        wgin[:, 32:48] = w_in0[:, 6160:6176]
        wgin[:, 64:80] = w_in0[:, 6144 + 16 * ext_dir:6160 + 16 * ext_dir]
        wgb = np.zeros((96, 8, 256), np.float32)
        wgb[0:16, :, 0:128] = wg2[0].reshape(16, 8, 128)
        wgb[16, :, 0:128] = bg2[0].reshape(8, 128)
        wgb[32:48, :, 128:256] = wg2[1].reshape(16, 8, 128)
        wgb[48, :, 128:256] = bg2[1].reshape(8, 128)
        wgb[64:80, :, 0:128] = wg2[ext_dir].reshape(16, 8, 128)
        wgb[80, :, 0:128] = bg2[ext_dir].reshape(8, 128)
        wgb = wgb.reshape(96, 2048)
        flags = np.zeros((128, 2), np.float32)
        flags[:, 0] = flag_f
        flags[:, 1] = flag_b
        m = dict(shared)
        m.update(xown=f(xown), xext=f(xext), wgin=wgin, wgb=wgb, flags=flags)
        return m

    maps = []
    for b in range(2):
        xa = np.concatenate([meta, x_prompt[b, 0:2048]], axis=0)
        ea = x_prompt[b, 2048:4096][::-1]
        maps.append(core(xa, ea, 1, 0.0, 1.0))
        xb = x_prompt[b, 2032:4096]
        eb = np.concatenate([meta, x_prompt[b, 0:2032]], axis=0)
        maps.append(core(xb, eb, 0, 1.0, 0.0))
    for s in range(4):
        xs = np.concatenate([meta, x_sample[s]], axis=0)
        maps.append(core(xs, np.zeros((E, D), np.float32), 0, 0.0, 0.0))
    return maps


_NC_CACHE = {}


def kernel(x_prompt, x_sample, meta_tokens, mix_norm_g, w_in, w_gate2, b_gate2, head_norm_g,
           conv_mix_w, w_out, ffn_norm_g, w_up, ffn_conv_w, ffn_conv_b, w_down, final_norm_g):
    maps = _prepare_inputs(x_prompt, x_sample, meta_tokens, mix_norm_g, w_in, w_gate2, b_gate2, head_norm_g,
                           conv_mix_w, w_out, ffn_norm_g, w_up, ffn_conv_w, ffn_conv_b, w_down, final_norm_g)
    if "nc" not in _NC_CACHE:
        _NC_CACHE["nc"] = build_program()[0]
    nc = _NC_CACHE["nc"]
    res = run_bass_kernel_spmd(nc, maps, core_ids=list(range(8)))
    ys = [np.asarray(r["y"], dtype=np.float32) for r in res.results]
    y_prompt = np.zeros((2, 4096, D), np.float32)
    y_sample = np.zeros((4, 2048, D), np.float32)
    for b in range(2):
        y_prompt[b, 0:2040] = ys[2 * b][16:16 + 2040]
        y_prompt[b, 2040:4096] = ys[2 * b + 1][8:2064]
    for s in range(4):
        y_sample[s] = ys[4 + s][16:2064]
    return (y_prompt, y_sample)
```

```python
import os
import numpy as np
from contextlib import ExitStack
import concourse.bass as bass
import concourse.mybir as mybir
from concourse.bass_utils import run_bass_kernel_spmd

F32 = mybir.dt.float32
BF16 = mybir.dt.bfloat16
AF = mybir.ActivationFunctionType
ALU = mybir.AluOpType

D = 4096
KC = 32
R = 2064
E = 2048
DFF = 11008
NF = 86
EPS = 1e-6
TB = [0, 413, 826, 1239, 1652, 2064]
ET = 256
CHUNKS = []
for _i in range(5):
    _s, _e = TB[_i], TB[_i + 1]
    _w = _e - _s
    _c0 = _w - 309
    CHUNKS.append((_s, _c0))
    for _k in range(3):
        CHUNKS.append((_s + _c0 + 103 * _k, 103))
FGROUPS = []
_f = 0
for _n in [11, 11, 11, 11, 11, 11, 10, 10]:
    FGROUPS.append((_f, _n))
    _f += _n
OFF_Q, OFF_K, OFF_V, OFF_R, OFF_CB, OFF_CC, OFF_CH = 0, 1024, 2048, 4096, 6176, 8224, 10272


class _Op:
    __slots__ = ("eng", "fn", "deps", "is_dma", "sem", "val", "signal")

    def __init__(self, eng, fn, is_dma=False):
        self.eng = eng
        self.fn = fn
        self.deps = []
        self.is_dma = is_dma
        self.sem = None
        self.val = 0
        self.signal = False


class Prog:
    ENGS = ("pe", "act", "dve", "pool", "sp")

    def __init__(self, nc, es):
        self.nc = nc
        self.es = es
        self.ops = {e: [] for e in self.ENGS}
        self.last_w = {}
        self.readers = {}
        self.eng_sem = {e: es.enter_context(nc.semaphore("prog_" + e)) for e in self.ENGS}
        self.dma_sem_count = {}
        self.pending_dmas = []
        self.background_dmas = []

    def new_dma_sem(self, name):
        s = self.es.enter_context(self.nc.semaphore(name))
        self.dma_sem_count[id(s)] = 0
        return s

    def _collect(self, op, reads, writes):
        deps = []
        for k in reads:
            w = self.last_w.get(k)
            if w is not None:
                deps.append(w)
        for k in writes:
            w = self.last_w.get(k)
            if w is not None:
                deps.append(w)
            deps.extend(self.readers.get(k, ()))
        seen = set()
        for d in deps:
            if d is op or id(d) in seen:
                continue
            seen.add(id(d))
            if (not d.is_dma) and (not op.is_dma) and d.eng == "pe" and op.eng == "pe":
                continue
            op.deps.append(d)
            if not d.is_dma:
                d.signal = True
        for k in writes:
            self.last_w[k] = op
            self.readers[k] = []
        for k in reads:
            self.readers.setdefault(k, []).append(op)

    def op(self, eng, fn, reads=(), writes=()):
        pr = [k for k in reads if k.startswith("ps")]
        if pr:
            reads = [k for k in reads if not k.startswith("ps")]
            writes = list(writes) + [k for k in pr if k not in writes]
        o = _Op(eng, fn)
        self._collect(o, reads, writes)
        self.ops[eng].append(o)
        return o

    def dma(self, eng, fn, sem, reads=(), writes=(), background=False):
        o = _Op(eng, fn, is_dma=True)
        o.sem = sem
        self.dma_sem_count[id(sem)] += 16
        o.val = self.dma_sem_count[id(sem)]
        self._collect(o, reads, writes)
        self.ops[eng].append(o)
        if background:
            self.background_dmas.append(o)
        else:
            self.pending_dmas.append(o)
        return o

    def barrier_wait(self, eng, deps):
        o = _Op(eng, None)
        for d in deps:
            o.deps.append(d)
            if not d.is_dma:
                d.signal = True
        self.ops[eng].append(o)
        return o

    def fence(self, include_background=False):
        if include_background:
            self.pending_dmas = self.pending_dmas + self.background_dmas
            self.background_dmas = []
        lasts = []
        for e in self.ENGS:
            for o in reversed(self.ops[e]):
                if (not o.is_dma) and o.fn is not None:
                    lasts.append(o)
                    break
        deps = lasts + self.pending_dmas
        for e in self.ENGS:
            self.barrier_wait(e, deps)
        self.pending_dmas = []
        keep = {k: v for k, v in self.last_w.items() if k.startswith("cv")}
        self.last_w = keep
        self.readers = {}

    def emit(self, block):
        for e in self.ENGS:
            c = 0
            for o in self.ops[e]:
                if o.is_dma:
                    continue
                if o.signal:
                    c += 1
                    o.sem = self.eng_sem[e]
                    o.val = c
        stats = {}

        def run(e, engine):
            known = {}
            nw = 0
            for o in self.ops[e]:
                need = {}
                for d in o.deps:
                    key = id(d.sem)
                    if need.get(key, (None, 0))[1] < d.val:
                        need[key] = (d.sem, d.val)
                for key, (sem, val) in need.items():
                    if known.get(key, 0) < val:
                        engine.wait_ge(sem, val)
                        known[key] = val
                        nw += 1
                if o.fn is None:
                    continue
                ins = o.fn(engine)
                if o.is_dma:
                    ins.then_inc(o.sem, 16)
                elif o.signal:
                    ins.then_inc(o.sem, 1)
            stats[e] = (len(self.ops[e]), nw)

        @block.tensor
        def _(t):
            run("pe", t)

        @block.scalar
        def _(s):
            run("act", s)

        @block.vector
        def _(v):
            run("dve", v)

        @block.gpsimd
        def _(g):
            run("pool", g)

        @block.sync
        def _(s):
            run("sp", s)
        return stats


def build_program(debug=False, stages=3, cut=99):
    nc = bass.Bass("TRN2", target_bir_lowering=False)
    din = lambda name, shape: nc.dram_tensor(name, shape, F32, kind="ExternalInput").ap()
    xown = din("xown", [R, D])
    xext = din("xext", [E, D])
    w_in = din("w_in", [D, 12320])
    w_out = din("w_out", [D, D])
    w_up = din("w_up", [D, 2 * DFF])
    w_down = din("w_down", [DFF, D])
    wgin = din("wgin", [D, 80])
    wgb = din("wgb", [96, 2048])
    gvec_d = din("gvec", [128, 96])
    ghead_d = din("ghead", [128, 2])
    cmw_d = din("cmw", [128, 48])
    fcw_d = din("fcw", [128, NF * 4])
    consts_d = din("consts", [128, 7 * 128])
    flags_d = din("flags", [128, 2])
    y = nc.dram_tensor("y", [R, D], F32, kind="ExternalOutput").ap()
    skind = "ExternalOutput" if debug else "Internal"
    hT_scr = nc.dram_tensor("hT_scr", [128, KC, R + E], BF16, kind=skind).ap()
    xT_scr = nc.dram_tensor("xT_scr", [128, KC, R], F32, kind=skind).ap()
    mT_scr = nc.dram_tensor("mT_scr", [128, KC, R], BF16, kind=skind).ap()

    wo_scr = nc.dram_tensor("wo_scr", [16, 128, KC, 256], BF16).ap()
    wu_scr = nc.dram_tensor("wu_scr", [NF, 128, KC, 256], BF16).ap()
    wd_scr = nc.dram_tensor("wd_scr", [8, 8, 128, 11, 512], BF16).ap()
    w_inv = w_in.rearrange("(kc p) c -> p kc c", p=128)
    w_outv = w_out.rearrange("(kc p) c -> p kc c", p=128)
    w_upv = w_up.rearrange("(kc p) c -> p kc c", p=128)
    w_downv = w_down.rearrange("(f p) c -> p f c", p=128)
    wginv = wgin.rearrange("(kc p) c -> p kc c", p=128)

    with ExitStack() as es:
        P = Prog(nc, es)
        sb = lambda ctx, name, shape, dt: ctx.enter_context(nc.sbuf_tensor(name, shape, dt))
        psA = [es.enter_context(nc.psum_tensor("psA%d" % i, [128, 512], F32)) for i in range(2)]
        psG = [es.enter_context(nc.psum_tensor("psG%d" % i, [128, 512], F32)) for i in range(2)]
        psS = es.enter_context(nc.psum_tensor("psS", [128, 512], F32))
        psO = es.enter_context(nc.psum_tensor("psO", [128, 512], F32))
        psT = es.enter_context(nc.psum_tensor("psT", [128, 1024], BF16))
        psX = es.enter_context(nc.psum_tensor("psX", [128, 512], F32))

        def MM(out, lhsT, rhs, start=True, stop=True, r=(), w=()):
            return P.op("pe", lambda e: e.matmul(out, lhsT=lhsT, rhs=rhs, start=start, stop=stop), r, w)

        def TR(out, in_, ident, r=(), w=()):
            return P.op("pe", lambda e: e.transpose(out=out, in_=in_, identity=ident), r, w)

        def ACT(out, in_, func, r=(), w=(), **kw):
            return P.op("act", lambda e: e.activation(out=out, in_=in_, func=func, **kw), r, w)

        def TT(eng, out, in0, in1, op, r=(), w=()):
            return P.op(eng, lambda e: e.tensor_tensor(out=out, in0=in0, in1=in1, op=op), r, w)

        def TS(eng, out, in0, s1, s2, op0, op1=None, r=(), w=()):
            if op1 is None:
                return P.op(eng, lambda e: e.tensor_scalar(out=out, in0=in0, scalar1=s1, scalar2=None, op0=op0), r, w)
            return P.op(eng, lambda e: e.tensor_scalar(out=out, in0=in0, scalar1=s1, scalar2=s2, op0=op0, op1=op1), r, w)

        def STT(eng, out, in0, scalar, in1, op0, op1, r=(), w=()):
            return P.op(eng, lambda e: e.scalar_tensor_tensor(out=out, in0=in0, scalar=scalar, in1=in1, op0=op0, op1=op1), r, w)

        def CP(eng, out, in_, r=(), w=()):
            return P.op(eng, lambda e: e.tensor_copy(out=out, in_=in_), r, w)

        def MSET(eng, ap, val, r=(), w=()):
            return P.op(eng, lambda e: e.memset(ap, val), r, w)

        def DMA(eng, out, in_, sem, r=(), w=(), background=False):
            return P.dma(eng, lambda e: e.dma_start(out=out, in_=in_), sem, r, w, background=background)

        conv_jobs = []
        for cb2 in range(16):
            conv_jobs.append((wo_scr[cb2], w_outv[:, :, cb2 * 256:(cb2 + 1) * 256]))
        for f in range(NF):
            conv_jobs.append((wu_scr[f, :, :, 0:128], w_upv[:, :, f * 128:(f + 1) * 128]))
            conv_jobs.append((wu_scr[f, :, :, 128:256], w_upv[:, :, DFF + f * 128:DFF + (f + 1) * 128]))
        for gi, (f0, nf) in enumerate(FGROUPS):
            for cb4 in range(8):
                conv_jobs.append((wd_scr[gi, cb4, :, 0:nf, :], w_downv[:, f0:f0 + nf, cb4 * 512:(cb4 + 1) * 512]))
        conv_sems = [P.new_dma_sem("s_cv%d" % i) for i in range(4)]
        conv_state = [0, 0.0]
        n_hooks = 33 + 8 * 13 + 16 * 5
        per_hook = len(conv_jobs) / float(n_hooks) + 0.01

        def conv_hook(flush=False):
            conv_state[1] += per_hook
            while conv_state[0] < len(conv_jobs) and (flush or conv_state[0] < conv_state[1]):
                o_ap, i_ap = conv_jobs[conv_state[0]]
                DMA("pool", o_ap, i_ap, conv_sems[conv_state[0] % 4], w=["cv%d" % (conv_state[0] % 4)], background=True)
                conv_state[0] += 1

        consts = sb(es, "consts_sb", [128, 7, 128], F32)
        identb = sb(es, "identb", [128, 128], BF16)
        onesb = sb(es, "onesb", [128, 128], BF16)
        gvec = sb(es, "gvecs", [128, 96], F32)
        ghead = sb(es, "gheads", [128, 2], F32)
        cmw = sb(es, "cmws", [128, 16, 3], F32)
        fcw = sb(es, "fcws", [128, NF, 4], F32)
        flags = sb(es, "flagss", [128, 2], F32)
        sc = [P.new_dma_sem("sc%d" % i) for i in range(6)]
        DMA("sp", consts[:].rearrange("p a b -> p (a b)"), consts_d[:, :], sc[0], w=["consts"])
        DMA("sp", gvec[:], gvec_d[:, :], sc[1], w=["gvec"])
        DMA("sp", ghead[:], ghead_d[:, :], sc[2], w=["ghead"])
        DMA("sp", cmw[:].rearrange("p a b -> p (a b)"), cmw_d[:, :], sc[3], w=["cmw"])
        DMA("sp", fcw[:].rearrange("p a b -> p (a b)"), fcw_d[:, :], sc[4], w=["fcw"])
        DMA("sp", flags[:], flags_d[:, :], sc[5], w=["flags"])
        CP("dve", identb[:], consts[:, 0, :], r=["consts"], w=["identb"])
        MSET("dve", onesb[:], 1.0 / 4096.0, w=["onesb"])
        ident = consts[:, 0, :]
        Mle, Mge, Mgt, Mlt = consts[:, 1, :], consts[:, 2, :], consts[:, 3, :], consts[:, 4, :]
        masks = consts[:, 5:7, :]

        with ExitStack() as sa:
            xin = [sb(sa, "xin%d" % i, [128, D], F32) for i in range(2)]
            xs = [sb(sa, "xs%d" % i, [128, D], F32) for i in range(2)]
            xTs = [sb(sa, "xTs%d" % i, [128, KC, 128], F32) for i in range(2)]
            hTs = [sb(sa, "hTs%d" % i, [128, KC, 128], BF16) for i in range(2)]
            junkA = sb(sa, "junkA", [128, D], BF16)
            ssA = [sb(sa, "ssA%d" % i, [128, 1], F32) for i in range(2)]
            rsA = [sb(sa, "rsA%d" % i, [128, 1], F32) for i in range(2)]
            s_xin = [P.new_dma_sem("s_xin%d" % i) for i in range(2)]
            s_xst = [P.new_dma_sem("s_xst%d" % i) for i in range(2)]
            s_hst = [P.new_dma_sem("s_hst%d" % i) for i in range(2)]
            blocks = [(xown, i * 128, 128, True, i * 128) for i in range(16)] + [(xown, 2048, 16, True, 2048)]
            blocks += [(xext, i * 128, 128, False, R + i * 128) for i in range(16)]
            def load_blk(bi):
                src, r0, nr, own, c0 = blocks[bi]
                DMA("sp", xin[bi % 2][:nr, :], src[r0:r0 + nr, :], s_xin[bi % 2], w=["xin%d" % (bi % 2)])
            load_blk(0)
            for bi, (src, r0, nr, own, c0) in enumerate(blocks):
                sl = bi % 2
                conv_hook()
                MSET("dve", ssA[sl][:], 0.0, w=["ssA%d" % sl])
                ACT(junkA[:nr, :], xin[sl][:nr, :], AF.Square, r=["xin%d" % sl, "ssA%d" % sl], w=["junkA", "ssA%d" % sl],
                    accum_out=ssA[sl][:nr, 0:1])
                ACT(rsA[sl][:nr, :], ssA[sl][:nr, :], AF.Ln, r=["ssA%d" % sl], w=["rsA%d" % sl], scale=1.0 / D, bias=EPS)
                ACT(rsA[sl][:nr, :], rsA[sl][:nr, :], AF.Exp, r=["rsA%d" % sl], w=["rsA%d" % sl], scale=-0.5)
                TS("dve", xs[sl][:nr, :], xin[sl][:nr, :], rsA[sl][:nr, 0:1], None, ALU.mult, r=["xin%d" % sl, "rsA%d" % sl], w=["xs%d" % sl])
                for g in range(8):
                    if own:
                        bank = psA[g % 2]
                        for j in range(4):
                            kc = g * 4 + j
                            TR(bank[:, j * 128:j * 128 + nr], xin[sl][:nr, kc * 128:(kc + 1) * 128], ident[:nr, :nr],
                               r=["xin%d" % sl, "consts"], w=["psA%d" % (g % 2)])
                        ACT(xTs[sl][:, g * 4:(g + 1) * 4, :nr], bank[:].rearrange("p (a b) -> p a b", a=4)[:, :, :nr], AF.Copy,
                            r=["psA%d" % (g % 2)], w=["xTs%d_%d" % (sl, g)])
                    bank2 = psG[g % 2]
                    for j in range(4):
                        kc = g * 4 + j
                        TR(bank2[:, j * 128:j * 128 + nr], xs[sl][:nr, kc * 128:(kc + 1) * 128], ident[:nr, :nr],
                           r=["xs%d" % sl, "consts"], w=["psG%d" % (g % 2)])
                    TT("dve", hTs[sl][:, g * 4:(g + 1) * 4, :nr], bank2[:].rearrange("p (a b) -> p a b", a=4)[:, :, :nr],
                       gvec[:, g * 4:(g + 1) * 4].unsqueeze(2).to_broadcast([128, 4, nr]), ALU.mult,
                       r=["psG%d" % (g % 2), "gvec"], w=["hTs%d_%d" % (sl, g)])
                hk = ["hTs%d_%d" % (sl, g) for g in range(8)]
                if bi + 1 < len(blocks):
                    load_blk(bi + 1)
                DMA("sp", hT_scr[:, :, c0:c0 + nr], hTs[sl][:, :, :nr], s_hst[sl], r=hk, w=["hT_scr"])
                if own:
                    xk = ["xTs%d_%d" % (sl, g) for g in range(8)]
                    DMA("sp", xT_scr[:, :, r0:r0 + nr], xTs[sl][:, :, :nr], s_xst[sl], r=xk, w=["xT_scr"])
        P.fence()

        if stages >= 2:
          with ExitStack() as sB:
            wq = sb(sB, "wq", [128, KC, 128], BF16)
            wk = sb(sB, "wk", [128, KC, 128], BF16)
            wv = sb(sB, "wv", [128, KC, 256], BF16)
            wr = sb(sB, "wr", [128, KC, 256], BF16)
            hts = [sb(sB, "hts%d" % i, [128, KC, 413], BF16) for i in range(2)]
            s_w = {k: P.new_dma_sem("s_w" + k) for k in ("q", "k", "v", "r")}
            s_ht = [P.new_dma_sem("s_ht%d" % i) for i in range(2)]
            s_mst = P.new_dma_sem("s_mst")
            ht_ctr = [0]

            def load_ht(c0, n):
                sl = ht_ctr[0] % 2
                ht_ctr[0] += 1
                conv_hook()
                DMA("sp", hts[sl][:, :, :n], hT_scr[:, :, c0:c0 + n], s_ht[sl], r=["hT_scr"], w=["hts%d" % sl])
                return hts[sl], "hts%d" % sl

            acc_ctr = [0]

            def proj(wt, wkey, wcols, ht, htkey, n, evac, M=128):
                i = acc_ctr[0] % 2
                acc_ctr[0] += 1
                acc = psA[i]
                for kc in range(KC):
                    MM(acc[:M, :n], wt[:, kc, wcols], ht[:, kc, :n], start=(kc == 0), stop=(kc == KC - 1),
                       r=[wkey, htkey], w=["psA%d" % i])
                evac(acc, "psA%d" % i)

            with ExitStack() as sg:
                wg_sb = sb(sg, "wg_sb", [128, KC, 80], BF16)
                qT = sb(sg, "qT", [128, R], BF16)
                kT = sb(sg, "kT", [128, R], BF16)
                rs = sb(sg, "rs", [128, 2, R], BF16)
                vt = sb(sg, "vt", [128, 2, 413], BF16)
                kte = sb(sg, "kte", [128, ET], BF16)
                kv = sb(sg, "kv", [128, 20, 384], BF16)
                kve = sb(sg, "kve", [128, 2, 384], BF16)
                Sbb = sb(sg, "Sbb", [128, 20, 256], BF16)
                aT = sb(sg, "aT", [96, R], F32)
                wgb_sb = sb(sg, "wgb_sb", [96, 2048], F32)
                mst = sb(sg, "mst", [128, 2, R], BF16)
                ex = sb(sg, "ex", [128, 256], F32)
                lsb = sb(sg, "lsb", [128, 256], F32)
                E1 = sb(sg, "E1", [128, 256], F32)
                E2 = sb(sg, "E2", [128, 256], F32)
                E3 = sb(sg, "E3", [128, 256], F32)
                qq = sb(sg, "qq", [128, 2, 128], BF16)
                kk = sb(sg, "kk", [128, 2, 128], BF16)
                khf = sb(sg, "khf", [128, 128], BF16)
                khb = sb(sg, "khb", [128, 128], BF16)
                PT = sb(sg, "PT", [128, 2, 128], BF16)
                on = sb(sg, "on", [128, 256], BF16)
                junk = sb(sg, "junk", [128, 256], BF16)
                Sf = sb(sg, "Sf", [128, 256], F32)
                Sb_ = sb(sg, "Sb_", [128, 256], F32)
                Se = sb(sg, "Se", [128, 256], F32)
                Sfb = sb(sg, "Sfb", [128, 256], BF16)
                ss1 = sb(sg, "ss1", [128, 1], F32)
                rstd1 = sb(sg, "rstd1", [128, 1], F32)
                s_wg = P.new_dma_sem("s_wg")
                s_wgb = P.new_dma_sem("s_wgb")

                DMA("pool", wg_sb[:], wginv[:, :, :], s_wg, w=["wg_sb"])
                DMA("sp", wgb_sb[:], wgb[:, :], s_wgb, w=["wgb_sb"])
                MSET("dve", aT[:, :], 1.0, w=["aT"])

                for ti in range(5):
                    s0, e0 = TB[ti], TB[ti + 1]
                    n = e0 - s0
                    ht, hk = load_ht(s0, n)

                    def ev(acc, ak, s0=s0, n=n):
                        ACT(aT[0:16, s0:s0 + n], acc[0:16, :n], AF.Copy, r=[ak], w=["aT"])
                        ACT(aT[32:48, s0:s0 + n], acc[32:48, :n], AF.Copy, r=[ak], w=["aT"])
                    proj(wg_sb, "wg_sb", slice(0, 80), ht, hk, n, ev, M=80)
                for et in range(E // ET):
                    ht, hk = load_ht(R + et * ET, ET)

                    def ev(acc, ak, et=et):
                        ACT(aT[64:80, et * ET:(et + 1) * ET], acc[64:80, :ET], AF.Copy, r=[ak], w=["aT"])
                    proj(wg_sb, "wg_sb", slice(0, 80), ht, hk, ET, ev, M=80)

                def gates_decays(cols0, C, j, frow, do_b):
                    if do_b:
                        MM(psG[0][:C, 0:256], aT[0:64, cols0:cols0 + C], wgb_sb[0:64, j * 256:(j + 1) * 256],
                           r=["aT", "wgb_sb"], w=["psG0"])
                        W = 256
                    else:
                        MM(psG[0][:C, 0:128], aT[frow:frow + 32, cols0:cols0 + C], wgb_sb[frow:frow + 32, j * 256:j * 256 + 128],
                           r=["aT", "wgb_sb"], w=["psG0"])
                        W = 128
                    ACT(ex[:C, :W], psG[0][:C, :W], AF.Exp, r=["psG0"], w=["ex"], scale=-1.0)
                    ACT(lsb[:C, :W], ex[:C, :W], AF.Ln, r=["ex"], w=["lsb"], bias=1.0)
                    MM(psG[1][:, 0:C], lsb[:C, 0:128], Mle[:C, :C], r=["lsb", "consts"], w=["psG1"])
                    MM(psG[1][:C, 256:384], Mgt[:C, :C], lsb[:C, 0:128], r=["lsb", "consts"], w=["psG1"])
                    if do_b:
                        MM(psG[1][:, 128:128 + C], lsb[:C, 128:256], Mge[:C, :C], r=["lsb", "consts"], w=["psG1"])
                        MM(psG[1][:C, 384:512], Mlt[:C, :C], lsb[:C, 128:256], r=["lsb", "consts"], w=["psG1"])

                def load_head_w(j):
                    DMA("pool", wk[:], w_inv[:, :, OFF_K + j * 128:OFF_K + (j + 1) * 128], s_w["k"], w=["wk"])
                    DMA("pool", wv[:], w_inv[:, :, OFF_V + j * 256:OFF_V + (j + 1) * 256], s_w["v"], w=["wv"])
                    DMA("pool", wq[:], w_inv[:, :, OFF_Q + j * 128:OFF_Q + (j + 1) * 128], s_w["q"], w=["wq"])
                    DMA("pool", wr[:], w_inv[:, :, OFF_R + j * 256:OFF_R + (j + 1) * 256], s_w["r"], w=["wr"])

                NH = 8 if cut >= 6 else (1 if cut >= 2 else 0)
                if NH:
                    load_head_w(0)
                for j in range(NH):
                    MSET("dve", Se[:], 0.0, w=["Se"])
                    for et in range(E // ET):
                        ht, hk = load_ht(R + et * ET, ET)
                        proj(wk, "wk", slice(0, 128), ht, hk, ET,
                             lambda acc, ak: ACT(kte[:, :ET], acc[:, :ET], AF.Copy, r=[ak], w=["kte"]))
                        for h in range(2):
                            proj(wv, "wv", slice(h * 128, (h + 1) * 128), ht, hk, ET,
                                 lambda acc, ak, h=h: ACT(vt[:, h, :ET], acc[:, :ET], AF.Copy, r=[ak], w=["vt%d" % h]))
                        for ci in range(ET // 128):
                            lc = ci * 128
                            TR(psT[:, 0:128], kte[:, lc:lc + 128], identb[:, :], r=["kte", "identb"], w=["psT"])
                            for h in range(2):
                                TR(psT[:, 128 + h * 128:256 + h * 128], vt[:, h, lc:lc + 128], identb[:, :],
                                   r=["vt%d" % h, "identb"], w=["psT"])
                            CP("dve", kve[:, ci, :], psT[:, 0:384], r=["psT"], w=["kve%d" % ci])
                            ec = et * ET + lc
                            gates_decays(ec, 128, j, 64, False)
                            ACT(E1[:, 0:128], psG[1][:, 0:128], AF.Exp, r=["psG1"], w=["E1"])
                            ACT(E3[:, 0:128], psG[1][:, 256:384], AF.Exp, r=["psG1"], w=["E3"])
                            TT("dve", khf[:, :], kve[:, ci, 0:128], E3[:, 0:128], ALU.mult, r=["kve%d" % ci, "E3"], w=["khf"])
                            MM(psX[:, 0:256], khf[:, :], kve[:, ci, 128:384], r=["khf", "kve%d" % ci], w=["psX"])
                            STT("dve", Se[:], Se[:], E1[:, 127:128], psX[:, 0:256], ALU.mult, ALU.add,
                                r=["Se", "E1", "psX"], w=["Se"])
                    TS("dve", Sf[:], Se[:], flags[:, 0:1], None, ALU.mult, r=["Se", "flags"], w=["Sf"])
                    TS("dve", Sb_[:], Se[:], flags[:, 1:2], None, ALU.mult, r=["Se", "flags"], w=["Sb"])
                    for ti in range(5 if cut >= 3 else 0):
                        s0, e0 = TB[ti], TB[ti + 1]
                        n = e0 - s0
                        ht, hk = load_ht(s0, n)
                        proj(wq, "wq", slice(0, 128), ht, hk, n,
                             lambda acc, ak, s0=s0, n=n: ACT(qT[:, s0:s0 + n], acc[:, :n], AF.Copy, r=[ak], w=["qT"], scale=128.0 ** -0.5))
                        proj(wk, "wk", slice(0, 128), ht, hk, n,
                             lambda acc, ak, s0=s0, n=n: ACT(kT[:, s0:s0 + n], acc[:, :n], AF.Copy, r=[ak], w=["kT"]))
                        for h in range(2):
                            proj(wv, "wv", slice(h * 128, (h + 1) * 128), ht, hk, n,
                                 lambda acc, ak, h=h, n=n: ACT(vt[:, h, :n], acc[:, :n], AF.Copy, r=[ak], w=["vt%d" % h]))
                        for h in range(2):
                            proj(wr, "wr", slice(h * 128, (h + 1) * 128), ht, hk, n,
                                 lambda acc, ak, h=h, s0=s0, n=n: ACT(rs[:, h, s0:s0 + n], acc[:, :n], AF.Silu, r=[ak], w=["rs"]))
                        for c in range(ti * 4, ti * 4 + 4):
                            cs, C = CHUNKS[c]
                            lc = cs - s0
                            TR(psT[:C, 0:128], kT[:, cs:cs + C], identb[:, :], r=["kT", "identb"], w=["psT"])
                            for h in range(2):
                                TR(psT[:C, 128 + h * 128:256 + h * 128], vt[:, h, lc:lc + C], identb[:, :],
                                   r=["vt%d" % h, "identb"], w=["psT"])
                            CP("dve", kv[:C, c, :], psT[:C, 0:384], r=["psT"], w=["kv"])
                    if j + 1 < NH:
                        load_head_w(j + 1)
                    for c in reversed(range(20 if cut >= 4 else 0)):
                        cs, C = CHUNKS[c]
                        gates_decays(cs, C, j, 0, True)
                        ACT(E1[:, 0:256], psG[1][:, 0:256], AF.Exp, r=["psG1"], w=["E1"])
                        ACT(E3[:C, 0:256], psG[1][:C, 256:512], AF.Exp, r=["psG1"], w=["E3"])
                        ACT(Sbb[:, c, :], Sb_[:], AF.Copy, r=["Sb"], w=["Sbb"])
                        TT("dve", khb[:C, :], kv[:C, c, 0:128], E3[:C, 128:256], ALU.mult, r=["kv", "E3"], w=["khb"])
                        MM(psX[:, 0:256], khb[:C, :], kv[:C, c, 128:384], r=["khb", "kv"], w=["psX"])
                        STT("dve", Sb_[:], Sb_[:], E1[:, 128:129], psX[:, 0:256], ALU.mult, ALU.add,
                            r=["Sb", "E1", "psX"], w=["Sb"])
                    for c in range(20 if cut >= 5 else 0):
                        cs, C = CHUNKS[c]
                        gates_decays(cs, C, j, 0, True)
                        ACT(E1[:, 0:256], psG[1][:, 0:256], AF.Exp, r=["psG1"], w=["E1"])
                        ACT(E2[:, 0:256], psG[1][:, 0:256], AF.Exp, r=["psG1"], w=["E2"], scale=-1.0)
                        ACT(E3[:C, 0:256], psG[1][:C, 256:512], AF.Exp, r=["psG1"], w=["E3"])
                        TT("dve", qq[:, :, :C], qT[:, cs:cs + C].unsqueeze(1).to_broadcast([128, 2, C]),
                           E1[:].rearrange("p (a b) -> p a b", a=2)[:, :, :C], ALU.mult, r=["qT", "E1"], w=["qq"])
                        TT("dve", kk[:, :, :C], kT[:, cs:cs + C].unsqueeze(1).to_broadcast([128, 2, C]),
                           E2[:].rearrange("p (a b) -> p a b", a=2)[:, :, :C], ALU.mult, r=["kT", "E2"], w=["kk"])
                        TT("dve", khf[:C, :], kv[:C, c, 0:128], E3[:C, 0:128], ALU.mult, r=["kv", "E3"], w=["khf"])
                        MM(psS[:C, 0:C], kk[:, 0, :C], qq[:, 0, :C], r=["kk", "qq"], w=["psS"])
                        MM(psS[:C, 128:128 + C], kk[:, 1, :C], qq[:, 1, :C], r=["kk", "qq"], w=["psS"])
                        TT("dve", PT[:C, :, :C], psS[:C, 0:256].rearrange("p (a b) -> p a b", a=2)[:, :, :C], masks[:C, :, :C],
                           ALU.mult, r=["psS", "consts"], w=["PT"])
                        ACT(Sfb[:], Sf[:], AF.Copy, r=["Sf"], w=["Sfb"])
                        MM(psO[:C, 0:256], PT[:C, 0, :C], kv[:C, c, 128:384], start=True, stop=False, r=["PT", "kv"], w=["psOo"])
                        MM(psO[:C, 0:256], PT[:C, 1, :C], kv[:C, c, 128:384], start=False, stop=False, r=["PT", "kv"], w=["psOo"])
                        MM(psO[:C, 0:256], qq[:, 0, :C], Sfb[:, :], start=False, stop=False, r=["qq", "Sfb"], w=["psOo"])
                        MM(psO[:C, 0:256], qq[:, 1, :C], Sbb[:, c, :], start=False, stop=True, r=["qq", "Sbb"], w=["psOo"])
                        MM(psX[:, 0:256], khf[:C, :], kv[:C, c, 128:384], r=["khf", "kv"], w=["psX"])
                        STT("dve", Sf[:], Sf[:], E1[:, C - 1:C], psX[:, 0:256], ALU.mult, ALU.add,
                            r=["Sf", "E1", "psX"], w=["Sf"])
                        MSET("dve", ss1[:], 0.0, w=["ss1"])
                        ACT(junk[:C, :], psO[:C, 0:256], AF.Square, r=["psOo", "ss1"], w=["junk", "ss1"], accum_out=ss1[:C, 0:1])
                        ACT(rstd1[:C, :], ss1[:C, :], AF.Ln, r=["ss1"], w=["rstd1"], scale=1.0 / 256, bias=EPS)
                        ACT(rstd1[:C, :], rstd1[:C, :], AF.Exp, r=["rstd1"], w=["rstd1"], scale=-0.5)
                        TS("dve", on[:C, :], psO[:C, 0:256], rstd1[:C, 0:1], None, ALU.mult, r=["psOo", "rstd1"], w=["on"])
                        for h in range(2):
                            TR(psT[:, 512 + h * 128:512 + h * 128 + C], on[:C, h * 128:(h + 1) * 128], identb[:C, :C],
                               r=["on", "identb"], w=["psT"])
                        for h in range(2):
                            STT("dve", mst[:, h, cs:cs + C], psT[:, 512 + h * 128:512 + h * 128 + C], ghead[:, h:h + 1],
                                rs[:, h, cs:cs + C], ALU.mult, ALU.mult, r=["psT", "ghead", "rs"], w=["mst"])
                    DMA("sp", mT_scr[:, 2 * j:2 * j + 2, :], mst[:, :, :], s_mst, r=["mst"], w=["mT_scr"])
            P.fence()
            with ExitStack() as scv:
                ccs = sb(scv, "ccs", [128, 413], F32)
                prod = sb(scv, "prod", [128, R + 2], F32)
                cbs = sb(scv, "cbs", [128, R], F32)
                t1 = sb(scv, "t1", [128, R], F32)
                mcv = sb(scv, "mcv", [128, R], BF16)
                s_mcv = P.new_dma_sem("s_mcv")
                MSET("dve", prod[:, 0:1], 0.0, w=["prod"])
                MSET("dve", prod[:, R + 1:R + 2], 0.0, w=["prod"])
                s_w2 = {k: P.new_dma_sem("s_w2" + k) for k in ("a", "b", "c")}
                NCG = 16 if cut >= 7 else 0
                cwsets = [((wq, slice(0, 128), "wq", s_w["q"]), (wk, slice(0, 128), "wk", s_w["k"]), (wv, slice(0, 128), "wv", s_w["v"])),
                          ((wr, slice(0, 128), "wr_a", s_w2["a"]), (wr, slice(128, 256), "wr_b", s_w2["b"]), (wv, slice(128, 256), "wv_b", s_w2["c"]))]

                def load_conv_w(cg):
                    for (wt_, sl_, key_, sem_), off_ in zip(cwsets[cg % 2], (OFF_CB, OFF_CC, OFF_CH)):
                        DMA("pool", wt_[:, :, sl_], w_inv[:, :, off_ + cg * 128:off_ + (cg + 1) * 128], sem_, w=[key_])
                if NCG:
                    load_conv_w(0)
                for cg in range(NCG):
                    if cg + 1 < NCG:
                        load_conv_w(cg + 1)
                    (wA, slA, kA, _), (wB, slB, kB, _), (wC, slC, kC, _) = cwsets[cg % 2]
                    for ti in range(5):
                        s0, e0 = TB[ti], TB[ti + 1]
                        n = e0 - s0
                        ht, hk = load_ht(s0, n)
                        proj(wA, kA, slA, ht, hk, n,
                             lambda acc, ak, s0=s0, n=n: ACT(cbs[:, s0:s0 + n], acc[:, :n], AF.Copy, r=[ak], w=["cbs"]))
                        proj(wB, kB, slB, ht, hk, n,
                             lambda acc, ak, n=n: ACT(ccs[:, :n], acc[:, :n], AF.Copy, r=[ak], w=["ccs"]))
                        proj(wC, kC, slC, ht, hk, n,
                             lambda acc, ak, s0=s0, n=n: TT("dve", prod[:, 1 + s0:1 + s0 + n], acc[:, :n], ccs[:, :n], ALU.mult,
                                                            r=[ak, "ccs"], w=["prod"]))
                    TS("dve", t1[:, :], prod[:, 0:R], cmw[:, cg, 0:1], None, ALU.mult, r=["prod", "cmw"], w=["t1"])
                    STT("dve", t1[:, :], prod[:, 1:R + 1], cmw[:, cg, 1:2], t1[:, :], ALU.mult, ALU.add, r=["prod", "cmw", "t1"], w=["t1"])
                    STT("dve", t1[:, :], prod[:, 2:R + 2], cmw[:, cg, 2:3], t1[:, :], ALU.mult, ALU.add, r=["prod", "cmw", "t1"], w=["t1"])
                    TT("dve", mcv[:, :], t1[:, :], cbs[:, :], ALU.mult, r=["t1", "cbs"], w=["mcv"])
                    DMA("sp", mT_scr[:, 16 + cg, :], mcv[:, :], s_mcv, r=["mcv"], w=["mT_scr"])
          conv_hook(flush=True)
          P.fence(include_background=True)

        if stages >= 3:
          with ExitStack() as sC:
            X1 = sb(sC, "X1", [128, KC, 415], F32)
            MH = sb(sC, "MH", [128, KC, 415], BF16)
            aTt = sb(sC, "aTt", [128, 11, 413], BF16)
            wring = [sb(sC, "wring%d" % i, [128, KC, 256], BF16) for i in range(3)]
            wdring = [sb(sC, "wdring%d" % i, [128, 11, 512], BF16) for i in range(2)]
            sqt = [sb(sC, "sqt%d" % i, [128, 415], BF16) for i in range(2)]
            rstdt = sb(sC, "rstdt", [128, 415], F32)
            c1 = [sb(sC, "c1_%d" % i, [128, 413], F32) for i in range(2)]
            sg_ = [sb(sC, "sg%d" % i, [128, 413], F32) for i in range(2)]
            ost = [sb(sC, "ost%d" % i, [128, 2048], F32) for i in range(2)]
            s_wr = [[P.new_dma_sem("s_wr%d_%d" % (i, h)) for h in range(2)] for i in range(3)]
            s_wd = [P.new_dma_sem("s_wd%d" % i) for i in range(2)]
            s_x1 = P.new_dma_sem("s_x1")
            s_mh = P.new_dma_sem("s_mh")
            s_ost = [P.new_dma_sem("s_ost%d" % i) for i in range(2)]
            wr_ctr = [0]
            wd_ctr = [0]
            ost_ctr = [0]
            out_dmas = []
            for ti in range(5):
                s0, e0 = TB[ti], TB[ti + 1]
                lo, hi = max(s0 - 1, 0), min(e0 + 1, R)
                N = e0 - s0 + 2
                NV = N - 2
                off = lo - (s0 - 1)
                if ti == 0:
                    MSET("dve", X1[:, :, 0:1], 0.0, w=["X1"])
                    MSET("dve", MH[:, :, 0:1], 0.0, w=["MH"])
                if ti == 4:
                    MSET("dve", X1[:, :, N - 1:N], 0.0, w=["X1"])
                    MSET("dve", MH[:, :, N - 1:N], 0.0, w=["MH"])
                DMA("sp", MH[:, :, off:off + hi - lo], mT_scr[:, :, lo:hi], s_mh, r=["mT_scr"], w=["MH"])
                DMA("sp", X1[:, :, off:off + hi - lo], xT_scr[:, :, lo:hi], s_x1, r=["xT_scr"], w=["X1"])
                for cb2 in range(16):
                    sl = wr_ctr[0] % 3
                    wr_ctr[0] += 1
                    DMA("pool", wring[sl][:, :, :], wo_scr[cb2], s_wr[sl][0], w=["wring%d" % sl])
                    for h in range(2):
                        cb = cb2 * 2 + h
                        acc = psA[cb % 2]
                        for kc in range(KC):
                            MM(acc[:, :N], wring[sl][:, kc, h * 128:(h + 1) * 128], MH[:, kc, :N], start=(kc == 0), stop=(kc == KC - 1),
                               r=["wring%d" % sl, "MH"], w=["psA%d" % (cb % 2)])
                        TT("dve", X1[:, cb, :N], acc[:, :N], X1[:, cb, :N], ALU.add, r=["psA%d" % (cb % 2), "X1"], w=["X1c%d" % cb])
                        ACT(sqt[cb % 2][:, :N], X1[:, cb, :N], AF.Square, r=["X1c%d" % cb], w=["sqt%d" % (cb % 2)])
                        MM(psX[:, :N], onesb[:, :], sqt[cb % 2][:, :N], start=(cb == 0), stop=(cb == 31), r=["sqt%d" % (cb % 2), "onesb"], w=["psX"])
                xck = ["X1c%d" % cb for cb in range(32)]
                ACT(rstdt[:, :N], psX[:, :N], AF.Ln, r=["psX"], w=["rstdt"], bias=EPS)
                ACT(rstdt[:, :N], rstdt[:, :N], AF.Exp, r=["rstdt"], w=["rstdt"], scale=-0.5)
                for kc in range(KC):
                    STT("dve", MH[:, kc, :N], X1[:, kc, :N], gvec[:, 32 + kc:33 + kc], rstdt[:, :N], ALU.mult, ALU.mult,
                        r=["X1c%d" % kc, "gvec", "rstdt"], w=["MH"])
                for gi, (f0, nf) in enumerate(FGROUPS):
                    for fl in range(nf):
                        f = f0 + fl
                        sl = wr_ctr[0] % 3
                        wr_ctr[0] += 1
                        DMA("pool", wring[sl][:, :, :], wu_scr[f], s_wr[sl][0], w=["wring%d" % sl])
                        pu, pg = psA[f % 2], psG[f % 2]
                        for kc in range(KC):
                            MM(pu[:, :N], wring[sl][:, kc, 0:128], MH[:, kc, :N], start=(kc == 0), stop=(kc == KC - 1),
                               r=["wring%d" % sl, "MH"], w=["psA%d" % (f % 2)])
                        for kc in range(KC):
                            MM(pg[:, :N], wring[sl][:, kc, 128:256], MH[:, kc, :N], start=(kc == 0), stop=(kc == KC - 1),
                               r=["wring%d" % sl, "MH"], w=["psG%d" % (f % 2)])
                        cc1 = c1[f % 2]
                        TS("dve", cc1[:, :NV], pg[:, 0:NV], fcw[:, f, 0:1], fcw[:, f, 3:4], ALU.mult, ALU.add,
                           r=["psG%d" % (f % 2), "fcw"], w=["c1_%d" % (f % 2)])
                        STT("dve", cc1[:, :NV], pg[:, 1:NV + 1], fcw[:, f, 1:2], cc1[:, :NV], ALU.mult, ALU.add,
                            r=["psG%d" % (f % 2), "fcw", "c1_%d" % (f % 2)], w=["c1_%d" % (f % 2)])
                        STT("dve", cc1[:, :NV], pg[:, 2:NV + 2], fcw[:, f, 2:3], cc1[:, :NV], ALU.mult, ALU.add,
                            r=["psG%d" % (f % 2), "fcw", "c1_%d" % (f % 2)], w=["c1_%d" % (f % 2)])
                        ACT(sg_[f % 2][:, :NV], cc1[:, :NV], AF.Silu, r=["c1_%d" % (f % 2)], w=["sg%d" % (f % 2)])
                        TT("dve", aTt[:, fl, :NV], sg_[f % 2][:, :NV], pu[:, 1:NV + 1], ALU.mult,
                           r=["sg%d" % (f % 2), "psA%d" % (f % 2)], w=["aTt"])
                    for cb4 in range(8):
                        dl = wd_ctr[0] % 2
                        wd_ctr[0] += 1
                        DMA("pool", wdring[dl][:, :nf, :], wd_scr[gi, cb4, :, 0:nf, :], s_wd[dl], w=["wdring%d" % dl])
                        for h in range(4):
                            cb = cb4 * 4 + h
                            yps = psS if cb % 2 == 0 else psO
                            yk = "psS" if cb % 2 == 0 else "psO"
                            for fl in range(nf):
                                MM(yps[:, :NV], wdring[dl][:, fl, h * 128:(h + 1) * 128], aTt[:, fl, :NV], start=(fl == 0), stop=(fl == nf - 1),
                                   r=["wdring%d" % dl, "aTt"], w=[yk])
                            TT("dve", X1[:, cb, 1:NV + 1], yps[:, :NV], X1[:, cb, 1:NV + 1], ALU.add, r=[yk, "X1c%d" % cb], w=["X1c%d" % cb])
                for cb in range(32):
                    ACT(sqt[cb % 2][:, :NV], X1[:, cb, 1:NV + 1], AF.Square, r=["X1c%d" % cb], w=["sqt%d" % (cb % 2)])
                    MM(psX[:, :NV], onesb[:, :], sqt[cb % 2][:, :NV], start=(cb == 0), stop=(cb == 31), r=["sqt%d" % (cb % 2), "onesb"], w=["psX"])
                ACT(rstdt[:, :NV], psX[:, :NV], AF.Ln, r=["psX"], w=["rstdt"], bias=EPS)
                ACT(rstdt[:, :NV], rstdt[:, :NV], AF.Exp, r=["rstdt"], w=["rstdt"], scale=-0.5)
                for kc in range(KC):
                    STT("dve", X1[:, kc, 1:NV + 1], X1[:, kc, 1:NV + 1], gvec[:, 64 + kc:65 + kc], rstdt[:, :NV], ALU.mult, ALU.mult,
                        r=["X1c%d" % kc, "gvec", "rstdt"], w=["X1c%d" % kc])
                for c in range(ti * 4, ti * 4 + 4):
                    cs, C = CHUNKS[c]
                    lc = cs - s0 + 1
                    for half in range(2):
                        ol = ost_ctr[0] % 2
                        ost_ctr[0] += 1
                        for g4 in range(4):
                            bank = psG[g4 % 2]
                            for jj in range(4):
                                kc = half * 16 + g4 * 4 + jj
                                TR(bank[:C, jj * 128:(jj + 1) * 128], X1[:, kc, lc:lc + C], ident[:, :],
                                   r=["X1c%d" % kc, "consts"], w=["psG%d" % (g4 % 2)])
                            ACT(ost[ol][:C, g4 * 512:(g4 + 1) * 512], bank[:C, :], AF.Copy, r=["psG%d" % (g4 % 2)], w=["ost%d" % ol])
                        out_dmas.append(DMA("sp", y[cs:cs + C, half * 2048:(half + 1) * 2048], ost[ol][:C, :], s_ost[ol], r=["ost%d" % ol]))
                P.op("dve", lambda e: e.memset(rstdt[:, 0:1], 0.0), reads=xck + ["MH"], writes=["X1", "MH", "rstdt"] + xck)
          P.fence()
        else:
            P.fence()
        P.barrier_wait("sp", list(P.ops["sp"][-1].deps))
        block = es.enter_context(nc.Block())
        stats = P.emit(block)
    return nc, stats


def _consts():
    a = np.arange(128)[:, None]
    b = np.arange(128)[None, :]
    c = np.zeros((128, 7, 128), np.float32)
    c[:, 0] = (a == b)
    c[:, 1] = (a <= b) * (-1.0 / 16.0)
    c[:, 2] = (a >= b) * (-1.0 / 16.0)
    c[:, 3] = (a > b) * (-1.0 / 16.0)
    c[:, 4] = (a < b) * (-1.0 / 16.0)
    c[:, 5] = (a <= b)
    c[:, 6] = (a > b)
    return c.reshape(128, 7 * 128)


def _prepare_inputs(x_prompt, x_sample, meta_tokens, mix_norm_g, w_in, w_gate2, b_gate2, head_norm_g,
                    conv_mix_w, w_out, ffn_norm_g, w_up, ffn_conv_w, ffn_conv_b, w_down, final_norm_g):
    f = lambda a: np.ascontiguousarray(np.asarray(a, dtype=np.float32))
    x_prompt, x_sample, meta = f(x_prompt), f(x_sample), f(meta_tokens)
    w_in0, w_out0, w_up0, w_down0 = f(w_in[0]), f(w_out[0]), f(w_up[0]), f(w_down[0])
    wg2, bg2 = f(w_gate2[0]), f(b_gate2[0])
    gvec = np.concatenate([f(mix_norm_g[0]).reshape(32, 128).T, f(ffn_norm_g[0]).reshape(32, 128).T,
                           f(final_norm_g).reshape(32, 128).T], axis=1)
    ghead = f(head_norm_g[0]).reshape(2, 128).T
    cmw = f(conv_mix_w[0]).reshape(3, 16, 128).transpose(2, 1, 0).reshape(128, 48)
    fcw = np.concatenate([f(ffn_conv_w[0]), f(ffn_conv_b[0])[None]], axis=0).reshape(4, NF, 128).transpose(2, 1, 0).reshape(128, NF * 4)
    consts = _consts()
    shared = dict(w_in=w_in0, w_out=w_out0, w_up=w_up0, w_down=w_down0,
                  gvec=f(gvec), ghead=f(ghead), cmw=f(cmw), fcw=f(fcw), consts=consts)

    def core(xown, xext, ext_dir, flag_f, flag_b):
        wgin = np.zeros((D, 80), np.float32)
        wgin[:, 0:16] = w_in0[:, 6144:6160]
        wgin[:, 32:48] = w_in0[:, 6160:6176]
        wgin[:, 64:80] = w_in0[:, 6144 + 16 * ext_dir:6160 + 16 * ext_dir]
        wgb = np.zeros((96, 8, 256), np.float32)
        wgb[0:16, :, 0:128] = wg2[0].reshape(16, 8, 128)
        wgb[16, :, 0:128] = bg2[0].reshape(8, 128)
        wgb[32:48, :, 128:256] = wg2[1].reshape(16, 8, 128)
        wgb[48, :, 128:256] = bg2[1].reshape(8, 128)
        wgb[64:80, :, 0:128] = wg2[ext_dir].reshape(16, 8, 128)
        wgb[80, :, 0:128] = bg2[ext_dir].reshape(8, 128)
        wgb = wgb.reshape(96, 2048)
        flags = np.zeros((128, 2), np.float32)
        flags[:, 0] = flag_f
        flags[:, 1] = flag_b
        m = dict(shared)
        m.update(xown=f(xown), xext=f(xext), wgin=wgin, wgb=wgb, flags=flags)
        return m

    maps = []
    for b in range(2):
        xa = np.concatenate([meta, x_prompt[b, 0:2048]], axis=0)
        ea = x_prompt[b, 2048:4096][::-1]
        maps.append(core(xa, ea, 1, 0.0, 1.0))
        xb = x_prompt[b, 2032:4096]
        eb = np.concatenate([meta, x_prompt[b, 0:2032]], axis=0)
        maps.append(core(xb, eb, 0, 1.0, 0.0))
    for s in range(4):
        xs = np.concatenate([meta, x_sample[s]], axis=0)
        maps.append(core(xs, np.zeros((E, D), np.float32), 0, 0.0, 0.0))
    return maps


_NC_CACHE = {}


def kernel(x_prompt, x_sample, meta_tokens, mix_norm_g, w_in, w_gate2, b_gate2, head_norm_g,
           conv_mix_w, w_out, ffn_norm_g, w_up, ffn_conv_w, ffn_conv_b, w_down, final_norm_g):
    maps = _prepare_inputs(x_prompt, x_sample, meta_tokens, mix_norm_g, w_in, w_gate2, b_gate2, head_norm_g,
                           conv_mix_w, w_out, ffn_norm_g, w_up, ffn_conv_w, ffn_conv_b, w_down, final_norm_g)
    if "nc" not in _NC_CACHE:
        _NC_CACHE["nc"] = build_program()[0]
    nc = _NC_CACHE["nc"]
    res = run_bass_kernel_spmd(nc, maps, core_ids=list(range(8)))
    ys = [np.asarray(r["y"], dtype=np.float32) for r in res.results]
    y_prompt = np.zeros((2, 4096, D), np.float32)
    y_sample = np.zeros((4, 2048, D), np.float32)
    for b in range(2):
        y_prompt[b, 0:2040] = ys[2 * b][16:16 + 2040]
        y_prompt[b, 2040:4096] = ys[2 * b + 1][8:2064]
    for s in range(4):
        y_sample[s] = ys[4 + s][16:2064]
    return (y_prompt, y_sample)
```

```python
import os
import numpy as np
from contextlib import ExitStack
import concourse.bass as bass
import concourse.mybir as mybir
from concourse.bass_utils import run_bass_kernel_spmd

F32 = mybir.dt.float32
BF16 = mybir.dt.bfloat16
AF = mybir.ActivationFunctionType
ALU = mybir.AluOpType

D = 4096
KC = 32
R = 2064
E = 2048
DFF = 11008
NF = 86
EPS = 1e-6
TB = [0, 413, 826, 1239, 1652, 2064]
ET = 256
CHUNKS = []
for _i in range(5):
    _s, _e = TB[_i], TB[_i + 1]
    _w = _e - _s
    _c0 = _w - 309
    CHUNKS.append((_s, _c0))
    for _k in range(3):
        CHUNKS.append((_s + _c0 + 103 * _k, 103))
FGROUPS = []
_f = 0
for _n in [11, 11, 11, 11, 11, 11, 10, 10]:
    FGROUPS.append((_f, _n))
    _f += _n
OFF_Q, OFF_K, OFF_V, OFF_R, OFF_CB, OFF_CC, OFF_CH = 0, 1024, 2048, 4096, 6176, 8224, 10272


class _Op:
    __slots__ = ("eng", "fn", "deps", "is_dma", "sem", "val", "signal")

    def __init__(self, eng, fn, is_dma=False):
        self.eng = eng
        self.fn = fn
        self.deps = []
        self.is_dma = is_dma
        self.sem = None
        self.val = 0
        self.signal = False


class Prog:
    ENGS = ("pe", "act", "dve", "pool", "sp")

    def __init__(self, nc, es):
        self.nc = nc
        self.es = es
        self.ops = {e: [] for e in self.ENGS}
        self.last_w = {}
        self.readers = {}
        self.eng_sem = {e: es.enter_context(nc.semaphore("prog_" + e)) for e in self.ENGS}
        self.dma_sem_count = {}
        self.pending_dmas = []

    def new_dma_sem(self, name):
        s = self.es.enter_context(self.nc.semaphore(name))
        self.dma_sem_count[id(s)] = 0
        return s

    def _collect(self, op, reads, writes):
        deps = []
        for k in reads:
            w = self.last_w.get(k)
            if w is not None:
                deps.append(w)
        for k in writes:
            w = self.last_w.get(k)
            if w is not None:
                deps.append(w)
            deps.extend(self.readers.get(k, ()))
        seen = set()
        for d in deps:
            if d is op or id(d) in seen:
                continue
            seen.add(id(d))
            if (not d.is_dma) and (not op.is_dma) and d.eng == "pe" and op.eng == "pe":
                continue
            op.deps.append(d)
            if not d.is_dma:
                d.signal = True
        for k in writes:
            self.last_w[k] = op
            self.readers[k] = []
        for k in reads:
            self.readers.setdefault(k, []).append(op)

    def op(self, eng, fn, reads=(), writes=()):
        pr = [k for k in reads if k.startswith("ps")]
        if pr:
            reads = [k for k in reads if not k.startswith("ps")]
            writes = list(writes) + [k for k in pr if k not in writes]
        o = _Op(eng, fn)
        self._collect(o, reads, writes)
        self.ops[eng].append(o)
        return o

    def dma(self, eng, fn, sem, reads=(), writes=()):
        o = _Op(eng, fn, is_dma=True)
        o.sem = sem
        self.dma_sem_count[id(sem)] += 16
        o.val = self.dma_sem_count[id(sem)]
        self._collect(o, reads, writes)
        self.ops[eng].append(o)
        self.pending_dmas.append(o)
        return o

    def barrier_wait(self, eng, deps):
        o = _Op(eng, None)
        for d in deps:
            o.deps.append(d)
            if not d.is_dma:
                d.signal = True
        self.ops[eng].append(o)
        return o

    def fence(self):
        lasts = []
        for e in self.ENGS:
            for o in reversed(self.ops[e]):
                if (not o.is_dma) and o.fn is not None:
                    lasts.append(o)
                    break
        deps = lasts + self.pending_dmas
        for e in self.ENGS:
            self.barrier_wait(e, deps)
        self.pending_dmas = []
        self.last_w = {}
        self.readers = {}

    def emit(self, block):
        for e in self.ENGS:
            c = 0
            for o in self.ops[e]:
                if o.is_dma:
                    continue
                if o.signal:
                    c += 1
                    o.sem = self.eng_sem[e]
                    o.val = c
        stats = {}

        def run(e, engine):
            known = {}
            nw = 0
            for o in self.ops[e]:
                need = {}
                for d in o.deps:
                    key = id(d.sem)
                    if need.get(key, (None, 0))[1] < d.val:
                        need[key] = (d.sem, d.val)
                for key, (sem, val) in need.items():
                    if known.get(key, 0) < val:
                        engine.wait_ge(sem, val)
                        known[key] = val
                        nw += 1
                if o.fn is None:
                    continue
                ins = o.fn(engine)
                if o.is_dma:
                    ins.then_inc(o.sem, 16)
                elif o.signal:
                    ins.then_inc(o.sem, 1)
            stats[e] = (len(self.ops[e]), nw)

        @block.tensor
        def _(t):
            run("pe", t)

        @block.scalar
        def _(s):
            run("act", s)

        @block.vector
        def _(v):
            run("dve", v)

        @block.gpsimd
        def _(g):
            run("pool", g)

        @block.sync
        def _(s):
            run("sp", s)
        return stats


def build_program(debug=False, stages=3, cut=99):
    nc = bass.Bass("TRN2", target_bir_lowering=False)
    din = lambda name, shape: nc.dram_tensor(name, shape, F32, kind="ExternalInput").ap()
    xown = din("xown", [R, D])
    xext = din("xext", [E, D])
    w_in = din("w_in", [D, 12320])
    w_out = din("w_out", [D, D])
    w_up = din("w_up", [D, 2 * DFF])
    w_down = din("w_down", [DFF, D])
    wgin = din("wgin", [D, 80])
    wgb = din("wgb", [96, 2048])
    gvec_d = din("gvec", [128, 96])
    ghead_d = din("ghead", [128, 2])
    cmw_d = din("cmw", [128, 48])
    fcw_d = din("fcw", [128, NF * 4])
    consts_d = din("consts", [128, 7 * 128])
    flags_d = din("flags", [128, 2])
    y = nc.dram_tensor("y", [R, D], F32, kind="ExternalOutput").ap()
    skind = "ExternalOutput" if debug else "Internal"
    hT_scr = nc.dram_tensor("hT_scr", [128, KC, R + E], BF16, kind=skind).ap()
    xT_scr = nc.dram_tensor("xT_scr", [128, KC, R], F32, kind=skind).ap()
    mT_scr = nc.dram_tensor("mT_scr", [128, KC, R], BF16, kind=skind).ap()

    wo_scr = nc.dram_tensor("wo_scr", [16, 128, KC, 256], BF16).ap()
    wu_scr = nc.dram_tensor("wu_scr", [NF, 128, KC, 256], BF16).ap()
    wd_scr = nc.dram_tensor("wd_scr", [8, 8, 128, 11, 512], BF16).ap()
    w_inv = w_in.rearrange("(kc p) c -> p kc c", p=128)
    w_outv = w_out.rearrange("(kc p) c -> p kc c", p=128)
    w_upv = w_up.rearrange("(kc p) c -> p kc c", p=128)
    w_downv = w_down.rearrange("(f p) c -> p f c", p=128)
    wginv = wgin.rearrange("(kc p) c -> p kc c", p=128)

    with ExitStack() as es:
        P = Prog(nc, es)
        sb = lambda ctx, name, shape, dt: ctx.enter_context(nc.sbuf_tensor(name, shape, dt))
        psA = [es.enter_context(nc.psum_tensor("psA%d" % i, [128, 512], F32)) for i in range(2)]
        psG = [es.enter_context(nc.psum_tensor("psG%d" % i, [128, 512], F32)) for i in range(2)]
        psS = es.enter_context(nc.psum_tensor("psS", [128, 512], F32))
        psO = es.enter_context(nc.psum_tensor("psO", [128, 512], F32))
        psT = es.enter_context(nc.psum_tensor("psT", [128, 1024], BF16))
        psX = es.enter_context(nc.psum_tensor("psX", [128, 512], F32))

        def MM(out, lhsT, rhs, start=True, stop=True, r=(), w=()):
            return P.op("pe", lambda e: e.matmul(out, lhsT=lhsT, rhs=rhs, start=start, stop=stop), r, w)

        def TR(out, in_, ident, r=(), w=()):
            return P.op("pe", lambda e: e.transpose(out=out, in_=in_, identity=ident), r, w)

        def ACT(out, in_, func, r=(), w=(), **kw):
            return P.op("act", lambda e: e.activation(out=out, in_=in_, func=func, **kw), r, w)

        def TT(eng, out, in0, in1, op, r=(), w=()):
            return P.op(eng, lambda e: e.tensor_tensor(out=out, in0=in0, in1=in1, op=op), r, w)

        def TS(eng, out, in0, s1, s2, op0, op1=None, r=(), w=()):
            if op1 is None:
                return P.op(eng, lambda e: e.tensor_scalar(out=out, in0=in0, scalar1=s1, scalar2=None, op0=op0), r, w)
            return P.op(eng, lambda e: e.tensor_scalar(out=out, in0=in0, scalar1=s1, scalar2=s2, op0=op0, op1=op1), r, w)

        def STT(eng, out, in0, scalar, in1, op0, op1, r=(), w=()):
            return P.op(eng, lambda e: e.scalar_tensor_tensor(out=out, in0=in0, scalar=scalar, in1=in1, op0=op0, op1=op1), r, w)

        def CP(eng, out, in_, r=(), w=()):
            return P.op(eng, lambda e: e.tensor_copy(out=out, in_=in_), r, w)

        def MSET(eng, ap, val, r=(), w=()):
            return P.op(eng, lambda e: e.memset(ap, val), r, w)

        def DMA(eng, out, in_, sem, r=(), w=()):
            return P.dma(eng, lambda e: e.dma_start(out=out, in_=in_), sem, r, w)

        conv_jobs = []
        for cb2 in range(16):
            conv_jobs.append((wo_scr[cb2], w_outv[:, :, cb2 * 256:(cb2 + 1) * 256]))
        for f in range(NF):
            conv_jobs.append((wu_scr[f, :, :, 0:128], w_upv[:, :, f * 128:(f + 1) * 128]))
            conv_jobs.append((wu_scr[f, :, :, 128:256], w_upv[:, :, DFF + f * 128:DFF + (f + 1) * 128]))
        for gi, (f0, nf) in enumerate(FGROUPS):
            for cb4 in range(8):
                conv_jobs.append((wd_scr[gi, cb4, :, 0:nf, :], w_downv[:, f0:f0 + nf, cb4 * 512:(cb4 + 1) * 512]))
        conv_sems = [P.new_dma_sem("s_cv%d" % i) for i in range(4)]
        conv_state = [0, 0.0]
        n_hooks = 33 + 8 * 13 + 16 * 5
        per_hook = len(conv_jobs) / float(n_hooks) + 0.01

        def conv_hook(flush=False):
            conv_state[1] += per_hook
            while conv_state[0] < len(conv_jobs) and (flush or conv_state[0] < conv_state[1]):
                o_ap, i_ap = conv_jobs[conv_state[0]]
                DMA("pool", o_ap, i_ap, conv_sems[conv_state[0] % 4], w=["cv%d" % (conv_state[0] % 4)])
                conv_state[0] += 1

        consts = sb(es, "consts_sb", [128, 7, 128], F32)
        identb = sb(es, "identb", [128, 128], BF16)
        onesb = sb(es, "onesb", [128, 128], BF16)
        gvec = sb(es, "gvecs", [128, 96], F32)
        ghead = sb(es, "gheads", [128, 2], F32)
        cmw = sb(es, "cmws", [128, 16, 3], F32)
        fcw = sb(es, "fcws", [128, NF, 4], F32)
        flags = sb(es, "flagss", [128, 2], F32)
        sc = [P.new_dma_sem("sc%d" % i) for i in range(6)]
        DMA("sp", consts[:].rearrange("p a b -> p (a b)"), consts_d[:, :], sc[0], w=["consts"])
        DMA("sp", gvec[:], gvec_d[:, :], sc[1], w=["gvec"])
        DMA("sp", ghead[:], ghead_d[:, :], sc[2], w=["ghead"])
        DMA("sp", cmw[:].rearrange("p a b -> p (a b)"), cmw_d[:, :], sc[3], w=["cmw"])
        DMA("sp", fcw[:].rearrange("p a b -> p (a b)"), fcw_d[:, :], sc[4], w=["fcw"])
        DMA("sp", flags[:], flags_d[:, :], sc[5], w=["flags"])
        CP("dve", identb[:], consts[:, 0, :], r=["consts"], w=["identb"])
        MSET("dve", onesb[:], 1.0 / 4096.0, w=["onesb"])
        ident = consts[:, 0, :]
        Mle, Mge, Mgt, Mlt = consts[:, 1, :], consts[:, 2, :], consts[:, 3, :], consts[:, 4, :]
        masks = consts[:, 5:7, :]

        with ExitStack() as sa:
            xin = [sb(sa, "xin%d" % i, [128, D], F32) for i in range(2)]
            xs = [sb(sa, "xs%d" % i, [128, D], F32) for i in range(2)]
            xTs = [sb(sa, "xTs%d" % i, [128, KC, 128], F32) for i in range(2)]
            hTs = [sb(sa, "hTs%d" % i, [128, KC, 128], BF16) for i in range(2)]
            junkA = sb(sa, "junkA", [128, D], BF16)
            ssA = [sb(sa, "ssA%d" % i, [128, 1], F32) for i in range(2)]
            rsA = [sb(sa, "rsA%d" % i, [128, 1], F32) for i in range(2)]
            s_xin = [P.new_dma_sem("s_xin%d" % i) for i in range(2)]
            s_xst = [P.new_dma_sem("s_xst%d" % i) for i in range(2)]
            s_hst = [P.new_dma_sem("s_hst%d" % i) for i in range(2)]
            blocks = [(xown, i * 128, 128, True, i * 128) for i in range(16)] + [(xown, 2048, 16, True, 2048)]
            blocks += [(xext, i * 128, 128, False, R + i * 128) for i in range(16)]
            for bi, (src, r0, nr, own, c0) in enumerate(blocks):
                sl = bi % 2
                conv_hook()
                DMA("sp", xin[sl][:nr, :], src[r0:r0 + nr, :], s_xin[sl], w=["xin%d" % sl])
                MSET("dve", ssA[sl][:], 0.0, w=["ssA%d" % sl])
                ACT(junkA[:nr, :], xin[sl][:nr, :], AF.Square, r=["xin%d" % sl, "ssA%d" % sl], w=["junkA", "ssA%d" % sl],
                    accum_out=ssA[sl][:nr, 0:1])
                ACT(rsA[sl][:nr, :], ssA[sl][:nr, :], AF.Ln, r=["ssA%d" % sl], w=["rsA%d" % sl], scale=1.0 / D, bias=EPS)
                ACT(rsA[sl][:nr, :], rsA[sl][:nr, :], AF.Exp, r=["rsA%d" % sl], w=["rsA%d" % sl], scale=-0.5)
                TS("dve", xs[sl][:nr, :], xin[sl][:nr, :], rsA[sl][:nr, 0:1], None, ALU.mult, r=["xin%d" % sl, "rsA%d" % sl], w=["xs%d" % sl])
                for g in range(8):
                    if own:
                        bank = psA[g % 2]
                        for j in range(4):
                            kc = g * 4 + j
                            TR(bank[:, j * 128:j * 128 + nr], xin[sl][:nr, kc * 128:(kc + 1) * 128], ident[:nr, :nr],
                               r=["xin%d" % sl, "consts"], w=["psA%d" % (g % 2)])
                        ACT(xTs[sl][:, g * 4:(g + 1) * 4, :nr], bank[:].rearrange("p (a b) -> p a b", a=4)[:, :, :nr], AF.Copy,
                            r=["psA%d" % (g % 2)], w=["xTs%d_%d" % (sl, g)])
                    bank2 = psG[g % 2]
                    for j in range(4):
                        kc = g * 4 + j
                        TR(bank2[:, j * 128:j * 128 + nr], xs[sl][:nr, kc * 128:(kc + 1) * 128], ident[:nr, :nr],
                           r=["xs%d" % sl, "consts"], w=["psG%d" % (g % 2)])
                    TT("dve", hTs[sl][:, g * 4:(g + 1) * 4, :nr], bank2[:].rearrange("p (a b) -> p a b", a=4)[:, :, :nr],
                       gvec[:, g * 4:(g + 1) * 4].unsqueeze(2).to_broadcast([128, 4, nr]), ALU.mult,
                       r=["psG%d" % (g % 2), "gvec"], w=["hTs%d_%d" % (sl, g)])
                hk = ["hTs%d_%d" % (sl, g) for g in range(8)]
                DMA("sp", hT_scr[:, :, c0:c0 + nr], hTs[sl][:, :, :nr], s_hst[sl], r=hk, w=["hT_scr"])
                if own:
                    xk = ["xTs%d_%d" % (sl, g) for g in range(8)]
                    DMA("sp", xT_scr[:, :, r0:r0 + nr], xTs[sl][:, :, :nr], s_xst[sl], r=xk, w=["xT_scr"])
        P.fence()

        if stages >= 2:
          with ExitStack() as sB:
            wq = sb(sB, "wq", [128, KC, 128], BF16)
            wk = sb(sB, "wk", [128, KC, 128], BF16)
            wv = sb(sB, "wv", [128, KC, 256], BF16)
            wr = sb(sB, "wr", [128, KC, 256], BF16)
            hts = [sb(sB, "hts%d" % i, [128, KC, 413], BF16) for i in range(2)]
            s_w = {k: P.new_dma_sem("s_w" + k) for k in ("q", "k", "v", "r")}
            s_ht = [P.new_dma_sem("s_ht%d" % i) for i in range(2)]
            s_mst = P.new_dma_sem("s_mst")
            ht_ctr = [0]

            def load_ht(c0, n):
                sl = ht_ctr[0] % 2
                ht_ctr[0] += 1
                conv_hook()
                DMA("sp", hts[sl][:, :, :n], hT_scr[:, :, c0:c0 + n], s_ht[sl], r=["hT_scr"], w=["hts%d" % sl])
                return hts[sl], "hts%d" % sl

            acc_ctr = [0]

            def proj(wt, wkey, wcols, ht, htkey, n, evac, M=128):
                i = acc_ctr[0] % 2
                acc_ctr[0] += 1
                acc = psA[i]
                for kc in range(KC):
                    MM(acc[:M, :n], wt[:, kc, wcols], ht[:, kc, :n], start=(kc == 0), stop=(kc == KC - 1),
                       r=[wkey, htkey], w=["psA%d" % i])
                evac(acc, "psA%d" % i)

            with ExitStack() as sg:
                wg_sb = sb(sg, "wg_sb", [128, KC, 80], BF16)
                qT = sb(sg, "qT", [128, R], BF16)
                kT = sb(sg, "kT", [128, R], BF16)
                rs = sb(sg, "rs", [128, 2, R], BF16)
                vt = sb(sg, "vt", [128, 2, 413], BF16)
                kte = sb(sg, "kte", [128, ET], BF16)
                kv = sb(sg, "kv", [128, 20, 384], BF16)
                kve = sb(sg, "kve", [128, 2, 384], BF16)
                Sbb = sb(sg, "Sbb", [128, 20, 256], BF16)
                aT = sb(sg, "aT", [96, R], F32)
                wgb_sb = sb(sg, "wgb_sb", [96, 2048], F32)
                mst = sb(sg, "mst", [128, 2, R], BF16)
                ex = sb(sg, "ex", [128, 256], F32)
                lsb = sb(sg, "lsb", [128, 256], F32)
                E1 = sb(sg, "E1", [128, 256], F32)
                E2 = sb(sg, "E2", [128, 256], F32)
                E3 = sb(sg, "E3", [128, 256], F32)
                qq = sb(sg, "qq", [128, 2, 128], BF16)
                qqp = [sb(sg, "qqp%d" % i, [128, 2, 128], BF16) for i in range(2)]
                khp = [sb(sg, "khp%d" % i, [128, 128], BF16) for i in range(2)]
                E1p = [sb(sg, "E1p%d" % i, [128, 256], F32) for i in range(2)]
                kk = sb(sg, "kk", [128, 2, 128], BF16)
                khf = sb(sg, "khf", [128, 128], BF16)
                khb = sb(sg, "khb", [128, 128], BF16)
                PT = sb(sg, "PT", [128, 2, 128], BF16)
                on = sb(sg, "on", [128, 256], BF16)
                junk = sb(sg, "junk", [128, 256], BF16)
                Sf = sb(sg, "Sf", [128, 256], F32)
                Sb_ = sb(sg, "Sb_", [128, 256], F32)
                Se = sb(sg, "Se", [128, 256], F32)
                Sfb = sb(sg, "Sfb", [128, 256], BF16)
                ss1 = sb(sg, "ss1", [128, 1], F32)
                rstd1 = sb(sg, "rstd1", [128, 1], F32)
                s_wg = P.new_dma_sem("s_wg")
                s_wgb = P.new_dma_sem("s_wgb")

                DMA("pool", wg_sb[:], wginv[:, :, :], s_wg, w=["wg_sb"])
                DMA("sp", wgb_sb[:], wgb[:, :], s_wgb, w=["wgb_sb"])
                MSET("dve", aT[:, :], 1.0, w=["aT"])

                for ti in range(5):
                    s0, e0 = TB[ti], TB[ti + 1]
                    n = e0 - s0
                    ht, hk = load_ht(s0, n)

                    def ev(acc, ak, s0=s0, n=n):
                        ACT(aT[0:16, s0:s0 + n], acc[0:16, :n], AF.Copy, r=[ak], w=["aT"])
                        ACT(aT[32:48, s0:s0 + n], acc[32:48, :n], AF.Copy, r=[ak], w=["aT"])
                    proj(wg_sb, "wg_sb", slice(0, 80), ht, hk, n, ev, M=80)
                for et in range(E // ET):
                    ht, hk = load_ht(R + et * ET, ET)

                    def ev(acc, ak, et=et):
                        ACT(aT[64:80, et * ET:(et + 1) * ET], acc[64:80, :ET], AF.Copy, r=[ak], w=["aT"])
                    proj(wg_sb, "wg_sb", slice(0, 80), ht, hk, ET, ev, M=80)

                def gates_decays(cols0, C, j, frow, do_b):
                    if do_b:
                        MM(psG[0][:C, 0:256], aT[0:64, cols0:cols0 + C], wgb_sb[0:64, j * 256:(j + 1) * 256],
                           r=["aT", "wgb_sb"], w=["psG0"])
                        W = 256
                    else:
                        MM(psG[0][:C, 0:128], aT[frow:frow + 32, cols0:cols0 + C], wgb_sb[frow:frow + 32, j * 256:j * 256 + 128],
                           r=["aT", "wgb_sb"], w=["psG0"])
                        W = 128
                    ACT(ex[:C, :W], psG[0][:C, :W], AF.Exp, r=["psG0"], w=["ex"], scale=-1.0)
                    ACT(lsb[:C, :W], ex[:C, :W], AF.Ln, r=["ex"], w=["lsb"], bias=1.0)
                    MM(psG[1][:, 0:C], lsb[:C, 0:128], Mle[:C, :C], r=["lsb", "consts"], w=["psG1"])
                    MM(psG[1][:C, 256:384], Mgt[:C, :C], lsb[:C, 0:128], r=["lsb", "consts"], w=["psG1"])
                    if do_b:
                        MM(psG[1][:, 128:128 + C], lsb[:C, 128:256], Mge[:C, :C], r=["lsb", "consts"], w=["psG1"])
                        MM(psG[1][:C, 384:512], Mlt[:C, :C], lsb[:C, 128:256], r=["lsb", "consts"], w=["psG1"])

                for j in range(8 if cut >= 6 else (1 if cut >= 2 else 0)):
                    DMA("pool", wq[:], w_inv[:, :, OFF_Q + j * 128:OFF_Q + (j + 1) * 128], s_w["q"], w=["wq"])
                    DMA("pool", wk[:], w_inv[:, :, OFF_K + j * 128:OFF_K + (j + 1) * 128], s_w["k"], w=["wk"])
                    DMA("pool", wv[:], w_inv[:, :, OFF_V + j * 256:OFF_V + (j + 1) * 256], s_w["v"], w=["wv"])
                    DMA("pool", wr[:], w_inv[:, :, OFF_R + j * 256:OFF_R + (j + 1) * 256], s_w["r"], w=["wr"])
                    MSET("dve", Se[:], 0.0, w=["Se"])
                    for et in range(E // ET):
                        ht, hk = load_ht(R + et * ET, ET)
                        proj(wk, "wk", slice(0, 128), ht, hk, ET,
                             lambda acc, ak: ACT(kte[:, :ET], acc[:, :ET], AF.Copy, r=[ak], w=["kte"]))
                        for h in range(2):
                            proj(wv, "wv", slice(h * 128, (h + 1) * 128), ht, hk, ET,
                                 lambda acc, ak, h=h: ACT(vt[:, h, :ET], acc[:, :ET], AF.Copy, r=[ak], w=["vt%d" % h]))
                        for ci in range(ET // 128):
                            lc = ci * 128
                            TR(psT[:, 0:128], kte[:, lc:lc + 128], identb[:, :], r=["kte", "identb"], w=["psT"])
                            for h in range(2):
                                TR(psT[:, 128 + h * 128:256 + h * 128], vt[:, h, lc:lc + 128], identb[:, :],
                                   r=["vt%d" % h, "identb"], w=["psT"])
                            CP("dve", kve[:, ci, :], psT[:, 0:384], r=["psT"], w=["kve%d" % ci])
                            ec = et * ET + lc
                            gates_decays(ec, 128, j, 64, False)
                            ACT(E1[:, 0:128], psG[1][:, 0:128], AF.Exp, r=["psG1"], w=["E1"])
                            ACT(E3[:, 0:128], psG[1][:, 256:384], AF.Exp, r=["psG1"], w=["E3"])
                            TT("dve", khf[:, :], kve[:, ci, 0:128], E3[:, 0:128], ALU.mult, r=["kve%d" % ci, "E3"], w=["khf"])
                            MM(psX[:, 0:256], khf[:, :], kve[:, ci, 128:384], r=["khf", "kve%d" % ci], w=["psX"])
                            STT("dve", Se[:], Se[:], E1[:, 127:128], psX[:, 0:256], ALU.mult, ALU.add,
                                r=["Se", "E1", "psX"], w=["Se"])
                    TS("dve", Sf[:], Se[:], flags[:, 0:1], None, ALU.mult, r=["Se", "flags"], w=["Sf"])
                    TS("dve", Sb_[:], Se[:], flags[:, 1:2], None, ALU.mult, r=["Se", "flags"], w=["Sb"])
                    for ti in range(5 if cut >= 3 else 0):
                        s0, e0 = TB[ti], TB[ti + 1]
                        n = e0 - s0
                        ht, hk = load_ht(s0, n)
                        proj(wq, "wq", slice(0, 128), ht, hk, n,
                             lambda acc, ak, s0=s0, n=n: ACT(qT[:, s0:s0 + n], acc[:, :n], AF.Copy, r=[ak], w=["qT"], scale=128.0 ** -0.5))
                        proj(wk, "wk", slice(0, 128), ht, hk, n,
                             lambda acc, ak, s0=s0, n=n: ACT(kT[:, s0:s0 + n], acc[:, :n], AF.Copy, r=[ak], w=["kT"]))
                        for h in range(2):
                            proj(wv, "wv", slice(h * 128, (h + 1) * 128), ht, hk, n,
                                 lambda acc, ak, h=h, n=n: ACT(vt[:, h, :n], acc[:, :n], AF.Copy, r=[ak], w=["vt%d" % h]))
                        for h in range(2):
                            proj(wr, "wr", slice(h * 128, (h + 1) * 128), ht, hk, n,
                                 lambda acc, ak, h=h, s0=s0, n=n: ACT(rs[:, h, s0:s0 + n], acc[:, :n], AF.Silu, r=[ak], w=["rs"]))
                        for c in range(ti * 4, ti * 4 + 4):
                            cs, C = CHUNKS[c]
                            lc = cs - s0
                            TR(psT[:C, 0:128], kT[:, cs:cs + C], identb[:, :], r=["kT", "identb"], w=["psT"])
                            for h in range(2):
                                TR(psT[:C, 128 + h * 128:256 + h * 128], vt[:, h, lc:lc + C], identb[:, :],
                                   r=["vt%d" % h, "identb"], w=["psT"])
                            CP("dve", kv[:C, c, :], psT[:C, 0:384], r=["psT"], w=["kv"])
                    def st_gate(c):
                        cs, C = CHUNKS[c]
                        MM(psG[0][:C, 0:256], aT[0:64, cs:cs + C], wgb_sb[0:64, j * 256:(j + 1) * 256],
                           r=["aT", "wgb_sb"], w=["psG0"])
                        ACT(ex[:C, :256], psG[0][:C, :256], AF.Exp, r=["psG0"], w=["ex"], scale=-1.0)
                        ACT(lsb[:C, :256], ex[:C, :256], AF.Ln, r=["ex"], w=["lsb"], bias=1.0)

                    def st_cum(c, fwd):
                        cs, C = CHUNKS[c]
                        p = c % 2
                        if fwd:
                            MM(psG[1][:, 0:C], lsb[:C, 0:128], Mle[:C, :C], r=["lsb", "consts"], w=["psG1"])
                            MM(psG[1][:C, 256:384], Mgt[:C, :C], lsb[:C, 0:128], r=["lsb", "consts"], w=["psG1"])
                        MM(psG[1][:, 128:128 + C], lsb[:C, 128:256], Mge[:C, :C], r=["lsb", "consts"], w=["psG1"])
                        MM(psG[1][:C, 384:512], Mlt[:C, :C], lsb[:C, 128:256], r=["lsb", "consts"], w=["psG1"])
                        if fwd:
                            ACT(E1p[p][:, 0:256], psG[1][:, 0:256], AF.Exp, r=["psG1"], w=["E1_%d" % p])
                            ACT(E2[:, 0:256], psG[1][:, 0:256], AF.Exp, r=["psG1"], w=["E2"], scale=-1.0)
                            ACT(E3[:C, 0:256], psG[1][:C, 256:512], AF.Exp, r=["psG1"], w=["E3"])
                            TT("dve", qqp[p][:, :, :C], qT[:, cs:cs + C].unsqueeze(1).to_broadcast([128, 2, C]),
                               E1p[p][:].rearrange("p (a b) -> p a b", a=2)[:, :, :C], ALU.mult, r=["qT", "E1_%d" % p], w=["qq%d" % p])
                            TT("dve", kk[:, :, :C], kT[:, cs:cs + C].unsqueeze(1).to_broadcast([128, 2, C]),
                               E2[:].rearrange("p (a b) -> p a b", a=2)[:, :, :C], ALU.mult, r=["kT", "E2"], w=["kk"])
                            TT("dve", khp[p][:C, :], kv[:C, c, 0:128], E3[:C, 0:128], ALU.mult, r=["kv", "E3"], w=["kh%d" % p])
                        else:
                            ACT(E1p[p][:, 128:256], psG[1][:, 128:256], AF.Exp, r=["psG1"], w=["E1_%d" % p])
                            ACT(E3[:C, 128:256], psG[1][:C, 384:512], AF.Exp, r=["psG1"], w=["E3"])
                            TT("dve", khp[p][:C, :], kv[:C, c, 0:128], E3[:C, 128:256], ALU.mult, r=["kv", "E3"], w=["kh%d" % p])

                    NCH = 20 if cut >= 4 else 0
                    order = list(reversed(range(NCH)))
                    for t in range(NCH + 2):
                        if 1 <= t <= NCH:
                            st_cum(order[t - 1], False)
                        if t < NCH:
                            st_gate(order[t])
                        if t >= 2:
                            c = order[t - 2]
                            cs, C = CHUNKS[c]
                            p = c % 2
                            ACT(Sbb[:, c, :], Sb_[:], AF.Copy, r=["Sb"], w=["Sbb"])
                            MM(psX[:, 0:256], khp[p][:C, :], kv[:C, c, 128:384], r=["kh%d" % p, "kv"], w=["psX"])
                            STT("dve", Sb_[:], Sb_[:], E1p[p][:, 128:129], psX[:, 0:256], ALU.mult, ALU.add,
                                r=["Sb", "E1_%d" % p, "psX"], w=["Sb"])
                    NCH = 20 if cut >= 5 else 0
                    for t in range(NCH + 4):
                        if 3 <= t < NCH + 3:
                            c = t - 3
                            cs, C = CHUNKS[c]
                            for h in range(2):
                                TR(psT[:, 512 + h * 128:512 + h * 128 + C], on[:C, h * 128:(h + 1) * 128], identb[:C, :C],
                                   r=["on", "identb"], w=["psT"])
                            for h in range(2):
                                STT("dve", mst[:, h, cs:cs + C], psT[:, 512 + h * 128:512 + h * 128 + C], ghead[:, h:h + 1],
                                    rs[:, h, cs:cs + C], ALU.mult, ALU.mult, r=["psT", "ghead", "rs"], w=["mst"])
                        if 2 <= t < NCH + 2:
                            c = t - 2
                            cs, C = CHUNKS[c]
                            p = c % 2
                            ACT(Sfb[:], Sf[:], AF.Copy, r=["Sf"], w=["Sfb"])
                            MM(psS[:C, 0:C], kk[:, 0, :C], qqp[p][:, 0, :C], r=["kk", "qq%d" % p], w=["psS"])
                            MM(psS[:C, 128:128 + C], kk[:, 1, :C], qqp[p][:, 1, :C], r=["kk", "qq%d" % p], w=["psS"])
                            TT("dve", PT[:C, :, :C], psS[:C, 0:256].rearrange("p (a b) -> p a b", a=2)[:, :, :C], masks[:C, :, :C],
                               ALU.mult, r=["psS", "consts"], w=["PT"])
                        if 1 <= t < NCH + 1:
                            st_cum(t - 1, True)
                        if t < NCH:
                            st_gate(t)
                        if 2 <= t < NCH + 2:
                            c = t - 2
                            cs, C = CHUNKS[c]
                            p = c % 2
                            MM(psO[:C, 0:256], PT[:C, 0, :C], kv[:C, c, 128:384], start=True, stop=False, r=["PT", "kv"], w=["psOo"])
                            MM(psO[:C, 0:256], PT[:C, 1, :C], kv[:C, c, 128:384], start=False, stop=False, r=["PT", "kv"], w=["psOo"])
                            MM(psO[:C, 0:256], qqp[p][:, 0, :C], Sfb[:, :], start=False, stop=False, r=["qq%d" % p, "Sfb"], w=["psOo"])
                            MM(psO[:C, 0:256], qqp[p][:, 1, :C], Sbb[:, c, :], start=False, stop=True, r=["qq%d" % p, "Sbb"], w=["psOo"])
                            MM(psX[:, 0:256], khp[p][:C, :], kv[:C, c, 128:384], r=["kh%d" % p, "kv"], w=["psX"])
                            STT("dve", Sf[:], Sf[:], E1p[p][:, C - 1:C], psX[:, 0:256], ALU.mult, ALU.add,
                                r=["Sf", "E1_%d" % p, "psX"], w=["Sf"])
                            MSET("dve", ss1[:], 0.0, w=["ss1"])
                            ACT(junk[:C, :], psO[:C, 0:256], AF.Square, r=["psOo", "ss1"], w=["junk", "ss1"], accum_out=ss1[:C, 0:1])
                            ACT(rstd1[:C, :], ss1[:C, :], AF.Ln, r=["ss1"], w=["rstd1"], scale=1.0 / 256, bias=EPS)
                            ACT(rstd1[:C, :], rstd1[:C, :], AF.Exp, r=["rstd1"], w=["rstd1"], scale=-0.5)
                            TS("dve", on[:C, :], psO[:C, 0:256], rstd1[:C, 0:1], None, ALU.mult, r=["psOo", "rstd1"], w=["on"])
                    DMA("sp", mT_scr[:, 2 * j:2 * j + 2, :], mst[:, :, :], s_mst, r=["mst"], w=["mT_scr"])
            P.fence()
            with ExitStack() as scv:
                ccs = sb(scv, "ccs", [128, 413], F32)
                prod = sb(scv, "prod", [128, R + 2], F32)
                cbs = sb(scv, "cbs", [128, R], F32)
                t1 = sb(scv, "t1", [128, R], F32)
                mcv = sb(scv, "mcv", [128, R], BF16)
                s_mcv = P.new_dma_sem("s_mcv")
                MSET("dve", prod[:, 0:1], 0.0, w=["prod"])
                MSET("dve", prod[:, R + 1:R + 2], 0.0, w=["prod"])
                for cg in range(16 if cut >= 7 else 0):
                    DMA("pool", wq[:], w_inv[:, :, OFF_CB + cg * 128:OFF_CB + (cg + 1) * 128], s_w["q"], w=["wq"])
                    DMA("pool", wk[:], w_inv[:, :, OFF_CC + cg * 128:OFF_CC + (cg + 1) * 128], s_w["k"], w=["wk"])
                    DMA("pool", wv[:, :, 0:128], w_inv[:, :, OFF_CH + cg * 128:OFF_CH + (cg + 1) * 128], s_w["v"], w=["wv"])
                    for ti in range(5):
                        s0, e0 = TB[ti], TB[ti + 1]
                        n = e0 - s0
                        ht, hk = load_ht(s0, n)
                        proj(wq, "wq", slice(0, 128), ht, hk, n,
                             lambda acc, ak, s0=s0, n=n: ACT(cbs[:, s0:s0 + n], acc[:, :n], AF.Copy, r=[ak], w=["cbs"]))
                        proj(wk, "wk", slice(0, 128), ht, hk, n,
                             lambda acc, ak, n=n: ACT(ccs[:, :n], acc[:, :n], AF.Copy, r=[ak], w=["ccs"]))
                        proj(wv, "wv", slice(0, 128), ht, hk, n,
                             lambda acc, ak, s0=s0, n=n: TT("dve", prod[:, 1 + s0:1 + s0 + n], acc[:, :n], ccs[:, :n], ALU.mult,
                                                            r=[ak, "ccs"], w=["prod"]))
                    TS("dve", t1[:, :], prod[:, 0:R], cmw[:, cg, 0:1], None, ALU.mult, r=["prod", "cmw"], w=["t1"])
                    STT("dve", t1[:, :], prod[:, 1:R + 1], cmw[:, cg, 1:2], t1[:, :], ALU.mult, ALU.add, r=["prod", "cmw", "t1"], w=["t1"])
                    STT("dve", t1[:, :], prod[:, 2:R + 2], cmw[:, cg, 2:3], t1[:, :], ALU.mult, ALU.add, r=["prod", "cmw", "t1"], w=["t1"])
                    TT("dve", mcv[:, :], t1[:, :], cbs[:, :], ALU.mult, r=["t1", "cbs"], w=["mcv"])
                    DMA("sp", mT_scr[:, 16 + cg, :], mcv[:, :], s_mcv, r=["mcv"], w=["mT_scr"])
          conv_hook(flush=True)
          P.fence()

        if stages >= 3:
          with ExitStack() as sC:
            X1 = sb(sC, "X1", [128, KC, 415], F32)
            MH = sb(sC, "MH", [128, KC, 415], BF16)
            aTt = sb(sC, "aTt", [128, 11, 413], BF16)
            wring = [sb(sC, "wring%d" % i, [128, KC, 256], BF16) for i in range(3)]
            wdring = [sb(sC, "wdring%d" % i, [128, 11, 512], BF16) for i in range(2)]
            sqt = [sb(sC, "sqt%d" % i, [128, 415], BF16) for i in range(2)]
            rstdt = sb(sC, "rstdt", [128, 415], F32)
            c1 = [sb(sC, "c1_%d" % i, [128, 413], F32) for i in range(2)]
            sg_ = [sb(sC, "sg%d" % i, [128, 413], F32) for i in range(2)]
            ost = [sb(sC, "ost%d" % i, [128, 2048], F32) for i in range(2)]
            s_wr = [[P.new_dma_sem("s_wr%d_%d" % (i, h)) for h in range(2)] for i in range(3)]
            s_wd = [P.new_dma_sem("s_wd%d" % i) for i in range(2)]
            s_x1 = P.new_dma_sem("s_x1")
            s_mh = P.new_dma_sem("s_mh")
            s_ost = [P.new_dma_sem("s_ost%d" % i) for i in range(2)]
            wr_ctr = [0]
            wd_ctr = [0]
            ost_ctr = [0]
            out_dmas = []
            for ti in range(5):
                s0, e0 = TB[ti], TB[ti + 1]
                lo, hi = max(s0 - 1, 0), min(e0 + 1, R)
                N = e0 - s0 + 2
                NV = N - 2
                off = lo - (s0 - 1)
                if ti == 0:
                    MSET("dve", X1[:, :, 0:1], 0.0, w=["X1"])
                    MSET("dve", MH[:, :, 0:1], 0.0, w=["MH"])
                if ti == 4:
                    MSET("dve", X1[:, :, N - 1:N], 0.0, w=["X1"])
                    MSET("dve", MH[:, :, N - 1:N], 0.0, w=["MH"])
                DMA("sp", MH[:, :, off:off + hi - lo], mT_scr[:, :, lo:hi], s_mh, r=["mT_scr"], w=["MH"])
                DMA("sp", X1[:, :, off:off + hi - lo], xT_scr[:, :, lo:hi], s_x1, r=["xT_scr"], w=["X1"])
                for cb2 in range(16):
                    sl = wr_ctr[0] % 3
                    wr_ctr[0] += 1
                    DMA("pool", wring[sl][:, :, :], wo_scr[cb2], s_wr[sl][0], w=["wring%d" % sl])
                    for h in range(2):
                        cb = cb2 * 2 + h
                        acc = psA[cb % 2]
                        for kc in range(KC):
                            MM(acc[:, :N], wring[sl][:, kc, h * 128:(h + 1) * 128], MH[:, kc, :N], start=(kc == 0), stop=(kc == KC - 1),
                               r=["wring%d" % sl, "MH"], w=["psA%d" % (cb % 2)])
                        TT("dve", X1[:, cb, :N], acc[:, :N], X1[:, cb, :N], ALU.add, r=["psA%d" % (cb % 2), "X1"], w=["X1c%d" % cb])
                        ACT(sqt[cb % 2][:, :N], X1[:, cb, :N], AF.Square, r=["X1c%d" % cb], w=["sqt%d" % (cb % 2)])
                        MM(psX[:, :N], onesb[:, :], sqt[cb % 2][:, :N], start=(cb == 0), stop=(cb == 31), r=["sqt%d" % (cb % 2), "onesb"], w=["psX"])
                xck = ["X1c%d" % cb for cb in range(32)]
                ACT(rstdt[:, :N], psX[:, :N], AF.Ln, r=["psX"], w=["rstdt"], bias=EPS)
                ACT(rstdt[:, :N], rstdt[:, :N], AF.Exp, r=["rstdt"], w=["rstdt"], scale=-0.5)
                for kc in range(KC):
                    STT("dve", MH[:, kc, :N], X1[:, kc, :N], gvec[:, 32 + kc:33 + kc], rstdt[:, :N], ALU.mult, ALU.mult,
                        r=["X1c%d" % kc, "gvec", "rstdt"], w=["MH"])
                for gi, (f0, nf) in enumerate(FGROUPS):
                    for fl in range(nf):
                        f = f0 + fl
                        sl = wr_ctr[0] % 3
                        wr_ctr[0] += 1
                        DMA("pool", wring[sl][:, :, :], wu_scr[f], s_wr[sl][0], w=["wring%d" % sl])
                        pu, pg = psA[f % 2], psG[f % 2]
                        for kc in range(KC):
                            MM(pu[:, :N], wring[sl][:, kc, 0:128], MH[:, kc, :N], start=(kc == 0), stop=(kc == KC - 1),
                               r=["wring%d" % sl, "MH"], w=["psA%d" % (f % 2)])
                        for kc in range(KC):
                            MM(pg[:, :N], wring[sl][:, kc, 128:256], MH[:, kc, :N], start=(kc == 0), stop=(kc == KC - 1),
                               r=["wring%d" % sl, "MH"], w=["psG%d" % (f % 2)])
                        cc1 = c1[f % 2]
                        TS("dve", cc1[:, :NV], pg[:, 0:NV], fcw[:, f, 0:1], fcw[:, f, 3:4], ALU.mult, ALU.add,
                           r=["psG%d" % (f % 2), "fcw"], w=["c1_%d" % (f % 2)])
                        STT("dve", cc1[:, :NV], pg[:, 1:NV + 1], fcw[:, f, 1:2], cc1[:, :NV], ALU.mult, ALU.add,
                            r=["psG%d" % (f % 2), "fcw", "c1_%d" % (f % 2)], w=["c1_%d" % (f % 2)])
                        STT("dve", cc1[:, :NV], pg[:, 2:NV + 2], fcw[:, f, 2:3], cc1[:, :NV], ALU.mult, ALU.add,
                            r=["psG%d" % (f % 2), "fcw", "c1_%d" % (f % 2)], w=["c1_%d" % (f % 2)])
                        ACT(sg_[f % 2][:, :NV], cc1[:, :NV], AF.Silu, r=["c1_%d" % (f % 2)], w=["sg%d" % (f % 2)])
                        TT("dve", aTt[:, fl, :NV], sg_[f % 2][:, :NV], pu[:, 1:NV + 1], ALU.mult,
                           r=["sg%d" % (f % 2), "psA%d" % (f % 2)], w=["aTt"])
                    for cb4 in range(8):
                        dl = wd_ctr[0] % 2
                        wd_ctr[0] += 1
                        DMA("pool", wdring[dl][:, :nf, :], wd_scr[gi, cb4, :, 0:nf, :], s_wd[dl], w=["wdring%d" % dl])
                        for h in range(4):
                            cb = cb4 * 4 + h
                            yps = psS if cb % 2 == 0 else psO
                            yk = "psS" if cb % 2 == 0 else "psO"
                            for fl in range(nf):
                                MM(yps[:, :NV], wdring[dl][:, fl, h * 128:(h + 1) * 128], aTt[:, fl, :NV], start=(fl == 0), stop=(fl == nf - 1),
                                   r=["wdring%d" % dl, "aTt"], w=[yk])
                            TT("dve", X1[:, cb, 1:NV + 1], yps[:, :NV], X1[:, cb, 1:NV + 1], ALU.add, r=[yk, "X1c%d" % cb], w=["X1c%d" % cb])
                for cb in range(32):
                    ACT(sqt[cb % 2][:, :NV], X1[:, cb, 1:NV + 1], AF.Square, r=["X1c%d" % cb], w=["sqt%d" % (cb % 2)])
                    MM(psX[:, :NV], onesb[:, :], sqt[cb % 2][:, :NV], start=(cb == 0), stop=(cb == 31), r=["sqt%d" % (cb % 2), "onesb"], w=["psX"])
                ACT(rstdt[:, :NV], psX[:, :NV], AF.Ln, r=["psX"], w=["rstdt"], bias=EPS)
                ACT(rstdt[:, :NV], rstdt[:, :NV], AF.Exp, r=["rstdt"], w=["rstdt"], scale=-0.5)
                for kc in range(KC):
                    STT("dve", X1[:, kc, 1:NV + 1], X1[:, kc, 1:NV + 1], gvec[:, 64 + kc:65 + kc], rstdt[:, :NV], ALU.mult, ALU.mult,
                        r=["X1c%d" % kc, "gvec", "rstdt"], w=["X1c%d" % kc])
                for c in range(ti * 4, ti * 4 + 4):
                    cs, C = CHUNKS[c]
                    lc = cs - s0 + 1
                    for half in range(2):
                        ol = ost_ctr[0] % 2
                        ost_ctr[0] += 1
                        for g4 in range(4):
                            bank = psG[g4 % 2]
                            for jj in range(4):
                                kc = half * 16 + g4 * 4 + jj
                                TR(bank[:C, jj * 128:(jj + 1) * 128], X1[:, kc, lc:lc + C], ident[:, :],
                                   r=["X1c%d" % kc, "consts"], w=["psG%d" % (g4 % 2)])
                            ACT(ost[ol][:C, g4 * 512:(g4 + 1) * 512], bank[:C, :], AF.Copy, r=["psG%d" % (g4 % 2)], w=["ost%d" % ol])
                        out_dmas.append(DMA("sp", y[cs:cs + C, half * 2048:(half + 1) * 2048], ost[ol][:C, :], s_ost[ol], r=["ost%d" % ol]))
                P.op("dve", lambda e: e.memset(rstdt[:, 0:1], 0.0), reads=xck + ["MH"], writes=["X1", "MH", "rstdt"] + xck)
          P.fence()
        else:
            P.fence()
        P.barrier_wait("sp", list(P.ops["sp"][-1].deps))
        block = es.enter_context(nc.Block())
        stats = P.emit(block)
    return nc, stats


def _consts():
    a = np.arange(128)[:, None]
    b = np.arange(128)[None, :]
    c = np.zeros((128, 7, 128), np.float32)
    c[:, 0] = (a == b)
    c[:, 1] = (a <= b) * (-1.0 / 16.0)
    c[:, 2] = (a >= b) * (-1.0 / 16.0)
    c[:, 3] = (a > b) * (-1.0 / 16.0)
    c[:, 4] = (a < b) * (-1.0 / 16.0)
    c[:, 5] = (a <= b)
    c[:, 6] = (a > b)
    return c.reshape(128, 7 * 128)


def _prepare_inputs(x_prompt, x_sample, meta_tokens, mix_norm_g, w_in, w_gate2, b_gate2, head_norm_g,
                    conv_mix_w, w_out, ffn_norm_g, w_up, ffn_conv_w, ffn_conv_b, w_down, final_norm_g):
    f = lambda a: np.ascontiguousarray(np.asarray(a, dtype=np.float32))
    x_prompt, x_sample, meta = f(x_prompt), f(x_sample), f(meta_tokens)
    w_in0, w_out0, w_up0, w_down0 = f(w_in[0]), f(w_out[0]), f(w_up[0]), f(w_down[0])
    wg2, bg2 = f(w_gate2[0]), f(b_gate2[0])
    gvec = np.concatenate([f(mix_norm_g[0]).reshape(32, 128).T, f(ffn_norm_g[0]).reshape(32, 128).T,
                           f(final_norm_g).reshape(32, 128).T], axis=1)
    ghead = f(head_norm_g[0]).reshape(2, 128).T
    cmw = f(conv_mix_w[0]).reshape(3, 16, 128).transpose(2, 1, 0).reshape(128, 48)
    fcw = np.concatenate([f(ffn_conv_w[0]), f(ffn_conv_b[0])[None]], axis=0).reshape(4, NF, 128).transpose(2, 1, 0).reshape(128, NF * 4)
    consts = _consts()
    shared = dict(w_in=w_in0, w_out=w_out0, w_up=w_up0, w_down=w_down0,
                  gvec=f(gvec), ghead=f(ghead), cmw=f(cmw), fcw=f(fcw), consts=consts)

    def core(xown, xext, ext_dir, flag_f, flag_b):
        wgin = np.zeros((D, 80), np.float32)
        wgin[:, 0:16] = w_in0[:, 6144:6160]
        wgin[:, 32:48] = w_in0[:, 6160:6176]
        wgin[:, 64:80] = w_in0[:, 6144 + 16 * ext_dir:6160 + 16 * ext_dir]
        wgb = np.zeros((96, 8, 256), np.float32)
        wgb[0:16, :, 0:128] = wg2[0].reshape(16, 8, 128)
        wgb[16, :, 0:128] = bg2[0].reshape(8, 128)
        wgb[32:48, :, 128:256] = wg2[1].reshape(16, 8, 128)
        wgb[48, :, 128:256] = bg2[1].reshape(8, 128)
        wgb[64:80, :, 0:128] = wg2[ext_dir].reshape(16, 8, 128)
        wgb[80, :, 0:128] = bg2[ext_dir].reshape(8, 128)
        wgb = wgb.reshape(96, 2048)
        flags = np.zeros((128, 2), np.float32)
        flags[:, 0] = flag_f
        flags[:, 1] = flag_b
        m = dict(shared)
        m.update(xown=f(xown), xext=f(xext), wgin=wgin, wgb=wgb, flags=flags)
        return m

    maps = []
    for b in range(2):
        xa = np.concatenate([meta, x_prompt[b, 0:2048]], axis=0)
        ea = x_prompt[b, 2048:4096][::-1]
        maps.append(core(xa, ea, 1, 0.0, 1.0))
        xb = x_prompt[b, 2032:4096]
        eb = np.concatenate([meta, x_prompt[b, 0:2032]], axis=0)
        maps.append(core(xb, eb, 0, 1.0, 0.0))
    for s in range(4):
        xs = np.concatenate([meta, x_sample[s]], axis=0)
        maps.append(core(xs, np.zeros((E, D), np.float32), 0, 0.0, 0.0))
    return maps


_NC_CACHE = {}


def kernel(x_prompt, x_sample, meta_tokens, mix_norm_g, w_in, w_gate2, b_gate2, head_norm_g,
           conv_mix_w, w_out, ffn_norm_g, w_up, ffn_conv_w, ffn_conv_b, w_down, final_norm_g):
    maps = _prepare_inputs(x_prompt, x_sample, meta_tokens, mix_norm_g, w_in, w_gate2, b_gate2, head_norm_g,
                           conv_mix_w, w_out, ffn_norm_g, w_up, ffn_conv_w, ffn_conv_b, w_down, final_norm_g)
    if "nc" not in _NC_CACHE:
        _NC_CACHE["nc"] = build_program()[0]
    nc = _NC_CACHE["nc"]
    res = run_bass_kernel_spmd(nc, maps, core_ids=list(range(8)))
    ys = [np.asarray(r["y"], dtype=np.float32) for r in res.results]
    y_prompt = np.zeros((2, 4096, D), np.float32)
    y_sample = np.zeros((4, 2048, D), np.float32)
    for b in range(2):
        y_prompt[b, 0:2040] = ys[2 * b][16:16 + 2040]
        y_prompt[b, 2040:4096] = ys[2 * b + 1][8:2064]
    for s in range(4):
        y_sample[s] = ys[4 + s][16:2064]
    return (y_prompt, y_sample)
```

```python
import os
import numpy as np
from contextlib import ExitStack
import concourse.bass as bass
import concourse.mybir as mybir
from concourse.bass_utils import run_bass_kernel_spmd

F32 = mybir.dt.float32
BF16 = mybir.dt.bfloat16
AF = mybir.ActivationFunctionType
ALU = mybir.AluOpType

D = 4096
KC = 32
R = 2064
E = 2048
DFF = 11008
NF = 86
EPS = 1e-6
TB = [0, 413, 826, 1239, 1652, 2064]
ET = 256
CHUNKS = []
for _i in range(5):
    _s, _e = TB[_i], TB[_i + 1]
    _w = _e - _s
    _c0 = _w - 309
    CHUNKS.append((_s, _c0))
    for _k in range(3):
        CHUNKS.append((_s + _c0 + 103 * _k, 103))
FGROUPS = []
_f = 0
for _n in [11, 11, 11, 11, 11, 11, 10, 10]:
    FGROUPS.append((_f, _n))
    _f += _n
OFF_Q, OFF_K, OFF_V, OFF_R, OFF_CB, OFF_CC, OFF_CH = 0, 1024, 2048, 4096, 6176, 8224, 10272


class _Op:
    __slots__ = ("eng", "fn", "deps", "is_dma", "sem", "val", "signal")

    def __init__(self, eng, fn, is_dma=False):
        self.eng = eng
        self.fn = fn
        self.deps = []
        self.is_dma = is_dma
        self.sem = None
        self.val = 0
        self.signal = False


class Prog:
    ENGS = ("pe", "act", "dve", "pool", "sp")

    def __init__(self, nc, es):
        self.nc = nc
        self.es = es
        self.ops = {e: [] for e in self.ENGS}
        self.last_w = {}
        self.readers = {}
        self.eng_sem = {e: es.enter_context(nc.semaphore("prog_" + e)) for e in self.ENGS}
        self.dma_sem_count = {}
        self.pending_dmas = []

    def new_dma_sem(self, name):
        s = self.es.enter_context(self.nc.semaphore(name))
        self.dma_sem_count[id(s)] = 0
        return s

    def _collect(self, op, reads, writes):
        deps = []
        for k in reads:
            w = self.last_w.get(k)
            if w is not None:
                deps.append(w)
        for k in writes:
            w = self.last_w.get(k)
            if w is not None:
                deps.append(w)
            deps.extend(self.readers.get(k, ()))
        seen = set()
        for d in deps:
            if d is op or id(d) in seen:
                continue
            seen.add(id(d))
            if (not d.is_dma) and (not op.is_dma) and d.eng == "pe" and op.eng == "pe":
                continue
            op.deps.append(d)
            if not d.is_dma:
                d.signal = True
        for k in writes:
            self.last_w[k] = op
            self.readers[k] = []
        for k in reads:
            self.readers.setdefault(k, []).append(op)

    def op(self, eng, fn, reads=(), writes=()):
        pr = [k for k in reads if k.startswith("ps")]
        if pr:
            reads = [k for k in reads if not k.startswith("ps")]
            writes = list(writes) + [k for k in pr if k not in writes]
        o = _Op(eng, fn)
        self._collect(o, reads, writes)
        self.ops[eng].append(o)
        return o

    def dma(self, eng, fn, sem, reads=(), writes=()):
        o = _Op(eng, fn, is_dma=True)
        o.sem = sem
        self.dma_sem_count[id(sem)] += 16
        o.val = self.dma_sem_count[id(sem)]
        self._collect(o, reads, writes)
        self.ops[eng].append(o)
        self.pending_dmas.append(o)
        return o

    def barrier_wait(self, eng, deps):
        o = _Op(eng, None)
        for d in deps:
            o.deps.append(d)
            if not d.is_dma:
                d.signal = True
        self.ops[eng].append(o)
        return o

    def fence(self):
        lasts = []
        for e in self.ENGS:
            for o in reversed(self.ops[e]):
                if (not o.is_dma) and o.fn is not None:
                    lasts.append(o)
                    break
        deps = lasts + self.pending_dmas
        for e in self.ENGS:
            self.barrier_wait(e, deps)
        self.pending_dmas = []
        self.last_w = {}
        self.readers = {}

    def emit(self, block):
        for e in self.ENGS:
            c = 0
            for o in self.ops[e]:
                if o.is_dma:
                    continue
                if o.signal:
                    c += 1
                    o.sem = self.eng_sem[e]
                    o.val = c
        stats = {}

        def run(e, engine):
            known = {}
            nw = 0
            for o in self.ops[e]:
                need = {}
                for d in o.deps:
                    key = id(d.sem)
                    if need.get(key, (None, 0))[1] < d.val:
                        need[key] = (d.sem, d.val)
                for key, (sem, val) in need.items():
                    if known.get(key, 0) < val:
                        engine.wait_ge(sem, val)
                        known[key] = val
                        nw += 1
                if o.fn is None:
                    continue
                ins = o.fn(engine)
                if o.is_dma:
                    ins.then_inc(o.sem, 16)
                elif o.signal:
                    ins.then_inc(o.sem, 1)
            stats[e] = (len(self.ops[e]), nw)

        @block.tensor
        def _(t):
            run("pe", t)

        @block.scalar
        def _(s):
            run("act", s)

        @block.vector
        def _(v):
            run("dve", v)

        @block.gpsimd
        def _(g):
            run("pool", g)

        @block.sync
        def _(s):
            run("sp", s)
        return stats


def build_program(debug=False, stages=3, cut=99):
    nc = bass.Bass("TRN2", target_bir_lowering=False)
    din = lambda name, shape: nc.dram_tensor(name, shape, F32, kind="ExternalInput").ap()
    xown = din("xown", [R, D])
    xext = din("xext", [E, D])
    w_in = din("w_in", [D, 12320])
    w_out = din("w_out", [D, D])
    w_up = din("w_up", [D, 2 * DFF])
    w_down = din("w_down", [DFF, D])
    wgin = din("wgin", [D, 80])
    wgb = din("wgb", [96, 2048])
    gvec_d = din("gvec", [128, 96])
    ghead_d = din("ghead", [128, 2])
    cmw_d = din("cmw", [128, 48])
    fcw_d = din("fcw", [128, NF * 4])
    consts_d = din("consts", [128, 7 * 128])
    flags_d = din("flags", [128, 2])
    y = nc.dram_tensor("y", [R, D], F32, kind="ExternalOutput").ap()
    skind = "ExternalOutput" if debug else "Internal"
    hT_scr = nc.dram_tensor("hT_scr", [128, KC, R + E], BF16, kind=skind).ap()
    xT_scr = nc.dram_tensor("xT_scr", [128, KC, R], F32, kind=skind).ap()
    mT_scr = nc.dram_tensor("mT_scr", [128, KC, R], BF16, kind=skind).ap()

    wo_scr = nc.dram_tensor("wo_scr", [16, 128, KC, 256], BF16).ap()
    wu_scr = nc.dram_tensor("wu_scr", [NF, 128, KC, 256], BF16).ap()
    wd_scr = nc.dram_tensor("wd_scr", [8, 8, 128, 11, 512], BF16).ap()
    w_inv = w_in.rearrange("(kc p) c -> p kc c", p=128)
    w_outv = w_out.rearrange("(kc p) c -> p kc c", p=128)
    w_upv = w_up.rearrange("(kc p) c -> p kc c", p=128)
    w_downv = w_down.rearrange("(f p) c -> p f c", p=128)
    wginv = wgin.rearrange("(kc p) c -> p kc c", p=128)

    with ExitStack() as es:
        P = Prog(nc, es)
        sb = lambda ctx, name, shape, dt: ctx.enter_context(nc.sbuf_tensor(name, shape, dt))
        psA = [es.enter_context(nc.psum_tensor("psA%d" % i, [128, 512], F32)) for i in range(2)]
        psG = [es.enter_context(nc.psum_tensor("psG%d" % i, [128, 512], F32)) for i in range(2)]
        psS = es.enter_context(nc.psum_tensor("psS", [128, 512], F32))
        psO = es.enter_context(nc.psum_tensor("psO", [128, 512], F32))
        psT = es.enter_context(nc.psum_tensor("psT", [128, 1024], BF16))
        psX = es.enter_context(nc.psum_tensor("psX", [128, 512], F32))

        def MM(out, lhsT, rhs, start=True, stop=True, r=(), w=()):
            return P.op("pe", lambda e: e.matmul(out, lhsT=lhsT, rhs=rhs, start=start, stop=stop), r, w)

        def TR(out, in_, ident, r=(), w=()):
            return P.op("pe", lambda e: e.transpose(out=out, in_=in_, identity=ident), r, w)

        def ACT(out, in_, func, r=(), w=(), **kw):
            return P.op("act", lambda e: e.activation(out=out, in_=in_, func=func, **kw), r, w)

        def TT(eng, out, in0, in1, op, r=(), w=()):
            return P.op(eng, lambda e: e.tensor_tensor(out=out, in0=in0, in1=in1, op=op), r, w)

        def TS(eng, out, in0, s1, s2, op0, op1=None, r=(), w=()):
            if op1 is None:
                return P.op(eng, lambda e: e.tensor_scalar(out=out, in0=in0, scalar1=s1, scalar2=None, op0=op0), r, w)
            return P.op(eng, lambda e: e.tensor_scalar(out=out, in0=in0, scalar1=s1, scalar2=s2, op0=op0, op1=op1), r, w)

        def STT(eng, out, in0, scalar, in1, op0, op1, r=(), w=()):
            return P.op(eng, lambda e: e.scalar_tensor_tensor(out=out, in0=in0, scalar=scalar, in1=in1, op0=op0, op1=op1), r, w)

        def CP(eng, out, in_, r=(), w=()):
            return P.op(eng, lambda e: e.tensor_copy(out=out, in_=in_), r, w)

        def MSET(eng, ap, val, r=(), w=()):
            return P.op(eng, lambda e: e.memset(ap, val), r, w)

        def DMA(eng, out, in_, sem, r=(), w=()):
            return P.dma(eng, lambda e: e.dma_start(out=out, in_=in_), sem, r, w)

        conv_jobs = []
        for cb2 in range(16):
            conv_jobs.append((wo_scr[cb2], w_outv[:, :, cb2 * 256:(cb2 + 1) * 256]))
        for f in range(NF):
            conv_jobs.append((wu_scr[f, :, :, 0:128], w_upv[:, :, f * 128:(f + 1) * 128]))
            conv_jobs.append((wu_scr[f, :, :, 128:256], w_upv[:, :, DFF + f * 128:DFF + (f + 1) * 128]))
        for gi, (f0, nf) in enumerate(FGROUPS):
            for cb4 in range(8):
                conv_jobs.append((wd_scr[gi, cb4, :, 0:nf, :], w_downv[:, f0:f0 + nf, cb4 * 512:(cb4 + 1) * 512]))
        conv_sems = [P.new_dma_sem("s_cv%d" % i) for i in range(4)]
        conv_state = [0, 0.0]
        n_hooks = 13 + 8 * 13 + 16 * 5
        per_hook = len(conv_jobs) / float(n_hooks) + 0.01

        def conv_hook(flush=False):
            conv_state[1] += per_hook
            while conv_state[0] < len(conv_jobs) and (flush or conv_state[0] < conv_state[1]):
                o_ap, i_ap = conv_jobs[conv_state[0]]
                DMA("pool", o_ap, i_ap, conv_sems[conv_state[0] % 4], w=["cv%d" % (conv_state[0] % 4)])
                conv_state[0] += 1

        consts = sb(es, "consts_sb", [128, 7, 128], F32)
        identb = sb(es, "identb", [128, 128], BF16)
        onesb = sb(es, "onesb", [128, 128], BF16)
        gvec = sb(es, "gvecs", [128, 96], F32)
        ghead = sb(es, "gheads", [128, 2], F32)
        cmw = sb(es, "cmws", [128, 16, 3], F32)
        fcw = sb(es, "fcws", [128, NF, 4], F32)
        flags = sb(es, "flagss", [128, 2], F32)
        sc = [P.new_dma_sem("sc%d" % i) for i in range(6)]
        DMA("sp", consts[:].rearrange("p a b -> p (a b)"), consts_d[:, :], sc[0], w=["consts"])
        DMA("sp", gvec[:], gvec_d[:, :], sc[1], w=["gvec"])
        DMA("sp", ghead[:], ghead_d[:, :], sc[2], w=["ghead"])
        DMA("sp", cmw[:].rearrange("p a b -> p (a b)"), cmw_d[:, :], sc[3], w=["cmw"])
        DMA("sp", fcw[:].rearrange("p a b -> p (a b)"), fcw_d[:, :], sc[4], w=["fcw"])
        DMA("sp", flags[:], flags_d[:, :], sc[5], w=["flags"])
        CP("dve", identb[:], consts[:, 0, :], r=["consts"], w=["identb"])
        MSET("dve", onesb[:], 1.0 / 4096.0, w=["onesb"])
        ident = consts[:, 0, :]
        Mle, Mge, Mgt, Mlt = consts[:, 1, :], consts[:, 2, :], consts[:, 3, :], consts[:, 4, :]
        masks = consts[:, 5:7, :]

        with ExitStack() as sa:
            xin = [sb(sa, "xin%d" % i, [128, D], F32) for i in range(2)]
            xs = [sb(sa, "xs%d" % i, [128, D], F32) for i in range(2)]
            xTs = [sb(sa, "xTs%d" % i, [128, KC, 128], F32) for i in range(2)]
            hTs = [sb(sa, "hTs%d" % i, [128, KC, 128], BF16) for i in range(2)]
            junkA = sb(sa, "junkA", [128, D], BF16)
            ssA = [sb(sa, "ssA%d" % i, [128, 1], F32) for i in range(2)]
            rsA = [sb(sa, "rsA%d" % i, [128, 1], F32) for i in range(2)]
            s_xin = [P.new_dma_sem("s_xin%d" % i) for i in range(2)]
            s_xst = [P.new_dma_sem("s_xst%d" % i) for i in range(2)]
            s_hst = [P.new_dma_sem("s_hst%d" % i) for i in range(2)]
            blocks = [(xown, i * 128, 128, True, i * 128) for i in range(16)] + [(xown, 2048, 16, True, 2048)]
            blocks += [(xext, i * 128, 128, False, R + i * 128) for i in range(16)]
            def load_blk(bi):
                src_, r0_, nr_, own_, c0_ = blocks[bi]
                DMA("sp", xin[bi % 2][:nr_, :], src_[r0_:r0_ + nr_, :], s_xin[bi % 2], w=["xin%d" % (bi % 2)])
            load_blk(0)
            for bi, (src, r0, nr, own, c0) in enumerate(blocks):
                sl = bi % 2
                MSET("dve", ssA[sl][:], 0.0, w=["ssA%d" % sl])
                ACT(junkA[:nr, :], xin[sl][:nr, :], AF.Square, r=["xin%d" % sl, "ssA%d" % sl], w=["junkA", "ssA%d" % sl],
                    accum_out=ssA[sl][:nr, 0:1])
                ACT(rsA[sl][:nr, :], ssA[sl][:nr, :], AF.Ln, r=["ssA%d" % sl], w=["rsA%d" % sl], scale=1.0 / D, bias=EPS)
                ACT(rsA[sl][:nr, :], rsA[sl][:nr, :], AF.Exp, r=["rsA%d" % sl], w=["rsA%d" % sl], scale=-0.5)
                TS("dve", xs[sl][:nr, :], xin[sl][:nr, :], rsA[sl][:nr, 0:1], None, ALU.mult, r=["xin%d" % sl, "rsA%d" % sl], w=["xs%d" % sl])
                for g in range(8):
                    if own:
                        bank = psA[g % 2]
                        for j in range(4):
                            kc = g * 4 + j
                            TR(bank[:, j * 128:j * 128 + nr], xin[sl][:nr, kc * 128:(kc + 1) * 128], ident[:nr, :nr],
                               r=["xin%d" % sl, "consts"], w=["psA%d" % (g % 2)])
                        ACT(xTs[sl][:, g * 4:(g + 1) * 4, :nr], bank[:].rearrange("p (a b) -> p a b", a=4)[:, :, :nr], AF.Copy,
                            r=["psA%d" % (g % 2)], w=["xTs%d_%d" % (sl, g)])
                    bank2 = psG[g % 2]
                    for j in range(4):
                        kc = g * 4 + j
                        TR(bank2[:, j * 128:j * 128 + nr], xs[sl][:nr, kc * 128:(kc + 1) * 128], ident[:nr, :nr],
                           r=["xs%d" % sl, "consts"], w=["psG%d" % (g % 2)])
                    TT("dve", hTs[sl][:, g * 4:(g + 1) * 4, :nr], bank2[:].rearrange("p (a b) -> p a b", a=4)[:, :, :nr],
                       gvec[:, g * 4:(g + 1) * 4].unsqueeze(2).to_broadcast([128, 4, nr]), ALU.mult,
                       r=["psG%d" % (g % 2), "gvec"], w=["hTs%d_%d" % (sl, g)])
                hk = ["hTs%d_%d" % (sl, g) for g in range(8)]
                if bi + 1 < len(blocks):
                    load_blk(bi + 1)
                DMA("pool", hT_scr[:, :, c0:c0 + nr], hTs[sl][:, :, :nr], s_hst[sl], r=hk, w=["hT_scr"])
                if own:
                    xk = ["xTs%d_%d" % (sl, g) for g in range(8)]
                    DMA("sp", xT_scr[:, :, r0:r0 + nr], xTs[sl][:, :, :nr], s_xst[sl], r=xk, w=["xT_scr"])
        P.fence()

        if stages >= 2:
          with ExitStack() as sB:
            wq = sb(sB, "wq", [128, KC, 128], BF16)
            wk = sb(sB, "wk", [128, KC, 128], BF16)
            wv = sb(sB, "wv", [128, KC, 256], BF16)
            wr = sb(sB, "wr", [128, KC, 256], BF16)
            hts = [sb(sB, "hts%d" % i, [128, KC, 413], BF16) for i in range(2)]
            s_w = {k: P.new_dma_sem("s_w" + k) for k in ("q", "k", "v", "r")}
            s_ht = [P.new_dma_sem("s_ht%d" % i) for i in range(2)]
            s_mst = P.new_dma_sem("s_mst")
            ht_ctr = [0]

            def load_ht(c0, n):
                sl = ht_ctr[0] % 2
                ht_ctr[0] += 1
                conv_hook()
                DMA("sp", hts[sl][:, :, :n], hT_scr[:, :, c0:c0 + n], s_ht[sl], r=["hT_scr"], w=["hts%d" % sl])
                return hts[sl], "hts%d" % sl

            acc_ctr = [0]

            def proj(wt, wkey, wcols, ht, htkey, n, evac, M=128):
                i = acc_ctr[0] % 2
                acc_ctr[0] += 1
                acc = psA[i]
                for kc in range(KC):
                    MM(acc[:M, :n], wt[:, kc, wcols], ht[:, kc, :n], start=(kc == 0), stop=(kc == KC - 1),
                       r=[wkey, htkey], w=["psA%d" % i])
                evac(acc, "psA%d" % i)

            with ExitStack() as sg:
                wg_sb = sb(sg, "wg_sb", [128, KC, 80], BF16)
                qT = sb(sg, "qT", [128, R], BF16)
                kT = sb(sg, "kT", [128, R], BF16)
                rs = sb(sg, "rs", [128, 2, R], BF16)
                vt = sb(sg, "vt", [128, 2, 413], BF16)
                kte = sb(sg, "kte", [128, ET], BF16)
                kv = sb(sg, "kv", [128, 20, 384], BF16)
                kve = sb(sg, "kve", [128, 2, 384], BF16)
                Sbb = sb(sg, "Sbb", [128, 20, 256], BF16)
                aT = sb(sg, "aT", [96, R], F32)
                wgb_sb = sb(sg, "wgb_sb", [96, 2048], F32)
                mst = sb(sg, "mst", [128, 2, R], BF16)
                ex = sb(sg, "ex", [128, 256], F32)
                lsb = sb(sg, "lsb", [128, 256], F32)
                E1 = sb(sg, "E1", [128, 256], F32)
                E2 = sb(sg, "E2", [128, 256], F32)
                E3 = sb(sg, "E3", [128, 256], F32)
                qq = sb(sg, "qq", [128, 2, 128], BF16)
                qqp = [sb(sg, "qqp%d" % i, [128, 2, 128], BF16) for i in range(2)]
                khp = [sb(sg, "khp%d" % i, [128, 128], BF16) for i in range(2)]
                E1p = [sb(sg, "E1p%d" % i, [128, 256], F32) for i in range(2)]
                kk = sb(sg, "kk", [128, 2, 128], BF16)
                khf = sb(sg, "khf", [128, 128], BF16)
                khb = sb(sg, "khb", [128, 128], BF16)
                PT = sb(sg, "PT", [128, 2, 128], BF16)
                on = sb(sg, "on", [128, 256], BF16)
                junk = sb(sg, "junk", [128, 256], BF16)
                Sf = sb(sg, "Sf", [128, 256], F32)
                Sb_ = sb(sg, "Sb_", [128, 256], F32)
                Se = sb(sg, "Se", [128, 256], F32)
                Sfb = sb(sg, "Sfb", [128, 256], BF16)
                ss1 = sb(sg, "ss1", [128, 1], F32)
                rstd1 = sb(sg, "rstd1", [128, 1], F32)
                s_wg = P.new_dma_sem("s_wg")
                s_wgb = P.new_dma_sem("s_wgb")

                DMA("pool", wg_sb[:], wginv[:, :, :], s_wg, w=["wg_sb"])
                DMA("sp", wgb_sb[:], wgb[:, :], s_wgb, w=["wgb_sb"])
                MSET("dve", aT[:, :], 1.0, w=["aT"])

                for ti in range(5):
                    s0, e0 = TB[ti], TB[ti + 1]
                    n = e0 - s0
                    ht, hk = load_ht(s0, n)

                    def ev(acc, ak, s0=s0, n=n):
                        ACT(aT[0:16, s0:s0 + n], acc[0:16, :n], AF.Copy, r=[ak], w=["aT"])
                        ACT(aT[32:48, s0:s0 + n], acc[32:48, :n], AF.Copy, r=[ak], w=["aT"])
                    proj(wg_sb, "wg_sb", slice(0, 80), ht, hk, n, ev, M=80)
                for et in range(E // ET):
                    ht, hk = load_ht(R + et * ET, ET)

                    def ev(acc, ak, et=et):
                        ACT(aT[64:80, et * ET:(et + 1) * ET], acc[64:80, :ET], AF.Copy, r=[ak], w=["aT"])
                    proj(wg_sb, "wg_sb", slice(0, 80), ht, hk, ET, ev, M=80)

                def gates_decays(cols0, C, j, frow, do_b):
                    if do_b:
                        MM(psG[0][:C, 0:256], aT[0:64, cols0:cols0 + C], wgb_sb[0:64, j * 256:(j + 1) * 256],
                           r=["aT", "wgb_sb"], w=["psG0"])
                        W = 256
                    else:
                        MM(psG[0][:C, 0:128], aT[frow:frow + 32, cols0:cols0 + C], wgb_sb[frow:frow + 32, j * 256:j * 256 + 128],
                           r=["aT", "wgb_sb"], w=["psG0"])
                        W = 128
                    ACT(ex[:C, :W], psG[0][:C, :W], AF.Exp, r=["psG0"], w=["ex"], scale=-1.0)
                    ACT(lsb[:C, :W], ex[:C, :W], AF.Ln, r=["ex"], w=["lsb"], bias=1.0)
                    MM(psG[1][:, 0:C], lsb[:C, 0:128], Mle[:C, :C], r=["lsb", "consts"], w=["psG1"])
                    MM(psG[1][:C, 256:384], Mgt[:C, :C], lsb[:C, 0:128], r=["lsb", "consts"], w=["psG1"])
                    if do_b:
                        MM(psG[1][:, 128:128 + C], lsb[:C, 128:256], Mge[:C, :C], r=["lsb", "consts"], w=["psG1"])
                        MM(psG[1][:C, 384:512], Mlt[:C, :C], lsb[:C, 128:256], r=["lsb", "consts"], w=["psG1"])

                for j in range(8 if cut >= 6 else (1 if cut >= 2 else 0)):
                    DMA("pool", wq[:], w_inv[:, :, OFF_Q + j * 128:OFF_Q + (j + 1) * 128], s_w["q"], w=["wq"])
                    DMA("pool", wk[:], w_inv[:, :, OFF_K + j * 128:OFF_K + (j + 1) * 128], s_w["k"], w=["wk"])
                    DMA("pool", wv[:], w_inv[:, :, OFF_V + j * 256:OFF_V + (j + 1) * 256], s_w["v"], w=["wv"])
                    DMA("pool", wr[:], w_inv[:, :, OFF_R + j * 256:OFF_R + (j + 1) * 256], s_w["r"], w=["wr"])
                    MSET("dve", Se[:], 0.0, w=["Se"])
                    for et in range(E // ET):
                        ht, hk = load_ht(R + et * ET, ET)
                        proj(wk, "wk", slice(0, 128), ht, hk, ET,
                             lambda acc, ak: ACT(kte[:, :ET], acc[:, :ET], AF.Copy, r=[ak], w=["kte"]))
                        for h in range(2):
                            proj(wv, "wv", slice(h * 128, (h + 1) * 128), ht, hk, ET,
                                 lambda acc, ak, h=h: ACT(vt[:, h, :ET], acc[:, :ET], AF.Copy, r=[ak], w=["vt%d" % h]))
                        for ci in range(ET // 128):
                            lc = ci * 128
                            TR(psT[:, 0:128], kte[:, lc:lc + 128], identb[:, :], r=["kte", "identb"], w=["psT"])
                            for h in range(2):
                                TR(psT[:, 128 + h * 128:256 + h * 128], vt[:, h, lc:lc + 128], identb[:, :],
                                   r=["vt%d" % h, "identb"], w=["psT"])
                            CP("dve", kve[:, ci, :], psT[:, 0:384], r=["psT"], w=["kve%d" % ci])
                            ec = et * ET + lc
                            gates_decays(ec, 128, j, 64, False)
                            ACT(E1[:, 0:128], psG[1][:, 0:128], AF.Exp, r=["psG1"], w=["E1"])
                            ACT(E3[:, 0:128], psG[1][:, 256:384], AF.Exp, r=["psG1"], w=["E3"])
                            TT("dve", khf[:, :], kve[:, ci, 0:128], E3[:, 0:128], ALU.mult, r=["kve%d" % ci, "E3"], w=["khf"])
                            MM(psX[:, 0:256], khf[:, :], kve[:, ci, 128:384], r=["khf", "kve%d" % ci], w=["psX"])
                            STT("dve", Se[:], Se[:], E1[:, 127:128], psX[:, 0:256], ALU.mult, ALU.add,
                                r=["Se", "E1", "psX"], w=["Se"])
                    TS("dve", Sf[:], Se[:], flags[:, 0:1], None, ALU.mult, r=["Se", "flags"], w=["Sf"])
                    TS("dve", Sb_[:], Se[:], flags[:, 1:2], None, ALU.mult, r=["Se", "flags"], w=["Sb"])
                    for ti in range(5 if cut >= 3 else 0):
                        s0, e0 = TB[ti], TB[ti + 1]
                        n = e0 - s0
                        ht, hk = load_ht(s0, n)
                        proj(wq, "wq", slice(0, 128), ht, hk, n,
                             lambda acc, ak, s0=s0, n=n: ACT(qT[:, s0:s0 + n], acc[:, :n], AF.Copy, r=[ak], w=["qT"], scale=128.0 ** -0.5))
                        proj(wk, "wk", slice(0, 128), ht, hk, n,
                             lambda acc, ak, s0=s0, n=n: ACT(kT[:, s0:s0 + n], acc[:, :n], AF.Copy, r=[ak], w=["kT"]))
                        for h in range(2):
                            proj(wv, "wv", slice(h * 128, (h + 1) * 128), ht, hk, n,
                                 lambda acc, ak, h=h, n=n: ACT(vt[:, h, :n], acc[:, :n], AF.Copy, r=[ak], w=["vt%d" % h]))
                        for h in range(2):
                            proj(wr, "wr", slice(h * 128, (h + 1) * 128), ht, hk, n,
                                 lambda acc, ak, h=h, s0=s0, n=n: ACT(rs[:, h, s0:s0 + n], acc[:, :n], AF.Silu, r=[ak], w=["rs"]))
                        for c in range(ti * 4, ti * 4 + 4):
                            cs, C = CHUNKS[c]
                            lc = cs - s0
                            TR(psT[:C, 0:128], kT[:, cs:cs + C], identb[:, :], r=["kT", "identb"], w=["psT"])
                            for h in range(2):
                                TR(psT[:C, 128 + h * 128:256 + h * 128], vt[:, h, lc:lc + C], identb[:, :],
                                   r=["vt%d" % h, "identb"], w=["psT"])
                            CP("dve", kv[:C, c, :], psT[:C, 0:384], r=["psT"], w=["kv"])
                    def st_gate(c):
                        cs, C = CHUNKS[c]
                        MM(psG[0][:C, 0:256], aT[0:64, cs:cs + C], wgb_sb[0:64, j * 256:(j + 1) * 256],
                           r=["aT", "wgb_sb"], w=["psG0"])
                        ACT(ex[:C, :256], psG[0][:C, :256], AF.Exp, r=["psG0"], w=["ex"], scale=-1.0)
                        ACT(lsb[:C, :256], ex[:C, :256], AF.Ln, r=["ex"], w=["lsb"], bias=1.0)

                    def st_cum(c, fwd):
                        cs, C = CHUNKS[c]
                        p = c % 2
                        if fwd:
                            MM(psG[1][:, 0:C], lsb[:C, 0:128], Mle[:C, :C], r=["lsb", "consts"], w=["psG1"])
                            MM(psG[1][:C, 256:384], Mgt[:C, :C], lsb[:C, 0:128], r=["lsb", "consts"], w=["psG1"])
                        MM(psG[1][:, 128:128 + C], lsb[:C, 128:256], Mge[:C, :C], r=["lsb", "consts"], w=["psG1"])
                        MM(psG[1][:C, 384:512], Mlt[:C, :C], lsb[:C, 128:256], r=["lsb", "consts"], w=["psG1"])
                        if fwd:
                            ACT(E1p[p][:, 0:256], psG[1][:, 0:256], AF.Exp, r=["psG1"], w=["E1_%d" % p])
                            ACT(E2[:, 0:256], psG[1][:, 0:256], AF.Exp, r=["psG1"], w=["E2"], scale=-1.0)
                            ACT(E3[:C, 0:256], psG[1][:C, 256:512], AF.Exp, r=["psG1"], w=["E3"])
                            TT("dve", qqp[p][:, :, :C], qT[:, cs:cs + C].unsqueeze(1).to_broadcast([128, 2, C]),
                               E1p[p][:].rearrange("p (a b) -> p a b", a=2)[:, :, :C], ALU.mult, r=["qT", "E1_%d" % p], w=["qq%d" % p])
                            TT("dve", kk[:, :, :C], kT[:, cs:cs + C].unsqueeze(1).to_broadcast([128, 2, C]),
                               E2[:].rearrange("p (a b) -> p a b", a=2)[:, :, :C], ALU.mult, r=["kT", "E2"], w=["kk"])
                            TT("dve", khp[p][:C, :], kv[:C, c, 0:128], E3[:C, 0:128], ALU.mult, r=["kv", "E3"], w=["kh%d" % p])
                        else:
                            ACT(E1p[p][:, 128:256], psG[1][:, 128:256], AF.Exp, r=["psG1"], w=["E1_%d" % p])
                            ACT(E3[:C, 128:256], psG[1][:C, 384:512], AF.Exp, r=["psG1"], w=["E3"])
                            TT("dve", khp[p][:C, :], kv[:C, c, 0:128], E3[:C, 128:256], ALU.mult, r=["kv", "E3"], w=["kh%d" % p])

                    NCH = 20 if cut >= 4 else 0
                    order = list(reversed(range(NCH)))
                    for t in range(NCH + 2):
                        if 1 <= t <= NCH:
                            st_cum(order[t - 1], False)
                        if t < NCH:
                            st_gate(order[t])
                        if t >= 2:
                            c = order[t - 2]
                            cs, C = CHUNKS[c]
                            p = c % 2
                            ACT(Sbb[:, c, :], Sb_[:], AF.Copy, r=["Sb"], w=["Sbb"])
                            MM(psX[:, 0:256], khp[p][:C, :], kv[:C, c, 128:384], r=["kh%d" % p, "kv"], w=["psX"])
                            STT("dve", Sb_[:], Sb_[:], E1p[p][:, 128:129], psX[:, 0:256], ALU.mult, ALU.add,
                                r=["Sb", "E1_%d" % p, "psX"], w=["Sb"])
                    NCH = 20 if cut >= 5 else 0
                    for t in range(NCH + 4):
                        if 3 <= t < NCH + 3:
                            c = t - 3
                            cs, C = CHUNKS[c]
                            for h in range(2):
                                TR(psT[:, 512 + h * 128:512 + h * 128 + C], on[:C, h * 128:(h + 1) * 128], identb[:C, :C],
                                   r=["on", "identb"], w=["psT"])
                            for h in range(2):
                                STT("dve", mst[:, h, cs:cs + C], psT[:, 512 + h * 128:512 + h * 128 + C], ghead[:, h:h + 1],
                                    rs[:, h, cs:cs + C], ALU.mult, ALU.mult, r=["psT", "ghead", "rs"], w=["mst"])
                        if 2 <= t < NCH + 2:
                            c = t - 2
                            cs, C = CHUNKS[c]
                            p = c % 2
                            ACT(Sfb[:], Sf[:], AF.Copy, r=["Sf"], w=["Sfb"])
                            MM(psS[:C, 0:C], kk[:, 0, :C], qqp[p][:, 0, :C], r=["kk", "qq%d" % p], w=["psS"])
                            MM(psS[:C, 128:128 + C], kk[:, 1, :C], qqp[p][:, 1, :C], r=["kk", "qq%d" % p], w=["psS"])
                            TT("dve", PT[:C, :, :C], psS[:C, 0:256].rearrange("p (a b) -> p a b", a=2)[:, :, :C], masks[:C, :, :C],
                               ALU.mult, r=["psS", "consts"], w=["PT"])
                        if 1 <= t < NCH + 1:
                            st_cum(t - 1, True)
                        if t < NCH:
                            st_gate(t)
                        if 2 <= t < NCH + 2:
                            c = t - 2
                            cs, C = CHUNKS[c]
                            p = c % 2
                            MM(psO[:C, 0:256], PT[:C, 0, :C], kv[:C, c, 128:384], start=True, stop=False, r=["PT", "kv"], w=["psOo"])
                            MM(psO[:C, 0:256], PT[:C, 1, :C], kv[:C, c, 128:384], start=False, stop=False, r=["PT", "kv"], w=["psOo"])
                            MM(psO[:C, 0:256], qqp[p][:, 0, :C], Sfb[:, :], start=False, stop=False, r=["qq%d" % p, "Sfb"], w=["psOo"])
                            MM(psO[:C, 0:256], qqp[p][:, 1, :C], Sbb[:, c, :], start=False, stop=True, r=["qq%d" % p, "Sbb"], w=["psOo"])
                            MM(psX[:, 0:256], khp[p][:C, :], kv[:C, c, 128:384], r=["kh%d" % p, "kv"], w=["psX"])
                            STT("dve", Sf[:], Sf[:], E1p[p][:, C - 1:C], psX[:, 0:256], ALU.mult, ALU.add,
                                r=["Sf", "E1_%d" % p, "psX"], w=["Sf"])
                            MSET("dve", ss1[:], 0.0, w=["ss1"])
                            ACT(junk[:C, :], psO[:C, 0:256], AF.Square, r=["psOo", "ss1"], w=["junk", "ss1"], accum_out=ss1[:C, 0:1])
                            ACT(rstd1[:C, :], ss1[:C, :], AF.Ln, r=["ss1"], w=["rstd1"], scale=1.0 / 256, bias=EPS)
                            ACT(rstd1[:C, :], rstd1[:C, :], AF.Exp, r=["rstd1"], w=["rstd1"], scale=-0.5)
                            TS("dve", on[:C, :], psO[:C, 0:256], rstd1[:C, 0:1], None, ALU.mult, r=["psOo", "rstd1"], w=["on"])
                    DMA("sp", mT_scr[:, 2 * j:2 * j + 2, :], mst[:, :, :], s_mst, r=["mst"], w=["mT_scr"])
            P.fence()
            with ExitStack() as scv:
                ccs = sb(scv, "ccs", [128, 413], F32)
                prod = sb(scv, "prod", [128, R + 2], F32)
                cbs = sb(scv, "cbs", [128, R], F32)
                t1 = sb(scv, "t1", [128, R], F32)
                mcv = sb(scv, "mcv", [128, R], BF16)
                s_mcv = P.new_dma_sem("s_mcv")
                MSET("dve", prod[:, 0:1], 0.0, w=["prod"])
                MSET("dve", prod[:, R + 1:R + 2], 0.0, w=["prod"])
                s_w2 = {k: P.new_dma_sem("s_w2" + k) for k in ("a", "b", "c")}
                NCG = 16 if cut >= 7 else 0
                cwsets = [((wq, slice(0, 128), "wq", s_w["q"]), (wk, slice(0, 128), "wk", s_w["k"]), (wv, slice(0, 128), "wv", s_w["v"])),
                          ((wr, slice(0, 128), "wr_a", s_w2["a"]), (wr, slice(128, 256), "wr_b", s_w2["b"]), (wv, slice(128, 256), "wv_b", s_w2["c"]))]

                def load_conv_w(cg):
                    for (wt_, sl_, key_, sem_), off_ in zip(cwsets[cg % 2], (OFF_CB, OFF_CC, OFF_CH)):
                        DMA("pool", wt_[:, :, sl_], w_inv[:, :, off_ + cg * 128:off_ + (cg + 1) * 128], sem_, w=[key_])
                if NCG:
                    load_conv_w(0)
                for cg in range(NCG):
                    if cg + 1 < NCG:
                        load_conv_w(cg + 1)
                    (wA, slA, kA, _), (wB, slB, kB, _), (wC, slC, kC, _) = cwsets[cg % 2]
                    for ti in range(5):
                        s0, e0 = TB[ti], TB[ti + 1]
                        n = e0 - s0
                        ht, hk = load_ht(s0, n)
                        proj(wA, kA, slA, ht, hk, n,
                             lambda acc, ak, s0=s0, n=n: ACT(cbs[:, s0:s0 + n], acc[:, :n], AF.Copy, r=[ak], w=["cbs"]))
                        proj(wB, kB, slB, ht, hk, n,
                             lambda acc, ak, n=n: ACT(ccs[:, :n], acc[:, :n], AF.Copy, r=[ak], w=["ccs"]))
                        proj(wC, kC, slC, ht, hk, n,
                             lambda acc, ak, s0=s0, n=n: TT("dve", prod[:, 1 + s0:1 + s0 + n], acc[:, :n], ccs[:, :n], ALU.mult,
                                                            r=[ak, "ccs"], w=["prod"]))
                    TS("dve", t1[:, :], prod[:, 0:R], cmw[:, cg, 0:1], None, ALU.mult, r=["prod", "cmw"], w=["t1"])
                    STT("dve", t1[:, :], prod[:, 1:R + 1], cmw[:, cg, 1:2], t1[:, :], ALU.mult, ALU.add, r=["prod", "cmw", "t1"], w=["t1"])
                    STT("dve", t1[:, :], prod[:, 2:R + 2], cmw[:, cg, 2:3], t1[:, :], ALU.mult, ALU.add, r=["prod", "cmw", "t1"], w=["t1"])
                    TT("dve", mcv[:, :], t1[:, :], cbs[:, :], ALU.mult, r=["t1", "cbs"], w=["mcv"])
                    DMA("sp", mT_scr[:, 16 + cg, :], mcv[:, :], s_mcv, r=["mcv"], w=["mT_scr"])
          conv_hook(flush=True)
          P.fence()

        if stages >= 3:
          with ExitStack() as sC:
            X1 = sb(sC, "X1", [128, KC, 415], F32)
            MH = sb(sC, "MH", [128, KC, 415], BF16)
            aTt = sb(sC, "aTt", [128, 11, 413], BF16)
            wring = [sb(sC, "wring%d" % i, [128, KC, 256], BF16) for i in range(3)]
            wdring = [sb(sC, "wdring%d" % i, [128, 11, 512], BF16) for i in range(2)]
            sqt = [sb(sC, "sqt%d" % i, [128, 415], BF16) for i in range(2)]
            rstdt = sb(sC, "rstdt", [128, 415], F32)
            c1 = [sb(sC, "c1_%d" % i, [128, 413], F32) for i in range(2)]
            sg_ = [sb(sC, "sg%d" % i, [128, 413], F32) for i in range(2)]
            ost = [sb(sC, "ost%d" % i, [128, 2048], F32) for i in range(2)]
            s_wr = [[P.new_dma_sem("s_wr%d_%d" % (i, h)) for h in range(2)] for i in range(3)]
            s_wd = [P.new_dma_sem("s_wd%d" % i) for i in range(2)]
            s_x1 = P.new_dma_sem("s_x1")
            s_mh = P.new_dma_sem("s_mh")
            s_ost = [P.new_dma_sem("s_ost%d" % i) for i in range(2)]
            wr_ctr = [0]
            wd_ctr = [0]
            ost_ctr = [0]
            out_dmas = []
            for ti in range(5):
                s0, e0 = TB[ti], TB[ti + 1]
                lo, hi = max(s0 - 1, 0), min(e0 + 1, R)
                N = e0 - s0 + 2
                NV = N - 2
                off = lo - (s0 - 1)
                if ti == 0:
                    MSET("dve", X1[:, :, 0:1], 0.0, w=["X1"])
                    MSET("dve", MH[:, :, 0:1], 0.0, w=["MH"])
                if ti == 4:
                    MSET("dve", X1[:, :, N - 1:N], 0.0, w=["X1"])
                    MSET("dve", MH[:, :, N - 1:N], 0.0, w=["MH"])
                DMA("sp", MH[:, :, off:off + hi - lo], mT_scr[:, :, lo:hi], s_mh, r=["mT_scr"], w=["MH"])
                DMA("sp", X1[:, :, off:off + hi - lo], xT_scr[:, :, lo:hi], s_x1, r=["xT_scr"], w=["X1"])
                for cb2 in range(16):
                    sl = wr_ctr[0] % 3
                    wr_ctr[0] += 1
                    DMA("pool", wring[sl][:, :, :], wo_scr[cb2], s_wr[sl][0], w=["wring%d" % sl])
                    for h in range(2):
                        cb = cb2 * 2 + h
                        acc = psA[cb % 2]
                        for kc in range(KC):
                            MM(acc[:, :N], wring[sl][:, kc, h * 128:(h + 1) * 128], MH[:, kc, :N], start=(kc == 0), stop=(kc == KC - 1),
                               r=["wring%d" % sl, "MH"], w=["psA%d" % (cb % 2)])
                        if cb >= 1:
                            pcb = cb - 1
                            MM(psX[:, :N], onesb[:, :], sqt[pcb % 2][:, :N], start=(pcb == 0), stop=False, r=["sqt%d" % (pcb % 2), "onesb"], w=["psX"])
                        TT("dve", X1[:, cb, :N], acc[:, :N], X1[:, cb, :N], ALU.add, r=["psA%d" % (cb % 2), "X1"], w=["X1c%d" % cb])
                        ACT(sqt[cb % 2][:, :N], X1[:, cb, :N], AF.Square, r=["X1c%d" % cb], w=["sqt%d" % (cb % 2)])
                        if cb == 31:
                            MM(psX[:, :N], onesb[:, :], sqt[cb % 2][:, :N], start=False, stop=True, r=["sqt%d" % (cb % 2), "onesb"], w=["psX"])
                xck = ["X1c%d" % cb for cb in range(32)]
                ACT(rstdt[:, :N], psX[:, :N], AF.Ln, r=["psX"], w=["rstdt"], bias=EPS)
                ACT(rstdt[:, :N], rstdt[:, :N], AF.Exp, r=["rstdt"], w=["rstdt"], scale=-0.5)
                for kc in range(KC):
                    STT("dve", MH[:, kc, :N], X1[:, kc, :N], gvec[:, 32 + kc:33 + kc], rstdt[:, :N], ALU.mult, ALU.mult,
                        r=["X1c%d" % kc, "gvec", "rstdt"], w=["MH"])
                for gi, (f0, nf) in enumerate(FGROUPS):
                    for fl in range(nf):
                        f = f0 + fl
                        sl = wr_ctr[0] % 3
                        wr_ctr[0] += 1
                        DMA("pool", wring[sl][:, :, :], wu_scr[f], s_wr[sl][0], w=["wring%d" % sl])
                        pu, pg = psA[f % 2], psG[f % 2]
                        for kc in range(KC):
                            MM(pu[:, :N], wring[sl][:, kc, 0:128], MH[:, kc, :N], start=(kc == 0), stop=(kc == KC - 1),
                               r=["wring%d" % sl, "MH"], w=["psA%d" % (f % 2)])
                        for kc in range(KC):
                            MM(pg[:, :N], wring[sl][:, kc, 128:256], MH[:, kc, :N], start=(kc == 0), stop=(kc == KC - 1),
                               r=["wring%d" % sl, "MH"], w=["psG%d" % (f % 2)])
                        cc1 = c1[f % 2]
                        TS("dve", cc1[:, :NV], pg[:, 0:NV], fcw[:, f, 0:1], fcw[:, f, 3:4], ALU.mult, ALU.add,
                           r=["psG%d" % (f % 2), "fcw"], w=["c1_%d" % (f % 2)])
                        STT("dve", cc1[:, :NV], pg[:, 1:NV + 1], fcw[:, f, 1:2], cc1[:, :NV], ALU.mult, ALU.add,
                            r=["psG%d" % (f % 2), "fcw", "c1_%d" % (f % 2)], w=["c1_%d" % (f % 2)])
                        STT("dve", cc1[:, :NV], pg[:, 2:NV + 2], fcw[:, f, 2:3], cc1[:, :NV], ALU.mult, ALU.add,
                            r=["psG%d" % (f % 2), "fcw", "c1_%d" % (f % 2)], w=["c1_%d" % (f % 2)])
                        ACT(sg_[f % 2][:, :NV], cc1[:, :NV], AF.Silu, r=["c1_%d" % (f % 2)], w=["sg%d" % (f % 2)])
                        TT("dve", aTt[:, fl, :NV], sg_[f % 2][:, :NV], pu[:, 1:NV + 1], ALU.mult,
                           r=["sg%d" % (f % 2), "psA%d" % (f % 2)], w=["aTt"])
                    for cb4 in range(8):
                        dl = wd_ctr[0] % 2
                        wd_ctr[0] += 1
                        DMA("pool", wdring[dl][:, :nf, :], wd_scr[gi, cb4, :, 0:nf, :], s_wd[dl], w=["wdring%d" % dl])
                        for h in range(4):
                            cb = cb4 * 4 + h
                            yps = psS if cb % 2 == 0 else psO
                            yk = "psS" if cb % 2 == 0 else "psO"
                            for fl in range(nf):
                                MM(yps[:, :NV], wdring[dl][:, fl, h * 128:(h + 1) * 128], aTt[:, fl, :NV], start=(fl == 0), stop=(fl == nf - 1),
                                   r=["wdring%d" % dl, "aTt"], w=[yk])
                            TT("dve", X1[:, cb, 1:NV + 1], yps[:, :NV], X1[:, cb, 1:NV + 1], ALU.add, r=[yk, "X1c%d" % cb], w=["X1c%d" % cb])
                for cb in range(32):
                    ACT(sqt[cb % 2][:, :NV], X1[:, cb, 1:NV + 1], AF.Square, r=["X1c%d" % cb], w=["sqt%d" % (cb % 2)])
                    MM(psX[:, :NV], onesb[:, :], sqt[cb % 2][:, :NV], start=(cb == 0), stop=(cb == 31), r=["sqt%d" % (cb % 2), "onesb"], w=["psX"])
                ACT(rstdt[:, :NV], psX[:, :NV], AF.Ln, r=["psX"], w=["rstdt"], bias=EPS)
                ACT(rstdt[:, :NV], rstdt[:, :NV], AF.Exp, r=["rstdt"], w=["rstdt"], scale=-0.5)
                for kc in range(KC):
                    STT("dve", X1[:, kc, 1:NV + 1], X1[:, kc, 1:NV + 1], gvec[:, 64 + kc:65 + kc], rstdt[:, :NV], ALU.mult, ALU.mult,
                        r=["X1c%d" % kc, "gvec", "rstdt"], w=["X1c%d" % kc])
                for c in range(ti * 4, ti * 4 + 4):
                    cs, C = CHUNKS[c]
                    lc = cs - s0 + 1
                    for half in range(2):
                        ol = ost_ctr[0] % 2
                        ost_ctr[0] += 1
                        for g4 in range(4):
                            bank = psG[g4 % 2]
                            for jj in range(4):
                                kc = half * 16 + g4 * 4 + jj
                                TR(bank[:C, jj * 128:(jj + 1) * 128], X1[:, kc, lc:lc + C], ident[:, :],
                                   r=["X1c%d" % kc, "consts"], w=["psG%d" % (g4 % 2)])
                            ACT(ost[ol][:C, g4 * 512:(g4 + 1) * 512], bank[:C, :], AF.Copy, r=["psG%d" % (g4 % 2)], w=["ost%d" % ol])
                        out_dmas.append(DMA("sp", y[cs:cs + C, half * 2048:(half + 1) * 2048], ost[ol][:C, :], s_ost[ol], r=["ost%d" % ol]))
                P.op("dve", lambda e: e.memset(rstdt[:, 0:1], 0.0), reads=xck + ["MH"], writes=["X1", "MH", "rstdt"] + xck)
          P.fence()
        else:
            P.fence()
        P.barrier_wait("sp", list(P.ops["sp"][-1].deps))
        block = es.enter_context(nc.Block())
        stats = P.emit(block)
    return nc, stats


def _consts():
    a = np.arange(128)[:, None]
    b = np.arange(128)[None, :]
    c = np.zeros((128, 7, 128), np.float32)
    c[:, 0] = (a == b)
    c[:, 1] = (a <= b) * (-1.0 / 16.0)
    c[:, 2] = (a >= b) * (-1.0 / 16.0)
    c[:, 3] = (a > b) * (-1.0 / 16.0)
    c[:, 4] = (a < b) * (-1.0 / 16.0)
    c[:, 5] = (a <= b)
    c[:, 6] = (a > b)
    return c.reshape(128, 7 * 128)


def _prepare_inputs(x_prompt, x_sample, meta_tokens, mix_norm_g, w_in, w_gate2, b_gate2, head_norm_g,
                    conv_mix_w, w_out, ffn_norm_g, w_up, ffn_conv_w, ffn_conv_b, w_down, final_norm_g):
    f = lambda a: np.ascontiguousarray(np.asarray(a, dtype=np.float32))
    x_prompt, x_sample, meta = f(x_prompt), f(x_sample), f(meta_tokens)
    w_in0, w_out0, w_up0, w_down0 = f(w_in[0]), f(w_out[0]), f(w_up[0]), f(w_down[0])
    wg2, bg2 = f(w_gate2[0]), f(b_gate2[0])
    gvec = np.concatenate([f(mix_norm_g[0]).reshape(32, 128).T, f(ffn_norm_g[0]).reshape(32, 128).T,
                           f(final_norm_g).reshape(32, 128).T], axis=1)
    ghead = f(head_norm_g[0]).reshape(2, 128).T
    cmw = f(conv_mix_w[0]).reshape(3, 16, 128).transpose(2, 1, 0).reshape(128, 48)
    fcw = np.concatenate([f(ffn_conv_w[0]), f(ffn_conv_b[0])[None]], axis=0).reshape(4, NF, 128).transpose(2, 1, 0).reshape(128, NF * 4)
    consts = _consts()
    shared = dict(w_in=w_in0, w_out=w_out0, w_up=w_up0, w_down=w_down0,
                  gvec=f(gvec), ghead=f(ghead), cmw=f(cmw), fcw=f(fcw), consts=consts)

    def core(xown, xext, ext_dir, flag_f, flag_b):
        wgin = np.zeros((D, 80), np.float32)
        wgin[:, 0:16] = w_in0[:, 6144:6160]
        wgin[:, 32:48] = w_in0[:, 6160:6176]
        wgin[:, 64:80] = w_in0[:, 6144 + 16 * ext_dir:6160 + 16 * ext_dir]
        wgb = np.zeros((96, 8, 256), np.float32)
        wgb[0:16, :, 0:128] = wg2[0].reshape(16, 8, 128)
        wgb[16, :, 0:128] = bg2[0].reshape(8, 128)
        wgb[32:48, :, 128:256] = wg2[1].reshape(16, 8, 128)
        wgb[48, :, 128:256] = bg2[1].reshape(8, 128)
        wgb[64:80, :, 0:128] = wg2[ext_dir].reshape(16, 8, 128)
        wgb[80, :, 0:128] = bg2[ext_dir].reshape(8, 128)
        wgb = wgb.reshape(96, 2048)
        flags = np.zeros((128, 2), np.float32)
        flags[:, 0] = flag_f
        flags[:, 1] = flag_b
        m = dict(shared)
        m.update(xown=f(xown), xext=f(xext), wgin=wgin, wgb=wgb, flags=flags)
        return m

    maps = []
    for b in range(2):
        xa = np.concatenate([meta, x_prompt[b, 0:2048]], axis=0)
        ea = x_prompt[b, 2048:4096][::-1]
        maps.append(core(xa, ea, 1, 0.0, 1.0))
        xb = x_prompt[b, 2032:4096]
        eb = np.concatenate([meta, x_prompt[b, 0:2032]], axis=0)
        maps.append(core(xb, eb, 0, 1.0, 0.0))
    for s in range(4):
        xs = np.concatenate([meta, x_sample[s]], axis=0)
        maps.append(core(xs, np.zeros((E, D), np.float32), 0, 0.0, 0.0))
    return maps


_NC_CACHE = {}


def kernel(x_prompt, x_sample, meta_tokens, mix_norm_g, w_in, w_gate2, b_gate2, head_norm_g,
           conv_mix_w, w_out, ffn_norm_g, w_up, ffn_conv_w, ffn_conv_b, w_down, final_norm_g):
    maps = _prepare_inputs(x_prompt, x_sample, meta_tokens, mix_norm_g, w_in, w_gate2, b_gate2, head_norm_g,
                           conv_mix_w, w_out, ffn_norm_g, w_up, ffn_conv_w, ffn_conv_b, w_down, final_norm_g)
    if "nc" not in _NC_CACHE:
        _NC_CACHE["nc"] = build_program()[0]
    nc = _NC_CACHE["nc"]
    res = run_bass_kernel_spmd(nc, maps, core_ids=list(range(8)))
    ys = [np.asarray(r["y"], dtype=np.float32) for r in res.results]
    y_prompt = np.zeros((2, 4096, D), np.float32)
    y_sample = np.zeros((4, 2048, D), np.float32)
    for b in range(2):
        y_prompt[b, 0:2040] = ys[2 * b][16:16 + 2040]
        y_prompt[b, 2040:4096] = ys[2 * b + 1][8:2064]
    for s in range(4):
        y_sample[s] = ys[4 + s][16:2064]
    return (y_prompt, y_sample)
```

```python
import os
import numpy as np
from contextlib import ExitStack
import concourse.bass as bass
import concourse.mybir as mybir
from concourse.bass_utils import run_bass_kernel_spmd

F32 = mybir.dt.float32
BF16 = mybir.dt.bfloat16
AF = mybir.ActivationFunctionType
ALU = mybir.AluOpType

D = 4096
KC = 32
R = 2064
E = 2048
DFF = 11008
NF = 86
EPS = 1e-6
TB = [0, 413, 826, 1239, 1652, 2064]
ET = 256
CHUNKS = []
for _i in range(5):
    _s, _e = TB[_i], TB[_i + 1]
    _w = _e - _s
    _c0 = _w - 309
    CHUNKS.append((_s, _c0))
    for _k in range(3):
        CHUNKS.append((_s + _c0 + 103 * _k, 103))
FGROUPS = []
_f = 0
for _n in [11, 11, 11, 11, 11, 11, 10, 10]:
    FGROUPS.append((_f, _n))
    _f += _n
OFF_Q, OFF_K, OFF_V, OFF_R, OFF_CB, OFF_CC, OFF_CH = 0, 1024, 2048, 4096, 6176, 8224, 10272


class _Op:
    __slots__ = ("eng", "fn", "deps", "is_dma", "sem", "val", "signal")

    def __init__(self, eng, fn, is_dma=False):
        self.eng = eng
        self.fn = fn
        self.deps = []
        self.is_dma = is_dma
        self.sem = None
        self.val = 0
        self.signal = False


class Prog:
    ENGS = ("pe", "act", "dve", "pool", "sp")

    def __init__(self, nc, es):
        self.nc = nc
        self.es = es
        self.ops = {e: [] for e in self.ENGS}
        self.last_w = {}
        self.readers = {}
        self.eng_sem = {e: es.enter_context(nc.semaphore("prog_" + e)) for e in self.ENGS}
        self.dma_sem_count = {}
        self.pending_dmas = []

    def new_dma_sem(self, name):
        s = self.es.enter_context(self.nc.semaphore(name))
        self.dma_sem_count[id(s)] = 0
        return s

    def _collect(self, op, reads, writes):
        deps = []
        for k in reads:
            w = self.last_w.get(k)
            if w is not None:
                deps.append(w)
        for k in writes:
            w = self.last_w.get(k)
            if w is not None:
                deps.append(w)
            deps.extend(self.readers.get(k, ()))
        seen = set()
        for d in deps:
            if d is op or id(d) in seen:
                continue
            seen.add(id(d))
            if (not d.is_dma) and (not op.is_dma) and d.eng == "pe" and op.eng == "pe":
                continue
            op.deps.append(d)
            if not d.is_dma:
                d.signal = True
        for k in writes:
            self.last_w[k] = op
            self.readers[k] = []
        for k in reads:
            self.readers.setdefault(k, []).append(op)

    def op(self, eng, fn, reads=(), writes=()):
        pr = [k for k in reads if k.startswith("ps")]
        if pr:
            reads = [k for k in reads if not k.startswith("ps")]
            writes = list(writes) + [k for k in pr if k not in writes]
        o = _Op(eng, fn)
        self._collect(o, reads, writes)
        self.ops[eng].append(o)
        return o

    def dma(self, eng, fn, sem, reads=(), writes=()):
        o = _Op(eng, fn, is_dma=True)
        o.sem = sem
        self.dma_sem_count[id(sem)] += 16
        o.val = self.dma_sem_count[id(sem)]
        self._collect(o, reads, writes)
        self.ops[eng].append(o)
        self.pending_dmas.append(o)
        return o

    def barrier_wait(self, eng, deps):
        o = _Op(eng, None)
        for d in deps:
            o.deps.append(d)
            if not d.is_dma:
                d.signal = True
        self.ops[eng].append(o)
        return o

    def fence(self):
        lasts = []
        for e in self.ENGS:
            for o in reversed(self.ops[e]):
                if (not o.is_dma) and o.fn is not None:
                    lasts.append(o)
                    break
        deps = lasts + self.pending_dmas
        for e in self.ENGS:
            self.barrier_wait(e, deps)
        self.pending_dmas = []
        self.last_w = {}
        self.readers = {}

    def emit(self, block):
        for e in self.ENGS:
            c = 0
            for o in self.ops[e]:
                if o.is_dma:
                    continue
                if o.signal:
                    c += 1
                    o.sem = self.eng_sem[e]
                    o.val = c
        stats = {}

        def run(e, engine):
            known = {}
            nw = 0
            for o in self.ops[e]:
                need = {}
                for d in o.deps:
                    key = id(d.sem)
                    if need.get(key, (None, 0))[1] < d.val:
                        need[key] = (d.sem, d.val)
                for key, (sem, val) in need.items():
                    if known.get(key, 0) < val:
                        engine.wait_ge(sem, val)
                        known[key] = val
                        nw += 1
                if o.fn is None:
                    continue
                ins = o.fn(engine)
                if o.is_dma:
                    ins.then_inc(o.sem, 16)
                elif o.signal:
                    ins.then_inc(o.sem, 1)
            stats[e] = (len(self.ops[e]), nw)

        @block.tensor
        def _(t):
            run("pe", t)

        @block.scalar
        def _(s):
            run("act", s)

        @block.vector
        def _(v):
            run("dve", v)

        @block.gpsimd
        def _(g):
            run("pool", g)

        @block.sync
        def _(s):
            run("sp", s)
        return stats


def build_program(debug=False, stages=3, cut=99):
    nc = bass.Bass("TRN2", target_bir_lowering=False)
    din = lambda name, shape: nc.dram_tensor(name, shape, F32, kind="ExternalInput").ap()
    xown = din("xown", [R, D])
    xext = din("xext", [E, D])
    w_in = din("w_in", [D, 12320])
    w_out = din("w_out", [D, D])
    w_up = din("w_up", [D, 2 * DFF])
    w_down = din("w_down", [DFF, D])
    wgin = din("wgin", [D, 80])
    wgb = din("wgb", [96, 2048])
    gvec_d = din("gvec", [128, 96])
    ghead_d = din("ghead", [128, 2])
    cmw_d = din("cmw", [128, 48])
    fcw_d = din("fcw", [128, NF * 4])
    consts_d = din("consts", [128, 7 * 128])
    flags_d = din("flags", [128, 2])
    y = nc.dram_tensor("y", [R, D], F32, kind="ExternalOutput").ap()
    skind = "ExternalOutput" if debug else "Internal"
    hT_scr = nc.dram_tensor("hT_scr", [128, KC, R + E], BF16, kind=skind).ap()
    xT_scr = nc.dram_tensor("xT_scr", [128, KC, R], F32, kind=skind).ap()
    mT_scr = nc.dram_tensor("mT_scr", [128, KC, R], BF16, kind=skind).ap()

    wo_scr = nc.dram_tensor("wo_scr", [16, 128, KC, 256], BF16).ap()
    wu_scr = nc.dram_tensor("wu_scr", [NF, 128, KC, 256], BF16).ap()
    wd_scr = nc.dram_tensor("wd_scr", [8, 8, 128, 11, 512], BF16).ap()
    w_inv = w_in.rearrange("(kc p) c -> p kc c", p=128)
    w_outv = w_out.rearrange("(kc p) c -> p kc c", p=128)
    w_upv = w_up.rearrange("(kc p) c -> p kc c", p=128)
    w_downv = w_down.rearrange("(f p) c -> p f c", p=128)
    wginv = wgin.rearrange("(kc p) c -> p kc c", p=128)

    with ExitStack() as es:
        P = Prog(nc, es)
        sb = lambda ctx, name, shape, dt: ctx.enter_context(nc.sbuf_tensor(name, shape, dt))
        psA = [es.enter_context(nc.psum_tensor("psA%d" % i, [128, 512], F32)) for i in range(2)]
        psG = [es.enter_context(nc.psum_tensor("psG%d" % i, [128, 512], F32)) for i in range(2)]
        psS = es.enter_context(nc.psum_tensor("psS", [128, 512], F32))
        psO = es.enter_context(nc.psum_tensor("psO", [128, 512], F32))
        psT = es.enter_context(nc.psum_tensor("psT", [128, 1024], BF16))
        psX = es.enter_context(nc.psum_tensor("psX", [128, 512], F32))

        def MM(out, lhsT, rhs, start=True, stop=True, r=(), w=()):
            return P.op("pe", lambda e: e.matmul(out, lhsT=lhsT, rhs=rhs, start=start, stop=stop), r, w)

        def TR(out, in_, ident, r=(), w=()):
            return P.op("pe", lambda e: e.transpose(out=out, in_=in_, identity=ident), r, w)

        def ACT(out, in_, func, r=(), w=(), **kw):
            return P.op("act", lambda e: e.activation(out=out, in_=in_, func=func, **kw), r, w)

        def TT(eng, out, in0, in1, op, r=(), w=()):
            return P.op(eng, lambda e: e.tensor_tensor(out=out, in0=in0, in1=in1, op=op), r, w)

        def TS(eng, out, in0, s1, s2, op0, op1=None, r=(), w=()):
            if op1 is None:
                return P.op(eng, lambda e: e.tensor_scalar(out=out, in0=in0, scalar1=s1, scalar2=None, op0=op0), r, w)
            return P.op(eng, lambda e: e.tensor_scalar(out=out, in0=in0, scalar1=s1, scalar2=s2, op0=op0, op1=op1), r, w)

        def STT(eng, out, in0, scalar, in1, op0, op1, r=(), w=()):
            return P.op(eng, lambda e: e.scalar_tensor_tensor(out=out, in0=in0, scalar=scalar, in1=in1, op0=op0, op1=op1), r, w)

        def CP(eng, out, in_, r=(), w=()):
            return P.op(eng, lambda e: e.tensor_copy(out=out, in_=in_), r, w)

        def MSET(eng, ap, val, r=(), w=()):
            return P.op(eng, lambda e: e.memset(ap, val), r, w)

        def DMA(eng, out, in_, sem, r=(), w=()):
            return P.dma(eng, lambda e: e.dma_start(out=out, in_=in_), sem, r, w)

        conv_jobs = []
        for cb2 in range(16):
            conv_jobs.append((wo_scr[cb2], w_outv[:, :, cb2 * 256:(cb2 + 1) * 256]))
        for f in range(NF):
            conv_jobs.append((wu_scr[f, :, :, 0:128], w_upv[:, :, f * 128:(f + 1) * 128]))
            conv_jobs.append((wu_scr[f, :, :, 128:256], w_upv[:, :, DFF + f * 128:DFF + (f + 1) * 128]))
        for gi, (f0, nf) in enumerate(FGROUPS):
            for cb4 in range(8):
                conv_jobs.append((wd_scr[gi, cb4, :, 0:nf, :], w_downv[:, f0:f0 + nf, cb4 * 512:(cb4 + 1) * 512]))
        conv_sems = [P.new_dma_sem("s_cv%d" % i) for i in range(4)]
        conv_state = [0, 0.0]
        n_hooks = 13 + 8 * 13 + 16 * 5
        per_hook = len(conv_jobs) / float(n_hooks) + 0.01

        def conv_hook(flush=False):
            conv_state[1] += per_hook
            while conv_state[0] < len(conv_jobs) and (flush or conv_state[0] < conv_state[1]):
                o_ap, i_ap = conv_jobs[conv_state[0]]
                DMA("pool", o_ap, i_ap, conv_sems[conv_state[0] % 4], w=["cv%d" % (conv_state[0] % 4)])
                conv_state[0] += 1

        consts = sb(es, "consts_sb", [128, 7, 128], F32)
        identb = sb(es, "identb", [128, 128], BF16)
        onesb = sb(es, "onesb", [128, 128], BF16)
        gvec = sb(es, "gvecs", [128, 96], F32)
        ghead = sb(es, "gheads", [128, 2], F32)
        cmw = sb(es, "cmws", [128, 16, 3], F32)
        fcw = sb(es, "fcws", [128, NF, 4], F32)
        flags = sb(es, "flagss", [128, 2], F32)
        sc = [P.new_dma_sem("sc%d" % i) for i in range(6)]
        DMA("sp", consts[:].rearrange("p a b -> p (a b)"), consts_d[:, :], sc[0], w=["consts"])
        DMA("sp", gvec[:], gvec_d[:, :], sc[1], w=["gvec"])
        DMA("sp", ghead[:], ghead_d[:, :], sc[2], w=["ghead"])
        DMA("sp", cmw[:].rearrange("p a b -> p (a b)"), cmw_d[:, :], sc[3], w=["cmw"])
        DMA("sp", fcw[:].rearrange("p a b -> p (a b)"), fcw_d[:, :], sc[4], w=["fcw"])
        DMA("sp", flags[:], flags_d[:, :], sc[5], w=["flags"])
        CP("dve", identb[:], consts[:, 0, :], r=["consts"], w=["identb"])
        MSET("dve", onesb[:], 1.0 / 4096.0, w=["onesb"])
        ident = consts[:, 0, :]
        Mle, Mge, Mgt, Mlt = consts[:, 1, :], consts[:, 2, :], consts[:, 3, :], consts[:, 4, :]
        masks = consts[:, 5:7, :]

        with ExitStack() as sa:
            xin = [sb(sa, "xin%d" % i, [128, D], F32) for i in range(2)]
            xs = [sb(sa, "xs%d" % i, [128, D], F32) for i in range(2)]
            xTs = [sb(sa, "xTs%d" % i, [128, KC, 128], F32) for i in range(2)]
            hTs = [sb(sa, "hTs%d" % i, [128, KC, 128], BF16) for i in range(2)]
            junkA = sb(sa, "junkA", [128, D], BF16)
            ssA = [sb(sa, "ssA%d" % i, [128, 1], F32) for i in range(2)]
            rsA = [sb(sa, "rsA%d" % i, [128, 1], F32) for i in range(2)]
            s_xin = [P.new_dma_sem("s_xin%d" % i) for i in range(2)]
            s_xst = [P.new_dma_sem("s_xst%d" % i) for i in range(2)]
            s_hst = [P.new_dma_sem("s_hst%d" % i) for i in range(2)]
            blocks = [(xown, i * 128, 128, True, i * 128) for i in range(16)] + [(xown, 2048, 16, True, 2048)]
            blocks += [(xext, i * 128, 128, False, R + i * 128) for i in range(16)]
            def load_blk(bi):
                src_, r0_, nr_, own_, c0_ = blocks[bi]
                DMA("sp", xin[bi % 2][:nr_, :], src_[r0_:r0_ + nr_, :], s_xin[bi % 2], w=["xin%d" % (bi % 2)])
            def front(bi):
                src, r0, nr, own, c0 = blocks[bi]
                sl = bi % 2
                MSET("dve", ssA[sl][:], 0.0, w=["ssA%d" % sl])
                ACT(junkA[:nr, :], xin[sl][:nr, :], AF.Square, r=["xin%d" % sl, "ssA%d" % sl], w=["junkA", "ssA%d" % sl],
                    accum_out=ssA[sl][:nr, 0:1])
                ACT(rsA[sl][:nr, :], ssA[sl][:nr, :], AF.Ln, r=["ssA%d" % sl], w=["rsA%d" % sl], scale=1.0 / D, bias=EPS)
                ACT(rsA[sl][:nr, :], rsA[sl][:nr, :], AF.Exp, r=["rsA%d" % sl], w=["rsA%d" % sl], scale=-0.5)
                TS("dve", xs[sl][:nr, :], xin[sl][:nr, :], rsA[sl][:nr, 0:1], None, ALU.mult, r=["xin%d" % sl, "rsA%d" % sl], w=["xs%d" % sl])

            def back(bi):
                src, r0, nr, own, c0 = blocks[bi]
                sl = bi % 2
                for g in range(8):
                    if own:
                        bank = psA[g % 2]
                        for jq in range(4):
                            kc = g * 4 + jq
                            TR(bank[:, jq * 128:jq * 128 + nr], xin[sl][:nr, kc * 128:(kc + 1) * 128], ident[:nr, :nr],
                               r=["xin%d" % sl, "consts"], w=["psA%d" % (g % 2)])
                        ACT(xTs[sl][:, g * 4:(g + 1) * 4, :nr], bank[:].rearrange("p (a b) -> p a b", a=4)[:, :, :nr], AF.Copy,
                            r=["psA%d" % (g % 2)], w=["xTs%d_%d" % (sl, g)])
                    bank2 = psG[g % 2]
                    for jq in range(4):
                        kc = g * 4 + jq
                        TR(bank2[:, jq * 128:jq * 128 + nr], xs[sl][:nr, kc * 128:(kc + 1) * 128], ident[:nr, :nr],
                           r=["xs%d" % sl, "consts"], w=["psG%d" % (g % 2)])
                    TT("dve", hTs[sl][:, g * 4:(g + 1) * 4, :nr], bank2[:].rearrange("p (a b) -> p a b", a=4)[:, :, :nr],
                       gvec[:, g * 4:(g + 1) * 4].unsqueeze(2).to_broadcast([128, 4, nr]), ALU.mult,
                       r=["psG%d" % (g % 2), "gvec"], w=["hTs%d_%d" % (sl, g)])
                hk = ["hTs%d_%d" % (sl, g) for g in range(8)]
                DMA("pool", hT_scr[:, :, c0:c0 + nr], hTs[sl][:, :, :nr], s_hst[sl], r=hk, w=["hT_scr"])
                if own:
                    xk = ["xTs%d_%d" % (sl, g) for g in range(8)]
                    DMA("sp", xT_scr[:, :, r0:r0 + nr], xTs[sl][:, :, :nr], s_xst[sl], r=xk, w=["xT_scr"])

            load_blk(0)
            load_blk(1)
            front(0)
            for bi in range(len(blocks)):
                if bi + 1 < len(blocks):
                    front(bi + 1)
                back(bi)
                if bi + 2 < len(blocks):
                    load_blk(bi + 2)
        P.fence()

        if stages >= 2:
          with ExitStack() as sB:
            wq = sb(sB, "wq", [128, KC, 128], BF16)
            wk = sb(sB, "wk", [128, KC, 128], BF16)
            wv = sb(sB, "wv", [128, KC, 256], BF16)
            wr = sb(sB, "wr", [128, KC, 256], BF16)
            hts = [sb(sB, "hts%d" % i, [128, KC, 413], BF16) for i in range(2)]
            s_w = {k: P.new_dma_sem("s_w" + k) for k in ("q", "k", "v", "r")}
            s_ht = [P.new_dma_sem("s_ht%d" % i) for i in range(2)]
            s_mst = P.new_dma_sem("s_mst")
            ht_ctr = [0]

            def load_ht(c0, n):
                sl = ht_ctr[0] % 2
                ht_ctr[0] += 1
                conv_hook()
                DMA("sp", hts[sl][:, :, :n], hT_scr[:, :, c0:c0 + n], s_ht[sl], r=["hT_scr"], w=["hts%d" % sl])
                return hts[sl], "hts%d" % sl

            acc_ctr = [0]

            def proj(wt, wkey, wcols, ht, htkey, n, evac, M=128):
                i = acc_ctr[0] % 2
                acc_ctr[0] += 1
                acc = psA[i]
                for kc in range(KC):
                    MM(acc[:M, :n], wt[:, kc, wcols], ht[:, kc, :n], start=(kc == 0), stop=(kc == KC - 1),
                       r=[wkey, htkey], w=["psA%d" % i])
                evac(acc, "psA%d" % i)

            with ExitStack() as sg:
                wg_sb = sb(sg, "wg_sb", [128, KC, 80], BF16)
                qT = sb(sg, "qT", [128, R], BF16)
                kT = sb(sg, "kT", [128, R], BF16)
                rs = sb(sg, "rs", [128, 2, R], BF16)
                vt = sb(sg, "vt", [128, 2, 413], BF16)
                kte = sb(sg, "kte", [128, ET], BF16)
                kv = sb(sg, "kv", [128, 20, 384], BF16)
                kve = sb(sg, "kve", [128, 2, 384], BF16)
                kvep = [kve, sb(sg, "kve_b", [128, 2, 384], BF16)]
                Sbb = sb(sg, "Sbb", [128, 20, 256], BF16)
                aT = sb(sg, "aT", [96, R], F32)
                wgb_sb = sb(sg, "wgb_sb", [96, 2048], F32)
                mst = sb(sg, "mst", [128, 2, R], BF16)
                ex = sb(sg, "ex", [128, 256], F32)
                lsb = sb(sg, "lsb", [128, 256], F32)
                E1 = sb(sg, "E1", [128, 256], F32)
                E2 = sb(sg, "E2", [128, 256], F32)
                E3 = sb(sg, "E3", [128, 256], F32)
                qq = sb(sg, "qq", [128, 2, 128], BF16)
                qqp = [sb(sg, "qqp%d" % i, [128, 2, 128], BF16) for i in range(2)]
                khp = [sb(sg, "khp%d" % i, [128, 128], BF16) for i in range(2)]
                E1p = [sb(sg, "E1p%d" % i, [128, 256], F32) for i in range(2)]
                kk = sb(sg, "kk", [128, 2, 128], BF16)
                khf = sb(sg, "khf", [128, 128], BF16)
                khb = sb(sg, "khb", [128, 128], BF16)
                PT = sb(sg, "PT", [128, 2, 128], BF16)
                on = sb(sg, "on", [128, 256], BF16)
                junk = sb(sg, "junk", [128, 256], BF16)
                Sf = sb(sg, "Sf", [128, 256], F32)
                Sb_ = sb(sg, "Sb_", [128, 256], F32)
                Se = sb(sg, "Se", [128, 256], F32)
                Sfb = sb(sg, "Sfb", [128, 256], BF16)
                ss1 = sb(sg, "ss1", [128, 1], F32)
                rstd1 = sb(sg, "rstd1", [128, 1], F32)
                s_wg = P.new_dma_sem("s_wg")
                s_wgb = P.new_dma_sem("s_wgb")

                DMA("pool", wg_sb[:], wginv[:, :, :], s_wg, w=["wg_sb"])
                DMA("sp", wgb_sb[:], wgb[:, :], s_wgb, w=["wgb_sb"])
                MSET("dve", aT[:, :], 1.0, w=["aT"])

                for ti in range(5):
                    s0, e0 = TB[ti], TB[ti + 1]
                    n = e0 - s0
                    ht, hk = load_ht(s0, n)

                    def ev(acc, ak, s0=s0, n=n):
                        ACT(aT[0:16, s0:s0 + n], acc[0:16, :n], AF.Copy, r=[ak], w=["aT"])
                        ACT(aT[32:48, s0:s0 + n], acc[32:48, :n], AF.Copy, r=[ak], w=["aT"])
                    proj(wg_sb, "wg_sb", slice(0, 80), ht, hk, n, ev, M=80)
                for et in range(E // ET):
                    ht, hk = load_ht(R + et * ET, ET)

                    def ev(acc, ak, et=et):
                        ACT(aT[64:80, et * ET:(et + 1) * ET], acc[64:80, :ET], AF.Copy, r=[ak], w=["aT"])
                    proj(wg_sb, "wg_sb", slice(0, 80), ht, hk, ET, ev, M=80)

                def gates_decays(cols0, C, j, frow, do_b):
                    if do_b:
                        MM(psG[0][:C, 0:256], aT[0:64, cols0:cols0 + C], wgb_sb[0:64, j * 256:(j + 1) * 256],
                           r=["aT", "wgb_sb"], w=["psG0"])
                        W = 256
                    else:
                        MM(psG[0][:C, 0:128], aT[frow:frow + 32, cols0:cols0 + C], wgb_sb[frow:frow + 32, j * 256:j * 256 + 128],
                           r=["aT", "wgb_sb"], w=["psG0"])
                        W = 128
                    ACT(ex[:C, :W], psG[0][:C, :W], AF.Exp, r=["psG0"], w=["ex"], scale=-1.0)
                    ACT(lsb[:C, :W], ex[:C, :W], AF.Ln, r=["ex"], w=["lsb"], bias=1.0)
                    MM(psG[1][:, 0:C], lsb[:C, 0:128], Mle[:C, :C], r=["lsb", "consts"], w=["psG1"])
                    MM(psG[1][:C, 256:384], Mgt[:C, :C], lsb[:C, 0:128], r=["lsb", "consts"], w=["psG1"])
                    if do_b:
                        MM(psG[1][:, 128:128 + C], lsb[:C, 128:256], Mge[:C, :C], r=["lsb", "consts"], w=["psG1"])
                        MM(psG[1][:C, 384:512], Mlt[:C, :C], lsb[:C, 128:256], r=["lsb", "consts"], w=["psG1"])

                def load_head_w(jj):
                    DMA("pool", wk[:], w_inv[:, :, OFF_K + jj * 128:OFF_K + (jj + 1) * 128], s_w["k"], w=["wk"])
                    DMA("pool", wv[:], w_inv[:, :, OFF_V + jj * 256:OFF_V + (jj + 1) * 256], s_w["v"], w=["wv"])
                    DMA("pool", wq[:], w_inv[:, :, OFF_Q + jj * 128:OFF_Q + (jj + 1) * 128], s_w["q"], w=["wq"])
                    DMA("pool", wr[:], w_inv[:, :, OFF_R + jj * 256:OFF_R + (jj + 1) * 256], s_w["r"], w=["wr"])

                NH = 8 if cut >= 6 else (1 if cut >= 2 else 0)
                for j in range(NH):
                    load_head_w(j)
                    MSET("dve", Se[:], 0.0, w=["Se"])
                    NE = E // 128

                    def ext_step(t):
                        if 1 <= t <= NE:
                            e = t - 1
                            p = e % 2
                            kvt = kvep[(e // 2) % 2]
                            ci = e % 2
                            MM(psG[1][:, 0:128], lsb[:128, 0:128], Mle[:, :], r=["lsb", "consts"], w=["psG1"])
                            MM(psG[1][:128, 256:384], Mgt[:, :], lsb[:128, 0:128], r=["lsb", "consts"], w=["psG1"])
                            ACT(E1p[p][:, 0:128], psG[1][:, 0:128], AF.Exp, r=["psG1"], w=["E1_%d" % p])
                            ACT(E3[:, 0:128], psG[1][:, 256:384], AF.Exp, r=["psG1"], w=["E3"])
                            TT("dve", khp[p][:, :], kvt[:, ci, 0:128], E3[:, 0:128], ALU.mult,
                               r=["kve%d_%d" % ((e // 2) % 2, ci), "E3"], w=["kh%d" % p])
                        if t < NE:
                            ec = t * 128
                            MM(psG[0][:128, 0:128], aT[64:96, ec:ec + 128], wgb_sb[64:96, j * 256:j * 256 + 128],
                               r=["aT", "wgb_sb"], w=["psG0"])
                            ACT(ex[:, :128], psG[0][:, :128], AF.Exp, r=["psG0"], w=["ex"], scale=-1.0)
                            ACT(lsb[:, :128], ex[:, :128], AF.Ln, r=["ex"], w=["lsb"], bias=1.0)
                        if 2 <= t < NE + 2:
                            e = t - 2
                            p = e % 2
                            kvt = kvep[(e // 2) % 2]
                            ci = e % 2
                            MM(psX[:, 0:256], khp[p][:, :], kvt[:, ci, 128:384], r=["kh%d" % p, "kve%d_%d" % ((e // 2) % 2, ci)], w=["psX"])
                            STT("dve", Se[:], Se[:], E1p[p][:, 127:128], psX[:, 0:256], ALU.mult, ALU.add,
                                r=["Se", "E1_%d" % p, "psX"], w=["Se"])

                    ext_step(0)
                    for et in range(E // ET):
                        ht, hk = load_ht(R + et * ET, ET)
                        proj(wk, "wk", slice(0, 128), ht, hk, ET,
                             lambda acc, ak: ACT(kte[:, :ET], acc[:, :ET], AF.Copy, r=[ak], w=["kte"]))
                        for h in range(2):
                            proj(wv, "wv", slice(h * 128, (h + 1) * 128), ht, hk, ET,
                                 lambda acc, ak, h=h: ACT(vt[:, h, :ET], acc[:, :ET], AF.Copy, r=[ak], w=["vt%d" % h]))
                        for ci in range(ET // 128):
                            lc = ci * 128
                            TR(psT[:, 0:128], kte[:, lc:lc + 128], identb[:, :], r=["kte", "identb"], w=["psT"])
                            for h in range(2):
                                TR(psT[:, 128 + h * 128:256 + h * 128], vt[:, h, lc:lc + 128], identb[:, :],
                                   r=["vt%d" % h, "identb"], w=["psT"])
                            CP("dve", kvep[et % 2][:, ci, :], psT[:, 0:384], r=["psT"], w=["kve%d_%d" % (et % 2, ci)])
                        ext_step(2 * et + 1)
                        ext_step(2 * et + 2)
                    ext_step(NE + 1)
                    TS("dve", Sf[:], Se[:], flags[:, 0:1], None, ALU.mult, r=["Se", "flags"], w=["Sf"])
                    TS("dve", Sb_[:], Se[:], flags[:, 1:2], None, ALU.mult, r=["Se", "flags"], w=["Sb"])
                    for ti in range(5 if cut >= 3 else 0):
                        s0, e0 = TB[ti], TB[ti + 1]
                        n = e0 - s0
                        ht, hk = load_ht(s0, n)
                        proj(wq, "wq", slice(0, 128), ht, hk, n,
                             lambda acc, ak, s0=s0, n=n: ACT(qT[:, s0:s0 + n], acc[:, :n], AF.Copy, r=[ak], w=["qT"], scale=128.0 ** -0.5))
                        proj(wk, "wk", slice(0, 128), ht, hk, n,
                             lambda acc, ak, s0=s0, n=n: ACT(kT[:, s0:s0 + n], acc[:, :n], AF.Copy, r=[ak], w=["kT"]))
                        for h in range(2):
                            proj(wv, "wv", slice(h * 128, (h + 1) * 128), ht, hk, n,
                                 lambda acc, ak, h=h, n=n: ACT(vt[:, h, :n], acc[:, :n], AF.Copy, r=[ak], w=["vt%d" % h]))
                        for h in range(2):
                            proj(wr, "wr", slice(h * 128, (h + 1) * 128), ht, hk, n,
                                 lambda acc, ak, h=h, s0=s0, n=n: ACT(rs[:, h, s0:s0 + n], acc[:, :n], AF.Silu, r=[ak], w=["rs"]))
                        for c in range(ti * 4, ti * 4 + 4):
                            cs, C = CHUNKS[c]
                            lc = cs - s0
                            TR(psT[:C, 0:128], kT[:, cs:cs + C], identb[:, :], r=["kT", "identb"], w=["psT"])
                            for h in range(2):
                                TR(psT[:C, 128 + h * 128:256 + h * 128], vt[:, h, lc:lc + C], identb[:, :],
                                   r=["vt%d" % h, "identb"], w=["psT"])
                            CP("dve", kv[:C, c, :], psT[:C, 0:384], r=["psT"], w=["kv"])
                    def st_gate(c):
                        cs, C = CHUNKS[c]
                        MM(psG[0][:C, 0:256], aT[0:64, cs:cs + C], wgb_sb[0:64, j * 256:(j + 1) * 256],
                           r=["aT", "wgb_sb"], w=["psG0"])
                        ACT(ex[:C, :256], psG[0][:C, :256], AF.Exp, r=["psG0"], w=["ex"], scale=-1.0)
                        ACT(lsb[:C, :256], ex[:C, :256], AF.Ln, r=["ex"], w=["lsb"], bias=1.0)

                    def st_cum(c, fwd):
                        cs, C = CHUNKS[c]
                        p = c % 2
                        if fwd:
                            MM(psG[1][:, 0:C], lsb[:C, 0:128], Mle[:C, :C], r=["lsb", "consts"], w=["psG1"])
                            MM(psG[1][:C, 256:384], Mgt[:C, :C], lsb[:C, 0:128], r=["lsb", "consts"], w=["psG1"])
                        MM(psG[1][:, 128:128 + C], lsb[:C, 128:256], Mge[:C, :C], r=["lsb", "consts"], w=["psG1"])
                        MM(psG[1][:C, 384:512], Mlt[:C, :C], lsb[:C, 128:256], r=["lsb", "consts"], w=["psG1"])
                        if fwd:
                            ACT(E1p[p][:, 0:256], psG[1][:, 0:256], AF.Exp, r=["psG1"], w=["E1_%d" % p])
                            ACT(E2[:, 0:256], psG[1][:, 0:256], AF.Exp, r=["psG1"], w=["E2"], scale=-1.0)
                            ACT(E3[:C, 0:256], psG[1][:C, 256:512], AF.Exp, r=["psG1"], w=["E3"])
                            TT("dve", qqp[p][:, :, :C], qT[:, cs:cs + C].unsqueeze(1).to_broadcast([128, 2, C]),
                               E1p[p][:].rearrange("p (a b) -> p a b", a=2)[:, :, :C], ALU.mult, r=["qT", "E1_%d" % p], w=["qq%d" % p])
                            TT("dve", kk[:, :, :C], kT[:, cs:cs + C].unsqueeze(1).to_broadcast([128, 2, C]),
                               E2[:].rearrange("p (a b) -> p a b", a=2)[:, :, :C], ALU.mult, r=["kT", "E2"], w=["kk"])
                            TT("dve", khp[p][:C, :], kv[:C, c, 0:128], E3[:C, 0:128], ALU.mult, r=["kv", "E3"], w=["kh%d" % p])
                        else:
                            ACT(E1p[p][:, 128:256], psG[1][:, 128:256], AF.Exp, r=["psG1"], w=["E1_%d" % p])
                            ACT(E3[:C, 128:256], psG[1][:C, 384:512], AF.Exp, r=["psG1"], w=["E3"])
                            TT("dve", khp[p][:C, :], kv[:C, c, 0:128], E3[:C, 128:256], ALU.mult, r=["kv", "E3"], w=["kh%d" % p])

                    NCH = 20 if cut >= 4 else 0
                    order = list(reversed(range(NCH)))
                    for t in range(NCH + 2):
                        if 1 <= t <= NCH:
                            st_cum(order[t - 1], False)
                        if t < NCH:
                            st_gate(order[t])
                        if t >= 2:
                            c = order[t - 2]
                            cs, C = CHUNKS[c]
                            p = c % 2
                            ACT(Sbb[:, c, :], Sb_[:], AF.Copy, r=["Sb"], w=["Sbb"])
                            MM(psX[:, 0:256], khp[p][:C, :], kv[:C, c, 128:384], r=["kh%d" % p, "kv"], w=["psX"])
                            STT("dve", Sb_[:], Sb_[:], E1p[p][:, 128:129], psX[:, 0:256], ALU.mult, ALU.add,
                                r=["Sb", "E1_%d" % p, "psX"], w=["Sb"])
                    NCH = 20 if cut >= 5 else 0
                    for t in range(NCH + 4):
                        if 3 <= t < NCH + 3:
                            c = t - 3
                            cs, C = CHUNKS[c]
                            for h in range(2):
                                TR(psT[:, 512 + h * 128:512 + h * 128 + C], on[:C, h * 128:(h + 1) * 128], identb[:C, :C],
                                   r=["on", "identb"], w=["psT"])
                            for h in range(2):
                                STT("dve", mst[:, h, cs:cs + C], psT[:, 512 + h * 128:512 + h * 128 + C], ghead[:, h:h + 1],
                                    rs[:, h, cs:cs + C], ALU.mult, ALU.mult, r=["psT", "ghead", "rs"], w=["mst"])
                        if 2 <= t < NCH + 2:
                            c = t - 2
                            cs, C = CHUNKS[c]
                            p = c % 2
                            ACT(Sfb[:], Sf[:], AF.Copy, r=["Sf"], w=["Sfb"])
                            MM(psS[:C, 0:C], kk[:, 0, :C], qqp[p][:, 0, :C], r=["kk", "qq%d" % p], w=["psS"])
                            MM(psS[:C, 128:128 + C], kk[:, 1, :C], qqp[p][:, 1, :C], r=["kk", "qq%d" % p], w=["psS"])
                            TT("dve", PT[:C, :, :C], psS[:C, 0:256].rearrange("p (a b) -> p a b", a=2)[:, :, :C], masks[:C, :, :C],
                               ALU.mult, r=["psS", "consts"], w=["PT"])
                        if 1 <= t < NCH + 1:
                            st_cum(t - 1, True)
                        if t < NCH:
                            st_gate(t)
                        if 2 <= t < NCH + 2:
                            c = t - 2
                            cs, C = CHUNKS[c]
                            p = c % 2
                            MM(psO[:C, 0:256], PT[:C, 0, :C], kv[:C, c, 128:384], start=True, stop=False, r=["PT", "kv"], w=["psOo"])
                            MM(psO[:C, 0:256], PT[:C, 1, :C], kv[:C, c, 128:384], start=False, stop=False, r=["PT", "kv"], w=["psOo"])
                            MM(psO[:C, 0:256], qqp[p][:, 0, :C], Sfb[:, :], start=False, stop=False, r=["qq%d" % p, "Sfb"], w=["psOo"])
                            MM(psO[:C, 0:256], qqp[p][:, 1, :C], Sbb[:, c, :], start=False, stop=True, r=["qq%d" % p, "Sbb"], w=["psOo"])
                            MM(psX[:, 0:256], khp[p][:C, :], kv[:C, c, 128:384], r=["kh%d" % p, "kv"], w=["psX"])
                            STT("dve", Sf[:], Sf[:], E1p[p][:, C - 1:C], psX[:, 0:256], ALU.mult, ALU.add,
                                r=["Sf", "E1_%d" % p, "psX"], w=["Sf"])
                            MSET("dve", ss1[:], 0.0, w=["ss1"])
                            ACT(junk[:C, :], psO[:C, 0:256], AF.Square, r=["psOo", "ss1"], w=["junk", "ss1"], accum_out=ss1[:C, 0:1])
                            ACT(rstd1[:C, :], ss1[:C, :], AF.Ln, r=["ss1"], w=["rstd1"], scale=1.0 / 256, bias=EPS)
                            ACT(rstd1[:C, :], rstd1[:C, :], AF.Exp, r=["rstd1"], w=["rstd1"], scale=-0.5)
                            TS("dve", on[:C, :], psO[:C, 0:256], rstd1[:C, 0:1], None, ALU.mult, r=["psOo", "rstd1"], w=["on"])
                    DMA("sp", mT_scr[:, 2 * j:2 * j + 2, :], mst[:, :, :], s_mst, r=["mst"], w=["mT_scr"])
            P.fence()
            with ExitStack() as scv:
                ccs = sb(scv, "ccs", [128, 413], F32)
                prod = sb(scv, "prod", [128, R + 2], F32)
                cbs = sb(scv, "cbs", [128, R], F32)
                t1 = sb(scv, "t1", [128, R], F32)
                mcv = sb(scv, "mcv", [128, R], BF16)
                s_mcv = P.new_dma_sem("s_mcv")
                MSET("dve", prod[:, 0:1], 0.0, w=["prod"])
                MSET("dve", prod[:, R + 1:R + 2], 0.0, w=["prod"])
                s_w2 = {k: P.new_dma_sem("s_w2" + k) for k in ("a", "b", "c")}
                NCG = 16 if cut >= 7 else 0
                cwsets = [((wq, slice(0, 128), "wq", s_w["q"]), (wk, slice(0, 128), "wk", s_w["k"]), (wv, slice(0, 128), "wv", s_w["v"])),
                          ((wr, slice(0, 128), "wr_a", s_w2["a"]), (wr, slice(128, 256), "wr_b", s_w2["b"]), (wv, slice(128, 256), "wv_b", s_w2["c"]))]

                def load_conv_w(cg):
                    for (wt_, sl_, key_, sem_), off_ in zip(cwsets[cg % 2], (OFF_CB, OFF_CC, OFF_CH)):
                        DMA("pool", wt_[:, :, sl_], w_inv[:, :, off_ + cg * 128:off_ + (cg + 1) * 128], sem_, w=[key_])
                if NCG:
                    load_conv_w(0)
                for cg in range(NCG):
                    if cg + 1 < NCG:
                        load_conv_w(cg + 1)
                    (wA, slA, kA, _), (wB, slB, kB, _), (wC, slC, kC, _) = cwsets[cg % 2]
                    for ti in range(5):
                        s0, e0 = TB[ti], TB[ti + 1]
                        n = e0 - s0
                        ht, hk = load_ht(s0, n)
                        proj(wA, kA, slA, ht, hk, n,
                             lambda acc, ak, s0=s0, n=n: ACT(cbs[:, s0:s0 + n], acc[:, :n], AF.Copy, r=[ak], w=["cbs"]))
                        proj(wB, kB, slB, ht, hk, n,
                             lambda acc, ak, n=n: ACT(ccs[:, :n], acc[:, :n], AF.Copy, r=[ak], w=["ccs"]))
                        proj(wC, kC, slC, ht, hk, n,
                             lambda acc, ak, s0=s0, n=n: TT("dve", prod[:, 1 + s0:1 + s0 + n], acc[:, :n], ccs[:, :n], ALU.mult,
                                                            r=[ak, "ccs"], w=["prod"]))
                    TS("dve", t1[:, :], prod[:, 0:R], cmw[:, cg, 0:1], None, ALU.mult, r=["prod", "cmw"], w=["t1"])
                    STT("dve", t1[:, :], prod[:, 1:R + 1], cmw[:, cg, 1:2], t1[:, :], ALU.mult, ALU.add, r=["prod", "cmw", "t1"], w=["t1"])
                    STT("dve", t1[:, :], prod[:, 2:R + 2], cmw[:, cg, 2:3], t1[:, :], ALU.mult, ALU.add, r=["prod", "cmw", "t1"], w=["t1"])
                    TT("dve", mcv[:, :], t1[:, :], cbs[:, :], ALU.mult, r=["t1", "cbs"], w=["mcv"])
                    DMA("sp", mT_scr[:, 16 + cg, :], mcv[:, :], s_mcv, r=["mcv"], w=["mT_scr"])
          conv_hook(flush=True)
          P.fence()

        if stages >= 3:
          with ExitStack() as sC:
            X1 = sb(sC, "X1", [128, KC, 415], F32)
            MH = sb(sC, "MH", [128, KC, 415], BF16)
            aTt = sb(sC, "aTt", [128, 11, 413], BF16)
            wring = [sb(sC, "wring%d" % i, [128, KC, 256], BF16) for i in range(3)]
            wdring = [sb(sC, "wdring%d" % i, [128, 11, 512], BF16) for i in range(2)]
            sqt = [sb(sC, "sqt%d" % i, [128, 415], BF16) for i in range(2)]
            rstdt = sb(sC, "rstdt", [128, 415], F32)
            c1 = [sb(sC, "c1_%d" % i, [128, 413], F32) for i in range(2)]
            sg_ = [sb(sC, "sg%d" % i, [128, 413], F32) for i in range(2)]
            ost = [sb(sC, "ost%d" % i, [128, 2048], F32) for i in range(2)]
            s_wr = [[P.new_dma_sem("s_wr%d_%d" % (i, h)) for h in range(2)] for i in range(3)]
            s_wd = [P.new_dma_sem("s_wd%d" % i) for i in range(2)]
            s_x1 = P.new_dma_sem("s_x1")
            s_mh = P.new_dma_sem("s_mh")
            s_ost = [P.new_dma_sem("s_ost%d" % i) for i in range(2)]
            wr_ctr = [0]
            wd_ctr = [0]
            ost_ctr = [0]
            out_dmas = []
            for ti in range(5):
                s0, e0 = TB[ti], TB[ti + 1]
                lo, hi = max(s0 - 1, 0), min(e0 + 1, R)
                N = e0 - s0 + 2
                NV = N - 2
                off = lo - (s0 - 1)
                if ti == 0:
                    MSET("dve", X1[:, :, 0:1], 0.0, w=["X1"])
                    MSET("dve", MH[:, :, 0:1], 0.0, w=["MH"])
                if ti == 4:
                    MSET("dve", X1[:, :, N - 1:N], 0.0, w=["X1"])
                    MSET("dve", MH[:, :, N - 1:N], 0.0, w=["MH"])
                DMA("sp", MH[:, :, off:off + hi - lo], mT_scr[:, :, lo:hi], s_mh, r=["mT_scr"], w=["MH"])
                DMA("sp", X1[:, :, off:off + hi - lo], xT_scr[:, :, lo:hi], s_x1, r=["xT_scr"], w=["X1"])
                for cb2 in range(16):
                    sl = wr_ctr[0] % 3
                    wr_ctr[0] += 1
                    DMA("pool", wring[sl][:, :, :], wo_scr[cb2], s_wr[sl][0], w=["wring%d" % sl])
                    for h in range(2):
                        cb = cb2 * 2 + h
                        acc = psA[cb % 2]
                        for kc in range(KC):
                            MM(acc[:, :N], wring[sl][:, kc, h * 128:(h + 1) * 128], MH[:, kc, :N], start=(kc == 0), stop=(kc == KC - 1),
                               r=["wring%d" % sl, "MH"], w=["psA%d" % (cb % 2)])
                        if cb >= 1:
                            pcb = cb - 1
                            MM(psX[:, :N], onesb[:, :], sqt[pcb % 2][:, :N], start=(pcb == 0), stop=False, r=["sqt%d" % (pcb % 2), "onesb"], w=["psX"])
                        TT("dve", X1[:, cb, :N], acc[:, :N], X1[:, cb, :N], ALU.add, r=["psA%d" % (cb % 2), "X1"], w=["X1c%d" % cb])
                        ACT(sqt[cb % 2][:, :N], X1[:, cb, :N], AF.Square, r=["X1c%d" % cb], w=["sqt%d" % (cb % 2)])
                        if cb == 31:
                            MM(psX[:, :N], onesb[:, :], sqt[cb % 2][:, :N], start=False, stop=True, r=["sqt%d" % (cb % 2), "onesb"], w=["psX"])
                xck = ["X1c%d" % cb for cb in range(32)]
                ACT(rstdt[:, :N], psX[:, :N], AF.Ln, r=["psX"], w=["rstdt"], bias=EPS)
                ACT(rstdt[:, :N], rstdt[:, :N], AF.Exp, r=["rstdt"], w=["rstdt"], scale=-0.5)
                for kc in range(KC):
                    STT("dve", MH[:, kc, :N], X1[:, kc, :N], gvec[:, 32 + kc:33 + kc], rstdt[:, :N], ALU.mult, ALU.mult,
                        r=["X1c%d" % kc, "gvec", "rstdt"], w=["MH"])
                for gi, (f0, nf) in enumerate(FGROUPS):
                    for fl in range(nf):
                        f = f0 + fl
                        sl = wr_ctr[0] % 3
                        wr_ctr[0] += 1
                        DMA("pool", wring[sl][:, :, :], wu_scr[f], s_wr[sl][0], w=["wring%d" % sl])
                        pu, pg = psA[f % 2], psG[f % 2]
                        for kc in range(KC):
                            MM(pu[:, :N], wring[sl][:, kc, 0:128], MH[:, kc, :N], start=(kc == 0), stop=(kc == KC - 1),
                               r=["wring%d" % sl, "MH"], w=["psA%d" % (f % 2)])
                        for kc in range(KC):
                            MM(pg[:, :N], wring[sl][:, kc, 128:256], MH[:, kc, :N], start=(kc == 0), stop=(kc == KC - 1),
                               r=["wring%d" % sl, "MH"], w=["psG%d" % (f % 2)])
                        cc1 = c1[f % 2]
                        TS("dve", cc1[:, :NV], pg[:, 0:NV], fcw[:, f, 0:1], fcw[:, f, 3:4], ALU.mult, ALU.add,
                           r=["psG%d" % (f % 2), "fcw"], w=["c1_%d" % (f % 2)])
                        STT("dve", cc1[:, :NV], pg[:, 1:NV + 1], fcw[:, f, 1:2], cc1[:, :NV], ALU.mult, ALU.add,
                            r=["psG%d" % (f % 2), "fcw", "c1_%d" % (f % 2)], w=["c1_%d" % (f % 2)])
                        STT("dve", cc1[:, :NV], pg[:, 2:NV + 2], fcw[:, f, 2:3], cc1[:, :NV], ALU.mult, ALU.add,
                            r=["psG%d" % (f % 2), "fcw", "c1_%d" % (f % 2)], w=["c1_%d" % (f % 2)])
                        ACT(sg_[f % 2][:, :NV], cc1[:, :NV], AF.Silu, r=["c1_%d" % (f % 2)], w=["sg%d" % (f % 2)])
                        TT("dve", aTt[:, fl, :NV], sg_[f % 2][:, :NV], pu[:, 1:NV + 1], ALU.mult,
                           r=["sg%d" % (f % 2), "psA%d" % (f % 2)], w=["aTt"])
                    for cb4 in range(8):
                        dl = wd_ctr[0] % 2
                        wd_ctr[0] += 1
                        DMA("pool", wdring[dl][:, :nf, :], wd_scr[gi, cb4, :, 0:nf, :], s_wd[dl], w=["wdring%d" % dl])
                        for h in range(4):
                            cb = cb4 * 4 + h
                            yps = psS if cb % 2 == 0 else psO
                            yk = "psS" if cb % 2 == 0 else "psO"
                            for fl in range(nf):
                                MM(yps[:, :NV], wdring[dl][:, fl, h * 128:(h + 1) * 128], aTt[:, fl, :NV], start=(fl == 0), stop=(fl == nf - 1),
                                   r=["wdring%d" % dl, "aTt"], w=[yk])
                            TT("dve", X1[:, cb, 1:NV + 1], yps[:, :NV], X1[:, cb, 1:NV + 1], ALU.add, r=[yk, "X1c%d" % cb], w=["X1c%d" % cb])
                for cb in range(32):
                    ACT(sqt[cb % 2][:, :NV], X1[:, cb, 1:NV + 1], AF.Square, r=["X1c%d" % cb], w=["sqt%d" % (cb % 2)])
                    MM(psX[:, :NV], onesb[:, :], sqt[cb % 2][:, :NV], start=(cb == 0), stop=(cb == 31), r=["sqt%d" % (cb % 2), "onesb"], w=["psX"])
                ACT(rstdt[:, :NV], psX[:, :NV], AF.Ln, r=["psX"], w=["rstdt"], bias=EPS)
                ACT(rstdt[:, :NV], rstdt[:, :NV], AF.Exp, r=["rstdt"], w=["rstdt"], scale=-0.5)
                for kc in range(KC):
                    STT("dve", X1[:, kc, 1:NV + 1], X1[:, kc, 1:NV + 1], gvec[:, 64 + kc:65 + kc], rstdt[:, :NV], ALU.mult, ALU.mult,
                        r=["X1c%d" % kc, "gvec", "rstdt"], w=["X1c%d" % kc])
                for c in range(ti * 4, ti * 4 + 4):
                    cs, C = CHUNKS[c]
                    lc = cs - s0 + 1
                    for half in range(2):
                        ol = ost_ctr[0] % 2
                        ost_ctr[0] += 1
                        for g4 in range(4):
                            bank = psG[g4 % 2]
                            for jj in range(4):
                                kc = half * 16 + g4 * 4 + jj
                                TR(bank[:C, jj * 128:(jj + 1) * 128], X1[:, kc, lc:lc + C], ident[:, :],
                                   r=["X1c%d" % kc, "consts"], w=["psG%d" % (g4 % 2)])
                            ACT(ost[ol][:C, g4 * 512:(g4 + 1) * 512], bank[:C, :], AF.Copy, r=["psG%d" % (g4 % 2)], w=["ost%d" % ol])
                        out_dmas.append(DMA("sp", y[cs:cs + C, half * 2048:(half + 1) * 2048], ost[ol][:C, :], s_ost[ol], r=["ost%d" % ol]))
                P.op("dve", lambda e: e.memset(rstdt[:, 0:1], 0.0), reads=xck + ["MH"], writes=["X1", "MH", "rstdt"] + xck)
          P.fence()
        else:
            P.fence()
        P.barrier_wait("sp", list(P.ops["sp"][-1].deps))
        block = es.enter_context(nc.Block())
        stats = P.emit(block)
    return nc, stats


def _consts():
    a = np.arange(128)[:, None]
    b = np.arange(128)[None, :]
    c = np.zeros((128, 7, 128), np.float32)
    c[:, 0] = (a == b)
    c[:, 1] = (a <= b) * (-1.0 / 16.0)
    c[:, 2] = (a >= b) * (-1.0 / 16.0)
    c[:, 3] = (a > b) * (-1.0 / 16.0)
    c[:, 4] = (a < b) * (-1.0 / 16.0)
    c[:, 5] = (a <= b)
    c[:, 6] = (a > b)
    return c.reshape(128, 7 * 128)


def _prepare_inputs(x_prompt, x_sample, meta_tokens, mix_norm_g, w_in, w_gate2, b_gate2, head_norm_g,
                    conv_mix_w, w_out, ffn_norm_g, w_up, ffn_conv_w, ffn_conv_b, w_down, final_norm_g):
    f = lambda a: np.ascontiguousarray(np.asarray(a, dtype=np.float32))
    x_prompt, x_sample, meta = f(x_prompt), f(x_sample), f(meta_tokens)
    w_in0, w_out0, w_up0, w_down0 = f(w_in[0]), f(w_out[0]), f(w_up[0]), f(w_down[0])
    wg2, bg2 = f(w_gate2[0]), f(b_gate2[0])
    gvec = np.concatenate([f(mix_norm_g[0]).reshape(32, 128).T, f(ffn_norm_g[0]).reshape(32, 128).T,
                           f(final_norm_g).reshape(32, 128).T], axis=1)
    ghead = f(head_norm_g[0]).reshape(2, 128).T
    cmw = f(conv_mix_w[0]).reshape(3, 16, 128).transpose(2, 1, 0).reshape(128, 48)
    fcw = np.concatenate([f(ffn_conv_w[0]), f(ffn_conv_b[0])[None]], axis=0).reshape(4, NF, 128).transpose(2, 1, 0).reshape(128, NF * 4)
    consts = _consts()
    shared = dict(w_in=w_in0, w_out=w_out0, w_up=w_up0, w_down=w_down0,
                  gvec=f(gvec), ghead=f(ghead), cmw=f(cmw), fcw=f(fcw), consts=consts)

    def core(xown, xext, ext_dir, flag_f, flag_b):
        wgin = np.zeros((D, 80), np.float32)
        wgin[:, 0:16] = w_in0[:, 6144:6160]
        wgin[:, 32:48] = w_in0[:, 6160:6176]
        wgin[:, 64:80] = w_in0[:, 6144 + 16 * ext_dir:6160 + 16 * ext_dir]
        wgb = np.zeros((96, 8, 256), np.float32)
        wgb[0:16, :, 0:128] = wg2[0].reshape(16, 8, 128)
        wgb[16, :, 0:128] = bg2[0].reshape(8, 128)
        wgb[32:48, :, 128:256] = wg2[1].reshape(16, 8, 128)
        wgb[48, :, 128:256] = bg2[1].reshape(8, 128)
        wgb[64:80, :, 0:128] = wg2[ext_dir].reshape(16, 8, 128)
        wgb[80, :, 0:128] = bg2[ext_dir].reshape(8, 128)
        wgb = wgb.reshape(96, 2048)
        flags = np.zeros((128, 2), np.float32)
        flags[:, 0] = flag_f
        flags[:, 1] = flag_b
        m = dict(shared)
        m.update(xown=f(xown), xext=f(xext), wgin=wgin, wgb=wgb, flags=flags)
        return m

    maps = []
    for b in range(2):
        xa = np.concatenate([meta, x_prompt[b, 0:2048]], axis=0)
        ea = x_prompt[b, 2048:4096][::-1]
        maps.append(core(xa, ea, 1, 0.0, 1.0))
        xb = x_prompt[b, 2032:4096]
        eb = np.concatenate([meta, x_prompt[b, 0:2032]], axis=0)
        maps.append(core(xb, eb, 0, 1.0, 0.0))
    for s in range(4):
        xs = np.concatenate([meta, x_sample[s]], axis=0)
        maps.append(core(xs, np.zeros((E, D), np.float32), 0, 0.0, 0.0))
    return maps


_NC_CACHE = {}


def kernel(x_prompt, x_sample, meta_tokens, mix_norm_g, w_in, w_gate2, b_gate2, head_norm_g,
           conv_mix_w, w_out, ffn_norm_g, w_up, ffn_conv_w, ffn_conv_b, w_down, final_norm_g):
    maps = _prepare_inputs(x_prompt, x_sample, meta_tokens, mix_norm_g, w_in, w_gate2, b_gate2, head_norm_g,
                           conv_mix_w, w_out, ffn_norm_g, w_up, ffn_conv_w, ffn_conv_b, w_down, final_norm_g)
    if "nc" not in _NC_CACHE:
        _NC_CACHE["nc"] = build_program()[0]
    nc = _NC_CACHE["nc"]
    res = run_bass_kernel_spmd(nc, maps, core_ids=list(range(8)))
    ys = [np.asarray(r["y"], dtype=np.float32) for r in res.results]
    y_prompt = np.zeros((2, 4096, D), np.float32)
    y_sample = np.zeros((4, 2048, D), np.float32)
    for b in range(2):
        y_prompt[b, 0:2040] = ys[2 * b][16:16 + 2040]
        y_prompt[b, 2040:4096] = ys[2 * b + 1][8:2064]
    for s in range(4):
        y_sample[s] = ys[4 + s][16:2064]
    return (y_prompt, y_sample)
```

```python
import os
import numpy as np
from contextlib import ExitStack
import concourse.bass as bass
import concourse.mybir as mybir
from concourse.bass_utils import run_bass_kernel_spmd

F32 = mybir.dt.float32
BF16 = mybir.dt.bfloat16
AF = mybir.ActivationFunctionType
ALU = mybir.AluOpType

D = 4096
KC = 32
R = 2064
E = 2048
DFF = 11008
NF = 86
EPS = 1e-6
TB = [0, 413, 826, 1239, 1652, 2064]
ET = 256
CHUNKS = []
for _i in range(5):
    _s, _e = TB[_i], TB[_i + 1]
    _w = _e - _s
    _c0 = _w - 309
    CHUNKS.append((_s, _c0))
    for _k in range(3):
        CHUNKS.append((_s + _c0 + 103 * _k, 103))
FGROUPS = []
_f = 0
for _n in [11, 11, 11, 11, 11, 11, 10, 10]:
    FGROUPS.append((_f, _n))
    _f += _n
OFF_Q, OFF_K, OFF_V, OFF_R, OFF_CB, OFF_CC, OFF_CH = 0, 1024, 2048, 4096, 6176, 8224, 10272


class _Op:
    __slots__ = ("eng", "fn", "deps", "is_dma", "sem", "val", "signal")

    def __init__(self, eng, fn, is_dma=False):
        self.eng = eng
        self.fn = fn
        self.deps = []
        self.is_dma = is_dma
        self.sem = None
        self.val = 0
        self.signal = False


class Prog:
    ENGS = ("pe", "act", "dve", "pool", "sp")

    def __init__(self, nc, es):
        self.nc = nc
        self.es = es
        self.ops = {e: [] for e in self.ENGS}
        self.last_w = {}
        self.readers = {}
        self.eng_sem = {e: es.enter_context(nc.semaphore("prog_" + e)) for e in self.ENGS}
        self.dma_sem_count = {}
        self.pending_dmas = []

    def new_dma_sem(self, name):
        s = self.es.enter_context(self.nc.semaphore(name))
        self.dma_sem_count[id(s)] = 0
        return s

    def _collect(self, op, reads, writes):
        deps = []
        for k in reads:
            w = self.last_w.get(k)
            if w is not None:
                deps.append(w)
        for k in writes:
            w = self.last_w.get(k)
            if w is not None:
                deps.append(w)
            deps.extend(self.readers.get(k, ()))
        seen = set()
        for d in deps:
            if d is op or id(d) in seen:
                continue
            seen.add(id(d))
            if (not d.is_dma) and (not op.is_dma) and d.eng == "pe" and op.eng == "pe":
                continue
            op.deps.append(d)
            if not d.is_dma:
                d.signal = True
        for k in writes:
            self.last_w[k] = op
            self.readers[k] = []
        for k in reads:
            self.readers.setdefault(k, []).append(op)

    def op(self, eng, fn, reads=(), writes=()):
        pr = [k for k in reads if k.startswith("ps")]
        if pr:
            reads = [k for k in reads if not k.startswith("ps")]
            writes = list(writes) + [k for k in pr if k not in writes]
        o = _Op(eng, fn)
        self._collect(o, reads, writes)
        self.ops[eng].append(o)
        return o

    def dma(self, eng, fn, sem, reads=(), writes=()):
        o = _Op(eng, fn, is_dma=True)
        o.sem = sem
        self.dma_sem_count[id(sem)] += 16
        o.val = self.dma_sem_count[id(sem)]
        self._collect(o, reads, writes)
        self.ops[eng].append(o)
        self.pending_dmas.append(o)
        return o

    def barrier_wait(self, eng, deps):
        o = _Op(eng, None)
        for d in deps:
            o.deps.append(d)
            if not d.is_dma:
                d.signal = True
        self.ops[eng].append(o)
        return o

    def fence(self):
        lasts = []
        for e in self.ENGS:
            for o in reversed(self.ops[e]):
                if (not o.is_dma) and o.fn is not None:
                    lasts.append(o)
                    break
        deps = lasts + self.pending_dmas
        for e in self.ENGS:
            self.barrier_wait(e, deps)
        self.pending_dmas = []
        self.last_w = {}
        self.readers = {}

    def emit(self, block):
        for e in self.ENGS:
            c = 0
            for o in self.ops[e]:
                if o.is_dma:
                    continue
                if o.signal:
                    c += 1
                    o.sem = self.eng_sem[e]
                    o.val = c
        stats = {}

        def run(e, engine):
            known = {}
            nw = 0
            for o in self.ops[e]:
                need = {}
                for d in o.deps:
                    key = id(d.sem)
                    if need.get(key, (None, 0))[1] < d.val:
                        need[key] = (d.sem, d.val)
                for key, (sem, val) in need.items():
                    if known.get(key, 0) < val:
                        engine.wait_ge(sem, val)
                        known[key] = val
                        nw += 1
                if o.fn is None:
                    continue
                ins = o.fn(engine)
                if o.is_dma:
                    ins.then_inc(o.sem, 16)
                elif o.signal:
                    ins.then_inc(o.sem, 1)
            stats[e] = (len(self.ops[e]), nw)

        @block.tensor
        def _(t):
            run("pe", t)

        @block.scalar
        def _(s):
            run("act", s)

        @block.vector
        def _(v):
            run("dve", v)

        @block.gpsimd
        def _(g):
            run("pool", g)

        @block.sync
        def _(s):
            run("sp", s)
        return stats


def build_program(debug=False, stages=3, cut=99):
    nc = bass.Bass("TRN2", target_bir_lowering=False)
    din = lambda name, shape: nc.dram_tensor(name, shape, F32, kind="ExternalInput").ap()
    xown = din("xown", [R, D])
    xext = din("xext", [E, D])
    w_in = din("w_in", [D, 12320])
    w_out = din("w_out", [D, D])
    w_up = din("w_up", [D, 2 * DFF])
    w_down = din("w_down", [DFF, D])
    wgin = din("wgin", [D, 80])
    wgb = din("wgb", [96, 2048])
    gvec_d = din("gvec", [128, 96])
    ghead_d = din("ghead", [128, 2])
    cmw_d = din("cmw", [128, 48])
    fcw_d = din("fcw", [128, NF * 4])
    consts_d = din("consts", [128, 7 * 128])
    flags_d = din("flags", [128, 2])
    y = nc.dram_tensor("y", [R, D], F32, kind="ExternalOutput").ap()
    skind = "ExternalOutput" if debug else "Internal"
    hT_scr = nc.dram_tensor("hT_scr", [128, KC, R + E], BF16, kind=skind).ap()
    xT_scr = nc.dram_tensor("xT_scr", [128, KC, R], F32, kind=skind).ap()
    mT_scr = nc.dram_tensor("mT_scr", [128, KC, R], BF16, kind=skind).ap()

    wo_scr = nc.dram_tensor("wo_scr", [16, 128, KC, 256], BF16).ap()
    wu_scr = nc.dram_tensor("wu_scr", [NF, 128, KC, 256], BF16).ap()
    wd_scr = nc.dram_tensor("wd_scr", [8, 8, 128, 11, 512], BF16).ap()
    w_inv = w_in.rearrange("(kc p) c -> p kc c", p=128)
    w_outv = w_out.rearrange("(kc p) c -> p kc c", p=128)
    w_upv = w_up.rearrange("(kc p) c -> p kc c", p=128)
    w_downv = w_down.rearrange("(f p) c -> p f c", p=128)
    wginv = wgin.rearrange("(kc p) c -> p kc c", p=128)

    with ExitStack() as es:
        P = Prog(nc, es)
        sb = lambda ctx, name, shape, dt: ctx.enter_context(nc.sbuf_tensor(name, shape, dt))
        psA = [es.enter_context(nc.psum_tensor("psA%d" % i, [128, 512], F32)) for i in range(2)]
        psG = [es.enter_context(nc.psum_tensor("psG%d" % i, [128, 512], F32)) for i in range(2)]
        psS = es.enter_context(nc.psum_tensor("psS", [128, 512], F32))
        psO = es.enter_context(nc.psum_tensor("psO", [128, 512], F32))
        psT = es.enter_context(nc.psum_tensor("psT", [128, 1024], BF16))
        psX = es.enter_context(nc.psum_tensor("psX", [128, 512], F32))

        def MM(out, lhsT, rhs, start=True, stop=True, r=(), w=()):
            return P.op("pe", lambda e: e.matmul(out, lhsT=lhsT, rhs=rhs, start=start, stop=stop), r, w)

        def TR(out, in_, ident, r=(), w=()):
            return P.op("pe", lambda e: e.transpose(out=out, in_=in_, identity=ident), r, w)

        def ACT(out, in_, func, r=(), w=(), **kw):
            return P.op("act", lambda e: e.activation(out=out, in_=in_, func=func, **kw), r, w)

        def TT(eng, out, in0, in1, op, r=(), w=()):
            return P.op(eng, lambda e: e.tensor_tensor(out=out, in0=in0, in1=in1, op=op), r, w)

        def TS(eng, out, in0, s1, s2, op0, op1=None, r=(), w=()):
            if op1 is None:
                return P.op(eng, lambda e: e.tensor_scalar(out=out, in0=in0, scalar1=s1, scalar2=None, op0=op0), r, w)
            return P.op(eng, lambda e: e.tensor_scalar(out=out, in0=in0, scalar1=s1, scalar2=s2, op0=op0, op1=op1), r, w)

        def STT(eng, out, in0, scalar, in1, op0, op1, r=(), w=()):
            return P.op(eng, lambda e: e.scalar_tensor_tensor(out=out, in0=in0, scalar=scalar, in1=in1, op0=op0, op1=op1), r, w)

        def CP(eng, out, in_, r=(), w=()):
            return P.op(eng, lambda e: e.tensor_copy(out=out, in_=in_), r, w)

        def MSET(eng, ap, val, r=(), w=()):
            return P.op(eng, lambda e: e.memset(ap, val), r, w)

        def DMA(eng, out, in_, sem, r=(), w=()):
            return P.dma(eng, lambda e: e.dma_start(out=out, in_=in_), sem, r, w)

        conv_jobs = []
        for cb2 in range(16):
            conv_jobs.append((wo_scr[cb2], w_outv[:, :, cb2 * 256:(cb2 + 1) * 256]))
        for f in range(NF):
            conv_jobs.append((wu_scr[f, :, :, 0:128], w_upv[:, :, f * 128:(f + 1) * 128]))
            conv_jobs.append((wu_scr[f, :, :, 128:256], w_upv[:, :, DFF + f * 128:DFF + (f + 1) * 128]))
        for gi, (f0, nf) in enumerate(FGROUPS):
            for cb4 in range(8):
                conv_jobs.append((wd_scr[gi, cb4, :, 0:nf, :], w_downv[:, f0:f0 + nf, cb4 * 512:(cb4 + 1) * 512]))
        conv_sems = [P.new_dma_sem("s_cv%d" % i) for i in range(4)]
        conv_state = [0, 0.0]
        n_hooks = 13 + 8 * 13 + 16 * 5
        per_hook = len(conv_jobs) / float(n_hooks) + 0.01

        def conv_hook(flush=False):
            conv_state[1] += per_hook
            while conv_state[0] < len(conv_jobs) and (flush or conv_state[0] < conv_state[1]):
                o_ap, i_ap = conv_jobs[conv_state[0]]
                DMA("pool", o_ap, i_ap, conv_sems[conv_state[0] % 4], w=["cv%d" % (conv_state[0] % 4)])
                conv_state[0] += 1

        consts = sb(es, "consts_sb", [128, 7, 128], F32)
        identb = sb(es, "identb", [128, 128], BF16)
        onesb = sb(es, "onesb", [128, 128], BF16)
        gvec = sb(es, "gvecs", [128, 96], F32)
        ghead = sb(es, "gheads", [128, 2], F32)
        cmw = sb(es, "cmws", [128, 16, 3], F32)
        fcw = sb(es, "fcws", [128, NF, 4], F32)
        flags = sb(es, "flagss", [128, 2], F32)
        sc = [P.new_dma_sem("sc%d" % i) for i in range(6)]
        DMA("sp", consts[:].rearrange("p a b -> p (a b)"), consts_d[:, :], sc[0], w=["consts"])
        DMA("sp", gvec[:], gvec_d[:, :], sc[1], w=["gvec"])
        DMA("sp", ghead[:], ghead_d[:, :], sc[2], w=["ghead"])
        DMA("sp", cmw[:].rearrange("p a b -> p (a b)"), cmw_d[:, :], sc[3], w=["cmw"])
        DMA("sp", fcw[:].rearrange("p a b -> p (a b)"), fcw_d[:, :], sc[4], w=["fcw"])
        DMA("sp", flags[:], flags_d[:, :], sc[5], w=["flags"])
        CP("dve", identb[:], consts[:, 0, :], r=["consts"], w=["identb"])
        MSET("dve", onesb[:], 1.0 / 4096.0, w=["onesb"])
        ident = consts[:, 0, :]
        Mle, Mge, Mgt, Mlt = consts[:, 1, :], consts[:, 2, :], consts[:, 3, :], consts[:, 4, :]
        masks = consts[:, 5:7, :]

        with ExitStack() as sa:
            xin = [sb(sa, "xin%d" % i, [128, D], F32) for i in range(2)]
            xs = [sb(sa, "xs%d" % i, [128, D], F32) for i in range(2)]
            xTs = [sb(sa, "xTs%d" % i, [128, KC, 128], F32) for i in range(2)]
            hTs = [sb(sa, "hTs%d" % i, [128, KC, 128], BF16) for i in range(2)]
            junkA = sb(sa, "junkA", [128, D], BF16)
            ssA = [sb(sa, "ssA%d" % i, [128, 1], F32) for i in range(2)]
            rsA = [sb(sa, "rsA%d" % i, [128, 1], F32) for i in range(2)]
            s_xin = [P.new_dma_sem("s_xin%d" % i) for i in range(2)]
            s_xst = [P.new_dma_sem("s_xst%d" % i) for i in range(2)]
            s_hst = [P.new_dma_sem("s_hst%d" % i) for i in range(2)]
            blocks = [(xown, i * 128, 128, True, i * 128) for i in range(16)] + [(xown, 2048, 16, True, 2048)]
            blocks += [(xext, i * 128, 128, False, R + i * 128) for i in range(16)]
            def load_blk(bi):
                src_, r0_, nr_, own_, c0_ = blocks[bi]
                DMA("sp", xin[bi % 2][:nr_, :], src_[r0_:r0_ + nr_, :], s_xin[bi % 2], w=["xin%d" % (bi % 2)])
            load_blk(0)
            for bi, (src, r0, nr, own, c0) in enumerate(blocks):
                sl = bi % 2
                MSET("dve", ssA[sl][:], 0.0, w=["ssA%d" % sl])
                ACT(junkA[:nr, :], xin[sl][:nr, :], AF.Square, r=["xin%d" % sl, "ssA%d" % sl], w=["junkA", "ssA%d" % sl],
                    accum_out=ssA[sl][:nr, 0:1])
                ACT(rsA[sl][:nr, :], ssA[sl][:nr, :], AF.Ln, r=["ssA%d" % sl], w=["rsA%d" % sl], scale=1.0 / D, bias=EPS)
                ACT(rsA[sl][:nr, :], rsA[sl][:nr, :], AF.Exp, r=["rsA%d" % sl], w=["rsA%d" % sl], scale=-0.5)
                TS("dve", xs[sl][:nr, :], xin[sl][:nr, :], rsA[sl][:nr, 0:1], None, ALU.mult, r=["xin%d" % sl, "rsA%d" % sl], w=["xs%d" % sl])
                for g in range(8):
                    if own:
                        bank = psA[g % 2]
                        for j in range(4):
                            kc = g * 4 + j
                            TR(bank[:, j * 128:j * 128 + nr], xin[sl][:nr, kc * 128:(kc + 1) * 128], ident[:nr, :nr],
                               r=["xin%d" % sl, "consts"], w=["psA%d" % (g % 2)])
                        ACT(xTs[sl][:, g * 4:(g + 1) * 4, :nr], bank[:].rearrange("p (a b) -> p a b", a=4)[:, :, :nr], AF.Copy,
                            r=["psA%d" % (g % 2)], w=["xTs%d_%d" % (sl, g)])
                    bank2 = psG[g % 2]
                    for j in range(4):
                        kc = g * 4 + j
                        TR(bank2[:, j * 128:j * 128 + nr], xs[sl][:nr, kc * 128:(kc + 1) * 128], ident[:nr, :nr],
                           r=["xs%d" % sl, "consts"], w=["psG%d" % (g % 2)])
                    TT("dve", hTs[sl][:, g * 4:(g + 1) * 4, :nr], bank2[:].rearrange("p (a b) -> p a b", a=4)[:, :, :nr],
                       gvec[:, g * 4:(g + 1) * 4].unsqueeze(2).to_broadcast([128, 4, nr]), ALU.mult,
                       r=["psG%d" % (g % 2), "gvec"], w=["hTs%d_%d" % (sl, g)])
                hk = ["hTs%d_%d" % (sl, g) for g in range(8)]
                if bi + 1 < len(blocks):
                    load_blk(bi + 1)
                DMA("pool", hT_scr[:, :, c0:c0 + nr], hTs[sl][:, :, :nr], s_hst[sl], r=hk, w=["hT_scr"])
                if own:
                    xk = ["xTs%d_%d" % (sl, g) for g in range(8)]
                    DMA("sp", xT_scr[:, :, r0:r0 + nr], xTs[sl][:, :, :nr], s_xst[sl], r=xk, w=["xT_scr"])
        P.fence()

        if stages >= 2:
          with ExitStack() as sB:
            wq = sb(sB, "wq", [128, KC, 128], BF16)
            wk = sb(sB, "wk", [128, KC, 128], BF16)
            wv = sb(sB, "wv", [128, KC, 256], BF16)
            wr = sb(sB, "wr", [128, KC, 256], BF16)
            hts = [sb(sB, "hts%d" % i, [128, KC, 413], BF16) for i in range(2)]
            s_w = {k: P.new_dma_sem("s_w" + k) for k in ("q", "k", "v", "r")}
            s_ht = [P.new_dma_sem("s_ht%d" % i) for i in range(2)]
            s_mst = P.new_dma_sem("s_mst")
            ht_ctr = [0]

            def load_ht(c0, n):
                sl = ht_ctr[0] % 2
                ht_ctr[0] += 1
                conv_hook()
                DMA("sp", hts[sl][:, :, :n], hT_scr[:, :, c0:c0 + n], s_ht[sl], r=["hT_scr"], w=["hts%d" % sl])
                return hts[sl], "hts%d" % sl

            acc_ctr = [0]

            def proj(wt, wkey, wcols, ht, htkey, n, evac, M=128):
                i = acc_ctr[0] % 2
                acc_ctr[0] += 1
                acc = psA[i]
                for kc in range(KC):
                    MM(acc[:M, :n], wt[:, kc, wcols], ht[:, kc, :n], start=(kc == 0), stop=(kc == KC - 1),
                       r=[wkey, htkey], w=["psA%d" % i])
                evac(acc, "psA%d" % i)

            with ExitStack() as sg:
                wg_sb = sb(sg, "wg_sb", [128, KC, 80], BF16)
                qT = sb(sg, "qT", [128, R], BF16)
                kT = sb(sg, "kT", [128, R], BF16)
                rs = sb(sg, "rs", [128, 2, R], BF16)
                vt = sb(sg, "vt", [128, 2, 413], BF16)
                kte = sb(sg, "kte", [128, ET], BF16)
                kv = sb(sg, "kv", [128, 20, 384], BF16)
                kve = sb(sg, "kve", [128, 2, 384], BF16)
                kvep = [kve, sb(sg, "kve_b", [128, 2, 384], BF16)]
                Sbb = sb(sg, "Sbb", [128, 20, 256], BF16)
                aT = sb(sg, "aT", [96, R], F32)
                wgb_sb = sb(sg, "wgb_sb", [96, 2048], F32)
                mst = sb(sg, "mst", [128, 2, R], BF16)
                ex = sb(sg, "ex", [128, 256], F32)
                lsb = sb(sg, "lsb", [128, 256], F32)
                E1 = sb(sg, "E1", [128, 256], F32)
                E2 = sb(sg, "E2", [128, 256], F32)
                E3 = sb(sg, "E3", [128, 256], F32)
                qq = sb(sg, "qq", [128, 2, 128], BF16)
                qqp = [sb(sg, "qqp%d" % i, [128, 2, 128], BF16) for i in range(2)]
                khp = [sb(sg, "khp%d" % i, [128, 128], BF16) for i in range(2)]
                E1p = [sb(sg, "E1p%d" % i, [128, 256], F32) for i in range(2)]
                kk = sb(sg, "kk", [128, 2, 128], BF16)
                khf = sb(sg, "khf", [128, 128], BF16)
                khb = sb(sg, "khb", [128, 128], BF16)
                PT = sb(sg, "PT", [128, 2, 128], BF16)
                on = sb(sg, "on", [128, 256], BF16)
                junk = sb(sg, "junk", [128, 256], BF16)
                Sf = sb(sg, "Sf", [128, 256], F32)
                Sb_ = sb(sg, "Sb_", [128, 256], F32)
                Se = sb(sg, "Se", [128, 256], F32)
                Sfb = sb(sg, "Sfb", [128, 256], BF16)
                ss1 = sb(sg, "ss1", [128, 1], F32)
                rstd1 = sb(sg, "rstd1", [128, 1], F32)
                s_wg = P.new_dma_sem("s_wg")
                s_wgb = P.new_dma_sem("s_wgb")

                DMA("pool", wg_sb[:], wginv[:, :, :], s_wg, w=["wg_sb"])
                DMA("sp", wgb_sb[:], wgb[:, :], s_wgb, w=["wgb_sb"])
                MSET("dve", aT[:, :], 1.0, w=["aT"])

                for ti in range(5):
                    s0, e0 = TB[ti], TB[ti + 1]
                    n = e0 - s0
                    ht, hk = load_ht(s0, n)

                    def ev(acc, ak, s0=s0, n=n):
                        ACT(aT[0:16, s0:s0 + n], acc[0:16, :n], AF.Copy, r=[ak], w=["aT"])
                        ACT(aT[32:48, s0:s0 + n], acc[32:48, :n], AF.Copy, r=[ak], w=["aT"])
                    proj(wg_sb, "wg_sb", slice(0, 80), ht, hk, n, ev, M=80)
                for et in range(E // ET):
                    ht, hk = load_ht(R + et * ET, ET)

                    def ev(acc, ak, et=et):
                        ACT(aT[64:80, et * ET:(et + 1) * ET], acc[64:80, :ET], AF.Copy, r=[ak], w=["aT"])
                    proj(wg_sb, "wg_sb", slice(0, 80), ht, hk, ET, ev, M=80)

                def gates_decays(cols0, C, j, frow, do_b):
                    if do_b:
                        MM(psG[0][:C, 0:256], aT[0:64, cols0:cols0 + C], wgb_sb[0:64, j * 256:(j + 1) * 256],
                           r=["aT", "wgb_sb"], w=["psG0"])
                        W = 256
                    else:
                        MM(psG[0][:C, 0:128], aT[frow:frow + 32, cols0:cols0 + C], wgb_sb[frow:frow + 32, j * 256:j * 256 + 128],
                           r=["aT", "wgb_sb"], w=["psG0"])
                        W = 128
                    ACT(ex[:C, :W], psG[0][:C, :W], AF.Exp, r=["psG0"], w=["ex"], scale=-1.0)
                    ACT(lsb[:C, :W], ex[:C, :W], AF.Ln, r=["ex"], w=["lsb"], bias=1.0)
                    MM(psG[1][:, 0:C], lsb[:C, 0:128], Mle[:C, :C], r=["lsb", "consts"], w=["psG1"])
                    MM(psG[1][:C, 256:384], Mgt[:C, :C], lsb[:C, 0:128], r=["lsb", "consts"], w=["psG1"])
                    if do_b:
                        MM(psG[1][:, 128:128 + C], lsb[:C, 128:256], Mge[:C, :C], r=["lsb", "consts"], w=["psG1"])
                        MM(psG[1][:C, 384:512], Mlt[:C, :C], lsb[:C, 128:256], r=["lsb", "consts"], w=["psG1"])

                def load_head_w(jj):
                    DMA("pool", wk[:], w_inv[:, :, OFF_K + jj * 128:OFF_K + (jj + 1) * 128], s_w["k"], w=["wk"])
                    DMA("pool", wv[:], w_inv[:, :, OFF_V + jj * 256:OFF_V + (jj + 1) * 256], s_w["v"], w=["wv"])
                    DMA("pool", wq[:], w_inv[:, :, OFF_Q + jj * 128:OFF_Q + (jj + 1) * 128], s_w["q"], w=["wq"])
                    DMA("pool", wr[:], w_inv[:, :, OFF_R + jj * 256:OFF_R + (jj + 1) * 256], s_w["r"], w=["wr"])

                NH = 8 if cut >= 6 else (1 if cut >= 2 else 0)
                for j in range(NH):
                    load_head_w(j)
                    MSET("dve", Se[:], 0.0, w=["Se"])
                    NE = E // 128

                    def ext_step(t):
                        if 1 <= t <= NE:
                            e = t - 1
                            p = e % 2
                            kvt = kvep[(e // 2) % 2]
                            ci = e % 2
                            MM(psG[1][:, 0:128], lsb[:128, 0:128], Mle[:, :], r=["lsb", "consts"], w=["psG1"])
                            MM(psG[1][:128, 256:384], Mgt[:, :], lsb[:128, 0:128], r=["lsb", "consts"], w=["psG1"])
                            ACT(E1p[p][:, 0:128], psG[1][:, 0:128], AF.Exp, r=["psG1"], w=["E1_%d" % p])
                            ACT(E3[:, 0:128], psG[1][:, 256:384], AF.Exp, r=["psG1"], w=["E3"])
                            TT("dve", khp[p][:, :], kvt[:, ci, 0:128], E3[:, 0:128], ALU.mult,
                               r=["kve%d_%d" % ((e // 2) % 2, ci), "E3"], w=["kh%d" % p])
                        if t < NE:
                            ec = t * 128
                            MM(psG[0][:128, 0:128], aT[64:96, ec:ec + 128], wgb_sb[64:96, j * 256:j * 256 + 128],
                               r=["aT", "wgb_sb"], w=["psG0"])
                            ACT(ex[:, :128], psG[0][:, :128], AF.Exp, r=["psG0"], w=["ex"], scale=-1.0)
                            ACT(lsb[:, :128], ex[:, :128], AF.Ln, r=["ex"], w=["lsb"], bias=1.0)
                        if 2 <= t < NE + 2:
                            e = t - 2
                            p = e % 2
                            kvt = kvep[(e // 2) % 2]
                            ci = e % 2
                            MM(psX[:, 0:256], khp[p][:, :], kvt[:, ci, 128:384], r=["kh%d" % p, "kve%d_%d" % ((e // 2) % 2, ci)], w=["psX"])
                            STT("dve", Se[:], Se[:], E1p[p][:, 127:128], psX[:, 0:256], ALU.mult, ALU.add,
                                r=["Se", "E1_%d" % p, "psX"], w=["Se"])

                    ext_step(0)
                    for et in range(E // ET):
                        ht, hk = load_ht(R + et * ET, ET)
                        proj(wk, "wk", slice(0, 128), ht, hk, ET,
                             lambda acc, ak: ACT(kte[:, :ET], acc[:, :ET], AF.Copy, r=[ak], w=["kte"]))
                        for h in range(2):
                            proj(wv, "wv", slice(h * 128, (h + 1) * 128), ht, hk, ET,
                                 lambda acc, ak, h=h: ACT(vt[:, h, :ET], acc[:, :ET], AF.Copy, r=[ak], w=["vt%d" % h]))
                        for ci in range(ET // 128):
                            lc = ci * 128
                            TR(psT[:, 0:128], kte[:, lc:lc + 128], identb[:, :], r=["kte", "identb"], w=["psT"])
                            for h in range(2):
                                TR(psT[:, 128 + h * 128:256 + h * 128], vt[:, h, lc:lc + 128], identb[:, :],
                                   r=["vt%d" % h, "identb"], w=["psT"])
                            CP("dve", kvep[et % 2][:, ci, :], psT[:, 0:384], r=["psT"], w=["kve%d_%d" % (et % 2, ci)])
                        ext_step(2 * et + 1)
                        ext_step(2 * et + 2)
                    ext_step(NE + 1)
                    TS("dve", Sf[:], Se[:], flags[:, 0:1], None, ALU.mult, r=["Se", "flags"], w=["Sf"])
                    TS("dve", Sb_[:], Se[:], flags[:, 1:2], None, ALU.mult, r=["Se", "flags"], w=["Sb"])
                    for ti in range(5 if cut >= 3 else 0):
                        s0, e0 = TB[ti], TB[ti + 1]
                        n = e0 - s0
                        ht, hk = load_ht(s0, n)
                        proj(wq, "wq", slice(0, 128), ht, hk, n,
                             lambda acc, ak, s0=s0, n=n: ACT(qT[:, s0:s0 + n], acc[:, :n], AF.Copy, r=[ak], w=["qT"], scale=128.0 ** -0.5))
                        proj(wk, "wk", slice(0, 128), ht, hk, n,
                             lambda acc, ak, s0=s0, n=n: ACT(kT[:, s0:s0 + n], acc[:, :n], AF.Copy, r=[ak], w=["kT"]))
                        for h in range(2):
                            proj(wv, "wv", slice(h * 128, (h + 1) * 128), ht, hk, n,
                                 lambda acc, ak, h=h, n=n: ACT(vt[:, h, :n], acc[:, :n], AF.Copy, r=[ak], w=["vt%d" % h]))
                        for h in range(2):
                            proj(wr, "wr", slice(h * 128, (h + 1) * 128), ht, hk, n,
                                 lambda acc, ak, h=h, s0=s0, n=n: ACT(rs[:, h, s0:s0 + n], acc[:, :n], AF.Silu, r=[ak], w=["rs"]))
                        for c in range(ti * 4, ti * 4 + 4):
                            cs, C = CHUNKS[c]
                            lc = cs - s0
                            TR(psT[:C, 0:128], kT[:, cs:cs + C], identb[:, :], r=["kT", "identb"], w=["psT"])
                            for h in range(2):
                                TR(psT[:C, 128 + h * 128:256 + h * 128], vt[:, h, lc:lc + C], identb[:, :],
                                   r=["vt%d" % h, "identb"], w=["psT"])
                            CP("dve", kv[:C, c, :], psT[:C, 0:384], r=["psT"], w=["kv"])
                    def st_gate(c):
                        cs, C = CHUNKS[c]
                        MM(psG[0][:C, 0:256], aT[0:64, cs:cs + C], wgb_sb[0:64, j * 256:(j + 1) * 256],
                           r=["aT", "wgb_sb"], w=["psG0"])
                        ACT(ex[:C, :256], psG[0][:C, :256], AF.Exp, r=["psG0"], w=["ex"], scale=-1.0)
                        ACT(lsb[:C, :256], ex[:C, :256], AF.Ln, r=["ex"], w=["lsb"], bias=1.0)

                    def st_cum(c, fwd):
                        cs, C = CHUNKS[c]
                        p = c % 2
                        if fwd:
                            MM(psG[1][:, 0:C], lsb[:C, 0:128], Mle[:C, :C], r=["lsb", "consts"], w=["psG1"])
                            MM(psG[1][:C, 256:384], Mgt[:C, :C], lsb[:C, 0:128], r=["lsb", "consts"], w=["psG1"])
                        MM(psG[1][:, 128:128 + C], lsb[:C, 128:256], Mge[:C, :C], r=["lsb", "consts"], w=["psG1"])
                        MM(psG[1][:C, 384:512], Mlt[:C, :C], lsb[:C, 128:256], r=["lsb", "consts"], w=["psG1"])
                        if fwd:
                            ACT(E1p[p][:, 0:256], psG[1][:, 0:256], AF.Exp, r=["psG1"], w=["E1_%d" % p])
                            ACT(E2[:, 0:256], psG[1][:, 0:256], AF.Exp, r=["psG1"], w=["E2"], scale=-1.0)
                            ACT(E3[:C, 0:256], psG[1][:C, 256:512], AF.Exp, r=["psG1"], w=["E3"])
                            TT("dve", qqp[p][:, :, :C], qT[:, cs:cs + C].unsqueeze(1).to_broadcast([128, 2, C]),
                               E1p[p][:].rearrange("p (a b) -> p a b", a=2)[:, :, :C], ALU.mult, r=["qT", "E1_%d" % p], w=["qq%d" % p])
                            TT("dve", kk[:, :, :C], kT[:, cs:cs + C].unsqueeze(1).to_broadcast([128, 2, C]),
                               E2[:].rearrange("p (a b) -> p a b", a=2)[:, :, :C], ALU.mult, r=["kT", "E2"], w=["kk"])
                            TT("dve", khp[p][:C, :], kv[:C, c, 0:128], E3[:C, 0:128], ALU.mult, r=["kv", "E3"], w=["kh%d" % p])
                        else:
                            ACT(E1p[p][:, 128:256], psG[1][:, 128:256], AF.Exp, r=["psG1"], w=["E1_%d" % p])
                            ACT(E3[:C, 128:256], psG[1][:C, 384:512], AF.Exp, r=["psG1"], w=["E3"])
                            TT("dve", khp[p][:C, :], kv[:C, c, 0:128], E3[:C, 128:256], ALU.mult, r=["kv", "E3"], w=["kh%d" % p])

                    NCH = 20 if cut >= 4 else 0
                    order = list(reversed(range(NCH)))
                    for t in range(NCH + 2):
                        if 1 <= t <= NCH:
                            st_cum(order[t - 1], False)
                        if t < NCH:
                            st_gate(order[t])
                        if t >= 2:
                            c = order[t - 2]
                            cs, C = CHUNKS[c]
                            p = c % 2
                            ACT(Sbb[:, c, :], Sb_[:], AF.Copy, r=["Sb"], w=["Sbb"])
                            MM(psX[:, 0:256], khp[p][:C, :], kv[:C, c, 128:384], r=["kh%d" % p, "kv"], w=["psX"])
                            STT("dve", Sb_[:], Sb_[:], E1p[p][:, 128:129], psX[:, 0:256], ALU.mult, ALU.add,
                                r=["Sb", "E1_%d" % p, "psX"], w=["Sb"])
                    NCH = 20 if cut >= 5 else 0
                    for t in range(NCH + 4):
                        if 3 <= t < NCH + 3:
                            c = t - 3
                            cs, C = CHUNKS[c]
                            for h in range(2):
                                TR(psT[:, 512 + h * 128:512 + h * 128 + C], on[:C, h * 128:(h + 1) * 128], identb[:C, :C],
                                   r=["on", "identb"], w=["psT"])
                            for h in range(2):
                                STT("dve", mst[:, h, cs:cs + C], psT[:, 512 + h * 128:512 + h * 128 + C], ghead[:, h:h + 1],
                                    rs[:, h, cs:cs + C], ALU.mult, ALU.mult, r=["psT", "ghead", "rs"], w=["mst"])
                        if 2 <= t < NCH + 2:
                            c = t - 2
                            cs, C = CHUNKS[c]
                            p = c % 2
                            ACT(Sfb[:], Sf[:], AF.Copy, r=["Sf"], w=["Sfb"])
                            MM(psS[:C, 0:C], kk[:, 0, :C], qqp[p][:, 0, :C], r=["kk", "qq%d" % p], w=["psS"])
                            MM(psS[:C, 128:128 + C], kk[:, 1, :C], qqp[p][:, 1, :C], r=["kk", "qq%d" % p], w=["psS"])
                            TT("dve", PT[:C, :, :C], psS[:C, 0:256].rearrange("p (a b) -> p a b", a=2)[:, :, :C], masks[:C, :, :C],
                               ALU.mult, r=["psS", "consts"], w=["PT"])
                        if 1 <= t < NCH + 1:
                            st_cum(t - 1, True)
                        if t < NCH:
                            st_gate(t)
                        if 2 <= t < NCH + 2:
                            c = t - 2
                            cs, C = CHUNKS[c]
                            p = c % 2
                            MM(psO[:C, 0:256], PT[:C, 0, :C], kv[:C, c, 128:384], start=True, stop=False, r=["PT", "kv"], w=["psOo"])
                            MM(psO[:C, 0:256], PT[:C, 1, :C], kv[:C, c, 128:384], start=False, stop=False, r=["PT", "kv"], w=["psOo"])
                            MM(psO[:C, 0:256], qqp[p][:, 0, :C], Sfb[:, :], start=False, stop=False, r=["qq%d" % p, "Sfb"], w=["psOo"])
                            MM(psO[:C, 0:256], qqp[p][:, 1, :C], Sbb[:, c, :], start=False, stop=True, r=["qq%d" % p, "Sbb"], w=["psOo"])
                            MM(psX[:, 0:256], khp[p][:C, :], kv[:C, c, 128:384], r=["kh%d" % p, "kv"], w=["psX"])
                            STT("dve", Sf[:], Sf[:], E1p[p][:, C - 1:C], psX[:, 0:256], ALU.mult, ALU.add,
                                r=["Sf", "E1_%d" % p, "psX"], w=["Sf"])
                            MSET("dve", ss1[:], 0.0, w=["ss1"])
                            ACT(junk[:C, :], psO[:C, 0:256], AF.Square, r=["psOo", "ss1"], w=["junk", "ss1"], accum_out=ss1[:C, 0:1])
                            ACT(rstd1[:C, :], ss1[:C, :], AF.Ln, r=["ss1"], w=["rstd1"], scale=1.0 / 256, bias=EPS)
                            ACT(rstd1[:C, :], rstd1[:C, :], AF.Exp, r=["rstd1"], w=["rstd1"], scale=-0.5)
                            TS("dve", on[:C, :], psO[:C, 0:256], rstd1[:C, 0:1], None, ALU.mult, r=["psOo", "rstd1"], w=["on"])
                    DMA("sp", mT_scr[:, 2 * j:2 * j + 2, :], mst[:, :, :], s_mst, r=["mst"], w=["mT_scr"])
            P.fence()
            with ExitStack() as scv:
                ccs = sb(scv, "ccs", [128, 413], F32)
                prod = sb(scv, "prod", [128, R + 2], F32)
                cbs = sb(scv, "cbs", [128, R], F32)
                t1 = sb(scv, "t1", [128, R], F32)
                mcv = sb(scv, "mcv", [128, R], BF16)
                s_mcv = P.new_dma_sem("s_mcv")
                MSET("dve", prod[:, 0:1], 0.0, w=["prod"])
                MSET("dve", prod[:, R + 1:R + 2], 0.0, w=["prod"])
                s_w2 = {k: P.new_dma_sem("s_w2" + k) for k in ("a", "b", "c")}
                NCG = 16 if cut >= 7 else 0
                cwsets = [((wq, slice(0, 128), "wq", s_w["q"]), (wk, slice(0, 128), "wk", s_w["k"]), (wv, slice(0, 128), "wv", s_w["v"])),
                          ((wr, slice(0, 128), "wr_a", s_w2["a"]), (wr, slice(128, 256), "wr_b", s_w2["b"]), (wv, slice(128, 256), "wv_b", s_w2["c"]))]

                def load_conv_w(cg):
                    for (wt_, sl_, key_, sem_), off_ in zip(cwsets[cg % 2], (OFF_CB, OFF_CC, OFF_CH)):
                        DMA("pool", wt_[:, :, sl_], w_inv[:, :, off_ + cg * 128:off_ + (cg + 1) * 128], sem_, w=[key_])
                if NCG:
                    load_conv_w(0)
                for cg in range(NCG):
                    if cg + 1 < NCG:
                        load_conv_w(cg + 1)
                    (wA, slA, kA, _), (wB, slB, kB, _), (wC, slC, kC, _) = cwsets[cg % 2]
                    for ti in range(5):
                        s0, e0 = TB[ti], TB[ti + 1]
                        n = e0 - s0
                        ht, hk = load_ht(s0, n)
                        proj(wA, kA, slA, ht, hk, n,
                             lambda acc, ak, s0=s0, n=n: ACT(cbs[:, s0:s0 + n], acc[:, :n], AF.Copy, r=[ak], w=["cbs"]))
                        proj(wB, kB, slB, ht, hk, n,
                             lambda acc, ak, n=n: ACT(ccs[:, :n], acc[:, :n], AF.Copy, r=[ak], w=["ccs"]))
                        proj(wC, kC, slC, ht, hk, n,
                             lambda acc, ak, s0=s0, n=n: TT("dve", prod[:, 1 + s0:1 + s0 + n], acc[:, :n], ccs[:, :n], ALU.mult,
                                                            r=[ak, "ccs"], w=["prod"]))
                    TS("dve", t1[:, :], prod[:, 0:R], cmw[:, cg, 0:1], None, ALU.mult, r=["prod", "cmw"], w=["t1"])
                    STT("dve", t1[:, :], prod[:, 1:R + 1], cmw[:, cg, 1:2], t1[:, :], ALU.mult, ALU.add, r=["prod", "cmw", "t1"], w=["t1"])
                    STT("dve", t1[:, :], prod[:, 2:R + 2], cmw[:, cg, 2:3], t1[:, :], ALU.mult, ALU.add, r=["prod", "cmw", "t1"], w=["t1"])
                    TT("dve", mcv[:, :], t1[:, :], cbs[:, :], ALU.mult, r=["t1", "cbs"], w=["mcv"])
                    DMA("sp", mT_scr[:, 16 + cg, :], mcv[:, :], s_mcv, r=["mcv"], w=["mT_scr"])
          conv_hook(flush=True)
          P.fence()

        if stages >= 3:
          with ExitStack() as sC:
            X1 = sb(sC, "X1", [128, KC, 415], F32)
            MH = sb(sC, "MH", [128, KC, 415], BF16)
            aTt = sb(sC, "aTt", [128, 11, 413], BF16)
            wring = [sb(sC, "wring%d" % i, [128, KC, 256], BF16) for i in range(3)]
            wdring = [sb(sC, "wdring%d" % i, [128, 11, 512], BF16) for i in range(2)]
            sqt = [sb(sC, "sqt%d" % i, [128, 415], BF16) for i in range(2)]
            rstdt = sb(sC, "rstdt", [128, 415], F32)
            c1 = [sb(sC, "c1_%d" % i, [128, 413], F32) for i in range(2)]
            sg_ = [sb(sC, "sg%d" % i, [128, 413], F32) for i in range(2)]
            ost = [sb(sC, "ost%d" % i, [128, 2048], F32) for i in range(2)]
            s_wr = [[P.new_dma_sem("s_wr%d_%d" % (i, h)) for h in range(2)] for i in range(3)]
            s_wd = [P.new_dma_sem("s_wd%d" % i) for i in range(2)]
            s_x1 = P.new_dma_sem("s_x1")
            s_mh = P.new_dma_sem("s_mh")
            s_ost = [P.new_dma_sem("s_ost%d" % i) for i in range(2)]
            wr_ctr = [0]
            wd_ctr = [0]
            ost_ctr = [0]
            out_dmas = []
            for ti in range(5):
                s0, e0 = TB[ti], TB[ti + 1]
                lo, hi = max(s0 - 1, 0), min(e0 + 1, R)
                N = e0 - s0 + 2
                NV = N - 2
                off = lo - (s0 - 1)
                if ti == 0:
                    MSET("dve", X1[:, :, 0:1], 0.0, w=["X1"])
                    MSET("dve", MH[:, :, 0:1], 0.0, w=["MH"])
                if ti == 4:
                    MSET("dve", X1[:, :, N - 1:N], 0.0, w=["X1"])
                    MSET("dve", MH[:, :, N - 1:N], 0.0, w=["MH"])
                DMA("sp", MH[:, :, off:off + hi - lo], mT_scr[:, :, lo:hi], s_mh, r=["mT_scr"], w=["MH"])
                DMA("sp", X1[:, :, off:off + hi - lo], xT_scr[:, :, lo:hi], s_x1, r=["xT_scr"], w=["X1"])
                for cb2 in range(16):
                    sl = wr_ctr[0] % 3
                    wr_ctr[0] += 1
                    DMA("pool", wring[sl][:, :, :], wo_scr[cb2], s_wr[sl][0], w=["wring%d" % sl])
                    for h in range(2):
                        cb = cb2 * 2 + h
                        acc = psA[cb % 2]
                        for kc in range(KC):
                            MM(acc[:, :N], wring[sl][:, kc, h * 128:(h + 1) * 128], MH[:, kc, :N], start=(kc == 0), stop=(kc == KC - 1),
                               r=["wring%d" % sl, "MH"], w=["psA%d" % (cb % 2)])
                        if cb >= 1:
                            pcb = cb - 1
                            MM(psX[:, :N], onesb[:, :], sqt[pcb % 2][:, :N], start=(pcb == 0), stop=False, r=["sqt%d" % (pcb % 2), "onesb"], w=["psX"])
                        TT("dve", X1[:, cb, :N], acc[:, :N], X1[:, cb, :N], ALU.add, r=["psA%d" % (cb % 2), "X1"], w=["X1c%d" % cb])
                        ACT(sqt[cb % 2][:, :N], X1[:, cb, :N], AF.Square, r=["X1c%d" % cb], w=["sqt%d" % (cb % 2)])
                        if cb == 31:
                            MM(psX[:, :N], onesb[:, :], sqt[cb % 2][:, :N], start=False, stop=True, r=["sqt%d" % (cb % 2), "onesb"], w=["psX"])
                xck = ["X1c%d" % cb for cb in range(32)]
                ACT(rstdt[:, :N], psX[:, :N], AF.Ln, r=["psX"], w=["rstdt"], bias=EPS)
                ACT(rstdt[:, :N], rstdt[:, :N], AF.Exp, r=["rstdt"], w=["rstdt"], scale=-0.5)
                for kc in range(KC):
                    STT("dve", MH[:, kc, :N], X1[:, kc, :N], gvec[:, 32 + kc:33 + kc], rstdt[:, :N], ALU.mult, ALU.mult,
                        r=["X1c%d" % kc, "gvec", "rstdt"], w=["MH"])
                for gi, (f0, nf) in enumerate(FGROUPS):
                    for fl in range(nf):
                        f = f0 + fl
                        sl = wr_ctr[0] % 3
                        wr_ctr[0] += 1
                        DMA("pool", wring[sl][:, :, :], wu_scr[f], s_wr[sl][0], w=["wring%d" % sl])
                        pu, pg = psA[f % 2], psG[f % 2]
                        for kc in range(KC):
                            MM(pu[:, :N], wring[sl][:, kc, 0:128], MH[:, kc, :N], start=(kc == 0), stop=(kc == KC - 1),
                               r=["wring%d" % sl, "MH"], w=["psA%d" % (f % 2)])
                        for kc in range(KC):
                            MM(pg[:, :N], wring[sl][:, kc, 128:256], MH[:, kc, :N], start=(kc == 0), stop=(kc == KC - 1),
                               r=["wring%d" % sl, "MH"], w=["psG%d" % (f % 2)])
                        cc1 = c1[f % 2]
                        TS("dve", cc1[:, :NV], pg[:, 0:NV], fcw[:, f, 0:1], fcw[:, f, 3:4], ALU.mult, ALU.add,
                           r=["psG%d" % (f % 2), "fcw"], w=["c1_%d" % (f % 2)])
                        STT("dve", cc1[:, :NV], pg[:, 1:NV + 1], fcw[:, f, 1:2], cc1[:, :NV], ALU.mult, ALU.add,
                            r=["psG%d" % (f % 2), "fcw", "c1_%d" % (f % 2)], w=["c1_%d" % (f % 2)])
                        STT("dve", cc1[:, :NV], pg[:, 2:NV + 2], fcw[:, f, 2:3], cc1[:, :NV], ALU.mult, ALU.add,
                            r=["psG%d" % (f % 2), "fcw", "c1_%d" % (f % 2)], w=["c1_%d" % (f % 2)])
                        ACT(sg_[f % 2][:, :NV], cc1[:, :NV], AF.Silu, r=["c1_%d" % (f % 2)], w=["sg%d" % (f % 2)])
                        TT("dve", aTt[:, fl, :NV], sg_[f % 2][:, :NV], pu[:, 1:NV + 1], ALU.mult,
                           r=["sg%d" % (f % 2), "psA%d" % (f % 2)], w=["aTt"])
                    for cb4 in range(8):
                        dl = wd_ctr[0] % 2
                        wd_ctr[0] += 1
                        DMA("pool", wdring[dl][:, :nf, :], wd_scr[gi, cb4, :, 0:nf, :], s_wd[dl], w=["wdring%d" % dl])
                        for h in range(4):
                            cb = cb4 * 4 + h
                            yps = psS if cb % 2 == 0 else psO
                            yk = "psS" if cb % 2 == 0 else "psO"
                            for fl in range(nf):
                                MM(yps[:, :NV], wdring[dl][:, fl, h * 128:(h + 1) * 128], aTt[:, fl, :NV], start=(fl == 0), stop=(fl == nf - 1),
                                   r=["wdring%d" % dl, "aTt"], w=[yk])
                            TT("dve", X1[:, cb, 1:NV + 1], yps[:, :NV], X1[:, cb, 1:NV + 1], ALU.add, r=[yk, "X1c%d" % cb], w=["X1c%d" % cb])
                for cb in range(32):
                    ACT(sqt[cb % 2][:, :NV], X1[:, cb, 1:NV + 1], AF.Square, r=["X1c%d" % cb], w=["sqt%d" % (cb % 2)])
                    MM(psX[:, :NV], onesb[:, :], sqt[cb % 2][:, :NV], start=(cb == 0), stop=(cb == 31), r=["sqt%d" % (cb % 2), "onesb"], w=["psX"])
                ACT(rstdt[:, :NV], psX[:, :NV], AF.Ln, r=["psX"], w=["rstdt"], bias=EPS)
                ACT(rstdt[:, :NV], rstdt[:, :NV], AF.Exp, r=["rstdt"], w=["rstdt"], scale=-0.5)
                for kc in range(KC):
                    STT("dve", X1[:, kc, 1:NV + 1], X1[:, kc, 1:NV + 1], gvec[:, 64 + kc:65 + kc], rstdt[:, :NV], ALU.mult, ALU.mult,
                        r=["X1c%d" % kc, "gvec", "rstdt"], w=["X1c%d" % kc])
                for c in range(ti * 4, ti * 4 + 4):
                    cs, C = CHUNKS[c]
                    lc = cs - s0 + 1
                    for half in range(2):
                        ol = ost_ctr[0] % 2
                        ost_ctr[0] += 1
                        for g4 in range(4):
                            bank = psG[g4 % 2]
                            for jj in range(4):
                                kc = half * 16 + g4 * 4 + jj
                                TR(bank[:C, jj * 128:(jj + 1) * 128], X1[:, kc, lc:lc + C], ident[:, :],
                                   r=["X1c%d" % kc, "consts"], w=["psG%d" % (g4 % 2)])
                            ACT(ost[ol][:C, g4 * 512:(g4 + 1) * 512], bank[:C, :], AF.Copy, r=["psG%d" % (g4 % 2)], w=["ost%d" % ol])
                        out_dmas.append(DMA("sp", y[cs:cs + C, half * 2048:(half + 1) * 2048], ost[ol][:C, :], s_ost[ol], r=["ost%d" % ol]))
                P.op("dve", lambda e: e.memset(rstdt[:, 0:1], 0.0), reads=xck + ["MH"], writes=["X1", "MH", "rstdt"] + xck)
          P.fence()
        else:
            P.fence()
        P.barrier_wait("sp", list(P.ops["sp"][-1].deps))
        block = es.enter_context(nc.Block())
        stats = P.emit(block)
    return nc, stats


def _consts():
    a = np.arange(128)[:, None]
    b = np.arange(128)[None, :]
    c = np.zeros((128, 7, 128), np.float32)
    c[:, 0] = (a == b)
    c[:, 1] = (a <= b) * (-1.0 / 16.0)
    c[:, 2] = (a >= b) * (-1.0 / 16.0)
    c[:, 3] = (a > b) * (-1.0 / 16.0)
    c[:, 4] = (a < b) * (-1.0 / 16.0)
    c[:, 5] = (a <= b)
    c[:, 6] = (a > b)
    return c.reshape(128, 7 * 128)


def _prepare_inputs(x_prompt, x_sample, meta_tokens, mix_norm_g, w_in, w_gate2, b_gate2, head_norm_g,
                    conv_mix_w, w_out, ffn_norm_g, w_up, ffn_conv_w, ffn_conv_b, w_down, final_norm_g):
    f = lambda a: np.ascontiguousarray(np.asarray(a, dtype=np.float32))
    x_prompt, x_sample, meta = f(x_prompt), f(x_sample), f(meta_tokens)
    w_in0, w_out0, w_up0, w_down0 = f(w_in[0]), f(w_out[0]), f(w_up[0]), f(w_down[0])
    wg2, bg2 = f(w_gate2[0]), f(b_gate2[0])
    gvec = np.concatenate([f(mix_norm_g[0]).reshape(32, 128).T, f(ffn_norm_g[0]).reshape(32, 128).T,
                           f(final_norm_g).reshape(32, 128).T], axis=1)
    ghead = f(head_norm_g[0]).reshape(2, 128).T
    cmw = f(conv_mix_w[0]).reshape(3, 16, 128).transpose(2, 1, 0).reshape(128, 48)
    fcw = np.concatenate([f(ffn_conv_w[0]), f(ffn_conv_b[0])[None]], axis=0).reshape(4, NF, 128).transpose(2, 1, 0).reshape(128, NF * 4)
    consts = _consts()
    shared = dict(w_in=w_in0, w_out=w_out0, w_up=w_up0, w_down=w_down0,
                  gvec=f(gvec), ghead=f(ghead), cmw=f(cmw), fcw=f(fcw), consts=consts)

    def core(xown, xext, ext_dir, flag_f, flag_b):
        wgin = np.zeros((D, 80), np.float32)
        wgin[:, 0:16] = w_in0[:, 6144:6160]
        wgin[:, 32:48] = w_in0[:, 6160:6176]
        wgin[:, 64:80] = w_in0[:, 6144 + 16 * ext_dir:6160 + 16 * ext_dir]
        wgb = np.zeros((96, 8, 256), np.float32)
        wgb[0:16, :, 0:128] = wg2[0].reshape(16, 8, 128)
        wgb[16, :, 0:128] = bg2[0].reshape(8, 128)
        wgb[32:48, :, 128:256] = wg2[1].reshape(16, 8, 128)
        wgb[48, :, 128:256] = bg2[1].reshape(8, 128)
        wgb[64:80, :, 0:128] = wg2[ext_dir].reshape(16, 8, 128)
        wgb[80, :, 0:128] = bg2[ext_dir].reshape(8, 128)
        wgb = wgb.reshape(96, 2048)
        flags = np.zeros((128, 2), np.float32)
        flags[:, 0] = flag_f
        flags[:, 1] = flag_b
        m = dict(shared)
        m.update(xown=f(xown), xext=f(xext), wgin=wgin, wgb=wgb, flags=flags)
        return m

    maps = []
    for b in range(2):
        xa = np.concatenate([meta, x_prompt[b, 0:2048]], axis=0)
        ea = x_prompt[b, 2048:4096][::-1]
        maps.append(core(xa, ea, 1, 0.0, 1.0))
        xb = x_prompt[b, 2032:4096]
        eb = np.concatenate([meta, x_prompt[b, 0:2032]], axis=0)
        maps.append(core(xb, eb, 0, 1.0, 0.0))
    for s in range(4):
        xs = np.concatenate([meta, x_sample[s]], axis=0)
        maps.append(core(xs, np.zeros((E, D), np.float32), 0, 0.0, 0.0))
    return maps


_NC_CACHE = {}


def kernel(x_prompt, x_sample, meta_tokens, mix_norm_g, w_in, w_gate2, b_gate2, head_norm_g,
           conv_mix_w, w_out, ffn_norm_g, w_up, ffn_conv_w, ffn_conv_b, w_down, final_norm_g):
    maps = _prepare_inputs(x_prompt, x_sample, meta_tokens, mix_norm_g, w_in, w_gate2, b_gate2, head_norm_g,
                           conv_mix_w, w_out, ffn_norm_g, w_up, ffn_conv_w, ffn_conv_b, w_down, final_norm_g)
    if "nc" not in _NC_CACHE:
        _NC_CACHE["nc"] = build_program()[0]
    nc = _NC_CACHE["nc"]
    res = run_bass_kernel_spmd(nc, maps, core_ids=list(range(8)))
    ys = [np.asarray(r["y"], dtype=np.float32) for r in res.results]
    y_prompt = np.zeros((2, 4096, D), np.float32)
    y_sample = np.zeros((4, 2048, D), np.float32)
    for b in range(2):
        y_prompt[b, 0:2040] = ys[2 * b][16:16 + 2040]
        y_prompt[b, 2040:4096] = ys[2 * b + 1][8:2064]
    for s in range(4):
        y_sample[s] = ys[4 + s][16:2064]
    return (y_prompt, y_sample)
```
